# Optimizing a Trainium2 kernel written in Bass

```python
import math
import jax, jax.numpy as jnp
from jax import lax
import numpy as np

D_MODEL = 1024
BATCH = 4
SEQ = 8192
DEPTH = 1

CHUNK = 64
Q_BLOCK = 128
EPS = 1e-6
CONV_CH = D_MODEL // 2
CONV_WIDTH = 31
MLA_HEADS = 8
MLA_NOPE = 64
MLA_ROPE = 32
MLA_V = 64
MLA_Q_RANK = 256
MLA_KV_RANK = 128
ROPE_THETA = 10000.0
MIX_WIDTH = CONV_CH + MLA_HEADS * MLA_V
IN_COLS = 2 * CONV_CH + MLA_Q_RANK + MLA_KV_RANK + MLA_ROPE
MEM_LEN = 256
MEM_HEADS = 4
MEM_HEAD_DIM = D_MODEL // MEM_HEADS
D_FF = 2816
FFN_CONV_WIDTH = 3
MAX_START = 4096

kernel_name = "hybrid_conformer_mla_stream_layer"


def rms_norm(x, g):
    xf = x.astype(jnp.float32)
    y = xf * lax.rsqrt(jnp.mean(xf * xf, axis=-1, keepdims=True) + EPS)
    return (y * g.astype(jnp.float32)).astype(x.dtype)


def layer_norm(x, g, b):
    xf = x.astype(jnp.float32)
    mu = jnp.mean(xf, axis=-1, keepdims=True)
    xc = xf - mu
    y = xc * lax.rsqrt(jnp.mean(xc * xc, axis=-1, keepdims=True) + EPS)
    return (y * g.astype(jnp.float32) + b.astype(jnp.float32)).astype(x.dtype)


def causal_depthwise_conv(x, w, b):
    k_width, ch = w.shape
    y = lax.conv_general_dilated(
        x, w[:, None, :].astype(x.dtype), window_strides=(1,), padding=[(k_width - 1, 0)],
        dimension_numbers=('NWC', 'WIO', 'NWC'), feature_group_count=ch)
    return y + b.astype(x.dtype)


def rope_tables(positions, dim):
    inv_freq = ROPE_THETA ** (-jnp.arange(0, dim, 2, dtype=jnp.float32) / dim)
    ang = positions.astype(jnp.float32)[..., None] * inv_freq
    return jnp.cos(ang), jnp.sin(ang)


def apply_rope(x, cos, sin):
    x1, x2 = jnp.split(x.astype(jnp.float32), 2, axis=-1)
    return jnp.concatenate([x1 * cos - x2 * sin, x1 * sin + x2 * cos], axis=-1).astype(x.dtype)


def chunk_causal_attention(q, k, v, scale):
    bsz, seq, heads, dk = q.shape
    n_blocks = seq // Q_BLOCK
    q_blocks = q.reshape(bsz, n_blocks, Q_BLOCK, heads, dk).transpose(1, 0, 2, 3, 4)
    key_chunk = jnp.arange(seq) // CHUNK

    def one_block(args):
        i, qi = args
        s = jnp.einsum('bqhd,bkhd->bhqk', qi, k, preferred_element_type=jnp.float32) * scale
        q_chunk = (i * Q_BLOCK + jnp.arange(Q_BLOCK)) // CHUNK
        mask = key_chunk[None, :] <= q_chunk[:, None]
        s = jnp.where(mask[None, None], s, -jnp.inf)
        p = jax.nn.softmax(s, axis=-1)
        return jnp.einsum('bhqk,bkhd->bqhd', p.astype(v.dtype), v)

    out = lax.map(one_block, (jnp.arange(n_blocks), q_blocks))
    return out.transpose(1, 0, 2, 3, 4).reshape(bsz, seq, heads, v.shape[-1])


def hybrid_mixer(h, cos, sin, w_in, b_conv_in, w_conv_dw, b_conv_dw, conv_ln_g, conv_ln_b,
                 q_lat_norm_g, w_uq, kv_lat_norm_g, w_ukv, q_norm_g, k_norm_g, w_out):
    bsz, seq, _ = h.shape
    z = h @ w_in
    s1 = 2 * CONV_CH
    s2 = s1 + MLA_Q_RANK
    s3 = s2 + MLA_KV_RANK
    conv_in, c_q, c_kv, k_rope = jnp.split(z, [s1, s2, s3], axis=-1)

    a, gate = jnp.split(conv_in + b_conv_in, 2, axis=-1)
    u = a * jax.nn.sigmoid(gate)
    u = causal_depthwise_conv(u, w_conv_dw, b_conv_dw)
    u = jax.nn.silu(layer_norm(u, conv_ln_g, conv_ln_b))

    q = (rms_norm(c_q, q_lat_norm_g) @ w_uq).reshape(bsz, seq, MLA_HEADS, MLA_NOPE + MLA_ROPE)
    kv = (rms_norm(c_kv, kv_lat_norm_g) @ w_ukv).reshape(bsz, seq, MLA_HEADS, MLA_NOPE + MLA_V)
    k_nope, v = jnp.split(kv, [MLA_NOPE], axis=-1)
    k_r = jnp.broadcast_to(k_rope[:, :, None, :], (bsz, seq, MLA_HEADS, MLA_ROPE))
    k = jnp.concatenate([k_nope, k_r], axis=-1)
    q = rms_norm(q, q_norm_g)
    k = rms_norm(k, k_norm_g)
    q = jnp.concatenate([q[..., :MLA_NOPE], apply_rope(q[..., MLA_NOPE:], cos, sin)], axis=-1)
    k = jnp.concatenate([k[..., :MLA_NOPE], apply_rope(k[..., MLA_NOPE:], cos, sin)], axis=-1)
    attn = chunk_causal_attention(q, k, v, 1.0 / math.sqrt(MLA_NOPE + MLA_ROPE))
    attn = attn.reshape(bsz, seq, MLA_HEADS * MLA_V)

    return jnp.concatenate([u, attn], axis=-1) @ w_out


def memory_cross_attention(hq, hm, w_mem_q, w_mem_kv, mem_q_norm_g, mem_k_norm_g, w_mem_o):
    bsz, seq, _ = hq.shape
    q = (hq @ w_mem_q).reshape(bsz, seq, MEM_HEADS, MEM_HEAD_DIM)
    k, v = jnp.split(hm @ w_mem_kv, 2, axis=-1)
    k = k.reshape(bsz, MEM_LEN, MEM_HEADS, MEM_HEAD_DIM)
    v = v.reshape(bsz, MEM_LEN, MEM_HEADS, MEM_HEAD_DIM)
    q = rms_norm(q, mem_q_norm_g)
    k = rms_norm(k, mem_k_norm_g)
    s = jnp.einsum('bqhd,bkhd->bhqk', q, k, preferred_element_type=jnp.float32) / math.sqrt(MEM_HEAD_DIM)
    p = jax.nn.softmax(s, axis=-1)
    o = jnp.einsum('bhqk,bkhd->bqhd', p.astype(v.dtype), v).reshape(bsz, seq, D_MODEL)
    return o @ w_mem_o


def conv_gated_ffn(h, w_up, w_ffn_dw, b_ffn_dw, w_down):
    up = causal_depthwise_conv(h @ w_up, w_ffn_dw, b_ffn_dw)
    g, val = jnp.split(up, 2, axis=-1)
    return (jax.nn.silu(g) * val) @ w_down


def setup_inputs(seed: int = 0) -> dict:
    key = jax.random.key(seed)
    ks = iter(jax.random.split(key, 40))
    L = DEPTH

    def w(shape, fan_in):
        return jax.random.normal(next(ks), shape, jnp.float32) * fan_in ** -0.5

    def gain(shape):
        return 1.0 + 0.05 * jax.random.normal(next(ks), shape, jnp.float32)

    def bias(shape):
        return 0.02 * jax.random.normal(next(ks), shape, jnp.float32)

    x = jax.random.normal(next(ks), (BATCH, SEQ, D_MODEL), jnp.float32)
    mem = jax.random.normal(next(ks), (BATCH, MEM_LEN, D_MODEL), jnp.float32)
    start = jax.random.randint(next(ks), (BATCH, 1), 0, MAX_START, dtype=jnp.int32)
    positions = start + jnp.arange(SEQ, dtype=jnp.int32)[None, :]
    return {
        "x": x,
        "mem": mem,
        "positions": positions,
        "mix_norm_g": gain((L, D_MODEL)),
        "w_in": w((L, D_MODEL, IN_COLS), D_MODEL),
        "b_conv_in": bias((L, 2 * CONV_CH)),
        "w_conv_dw": w((L, CONV_WIDTH, CONV_CH), CONV_WIDTH),
        "b_conv_dw": bias((L, CONV_CH)),
        "conv_ln_g": gain((L, CONV_CH)),
        "conv_ln_b": bias((L, CONV_CH)),
        "q_lat_norm_g": gain((L, MLA_Q_RANK)),
        "w_uq": w((L, MLA_Q_RANK, MLA_HEADS * (MLA_NOPE + MLA_ROPE)), MLA_Q_RANK),
        "kv_lat_norm_g": gain((L, MLA_KV_RANK)),
        "w_ukv": w((L, MLA_KV_RANK, MLA_HEADS * (MLA_NOPE + MLA_V)), MLA_KV_RANK),
        "q_norm_g": gain((L, MLA_NOPE + MLA_ROPE)),
        "k_norm_g": gain((L, MLA_NOPE + MLA_ROPE)),
        "w_out": w((L, MIX_WIDTH, D_MODEL), MIX_WIDTH),
        "mem_norm_x_g": gain((L, D_MODEL)),
        "mem_norm_m_g": gain((L, D_MODEL)),
        "w_mem_q": w((L, D_MODEL, D_MODEL), D_MODEL),
        "w_mem_kv": w((L, D_MODEL, 2 * D_MODEL), D_MODEL),
        "mem_q_norm_g": gain((L, MEM_HEAD_DIM)),
        "mem_k_norm_g": gain((L, MEM_HEAD_DIM)),
        "w_mem_o": w((L, D_MODEL, D_MODEL), D_MODEL),
        "ffn_norm_g": gain((L, D_MODEL)),
        "w_up": w((L, D_MODEL, 2 * D_FF), D_MODEL),
        "w_ffn_dw": w((L, FFN_CONV_WIDTH, 2 * D_FF), FFN_CONV_WIDTH),
        "b_ffn_dw": bias((L, 2 * D_FF)),
        "w_down": w((L, D_FF, D_MODEL), D_FF),
    }


def reference(x, mem, positions, mix_norm_g, w_in, b_conv_in, w_conv_dw, b_conv_dw, conv_ln_g,
              conv_ln_b, q_lat_norm_g, w_uq, kv_lat_norm_g, w_ukv, q_norm_g, k_norm_g, w_out,
              mem_norm_x_g, mem_norm_m_g, w_mem_q, w_mem_kv, mem_q_norm_g, mem_k_norm_g, w_mem_o,
              ffn_norm_g, w_up, w_ffn_dw, b_ffn_dw, w_down):
    cos, sin = rope_tables(positions, MLA_ROPE)
    cos, sin = cos[:, :, None, :], sin[:, :, None, :]
    for l in range(DEPTH):
        h = rms_norm(x, mix_norm_g[l])
        x = x + hybrid_mixer(h, cos, sin, w_in[l], b_conv_in[l], w_conv_dw[l], b_conv_dw[l],
                             conv_ln_g[l], conv_ln_b[l], q_lat_norm_g[l], w_uq[l],
                             kv_lat_norm_g[l], w_ukv[l], q_norm_g[l], k_norm_g[l], w_out[l])
        hq = rms_norm(x, mem_norm_x_g[l])
        hm = rms_norm(mem, mem_norm_m_g[l])
        x = x + memory_cross_attention(hq, hm, w_mem_q[l], w_mem_kv[l], mem_q_norm_g[l],
                                       mem_k_norm_g[l], w_mem_o[l])
        h = rms_norm(x, ffn_norm_g[l])
        x = x + conv_gated_ffn(h, w_up[l], w_ffn_dw[l], b_ffn_dw[l], w_down[l])
    return x
```

```python
import contextlib
import numpy as np
import concourse.bass as bass
import concourse.mybir as mybir
from concourse.bass_utils import run_bass_kernel_spmd

F32 = mybir.dt.float32
BF16 = mybir.dt.bfloat16
I32 = mybir.dt.int32
AF = mybir.ActivationFunctionType
ALU = mybir.AluOpType
AX = mybir.AxisListType

D = 1024
SEQ = 8192
NTOK = 4096
NQ = 4224
EPS = 1e-6
DFF = 2816
PI = float(np.pi)

_COLS = [("mixg", 8), ("bcin", 8), ("wdw", 124), ("bdw", 4), ("lng", 4), ("lnb", 4), ("memxg", 8),
         ("memmg", 8), ("mqg", 2), ("mkg", 2), ("ffng", 8), ("wffn", 132), ("bffn", 44), ("eps", 1),
         ("flag", 1), ("kbias", 64), ("gql", 256), ("gkvl", 128), ("gqn", 96), ("gkn", 96),
         ("invf", 16), ("ident", 128)]
C = {}
_o = 0
for _n, _w in _COLS:
    C[_n] = _o
    _o += _w
NCST = _o


def _colmajor(v, k):
    return np.ascontiguousarray(np.asarray(v, np.float32).reshape(k, 128).T)


def pack_consts(inp, half):
    c = np.zeros((128, NCST), np.float32)

    def put(name, arr):
        arr = np.asarray(arr, np.float32)
        c[:, C[name]:C[name] + arr.shape[1]] = arr

    put("mixg", _colmajor(inp["mix_norm_g"][0], 8))
    put("bcin", _colmajor(inp["b_conv_in"][0], 8))
    wdw = np.asarray(inp["w_conv_dw"][0], np.float32)
    put("wdw", wdw.reshape(31, 4, 128).transpose(2, 1, 0).reshape(128, 124))
    put("bdw", _colmajor(inp["b_conv_dw"][0], 4))
    put("lng", _colmajor(inp["conv_ln_g"][0], 4))
    put("lnb", _colmajor(inp["conv_ln_b"][0], 4))
    put("memxg", _colmajor(inp["mem_norm_x_g"][0], 8))
    put("memmg", _colmajor(inp["mem_norm_m_g"][0], 8))
    put("mqg", _colmajor(inp["mem_q_norm_g"][0], 2))
    put("mkg", _colmajor(inp["mem_k_norm_g"][0], 2))
    put("ffng", _colmajor(inp["ffn_norm_g"][0], 8))
    wf = np.asarray(inp["w_ffn_dw"][0], np.float32)
    put("wffn", wf.reshape(3, 44, 128).transpose(2, 1, 0).reshape(128, 132))
    put("bffn", _colmajor(inp["b_ffn_dw"][0], 44))
    c[:, C["eps"]] = EPS
    c[:, C["flag"]] = 1.0 if half == 1 else 0.0
    if half == 0:
        c[:, C["kbias"]:C["kbias"] + 32] = -30000.0
    put("gql", np.tile(np.asarray(inp["q_lat_norm_g"][0], np.float32)[None, :], (128, 1)))
    put("gkvl", np.tile(np.asarray(inp["kv_lat_norm_g"][0], np.float32)[None, :], (128, 1)))
    put("gqn", np.tile(np.asarray(inp["q_norm_g"][0], np.float32)[None, :], (128, 1)))
    put("gkn", np.tile(np.asarray(inp["k_norm_g"][0], np.float32)[None, :], (128, 1)))
    invf = (np.float32(10000.0) ** (-np.arange(0, 32, 2, dtype=np.float32) / np.float32(32))).astype(np.float32)
    put("invf", np.tile(invf[None, :], (128, 1)))
    put("ident", np.eye(128, dtype=np.float32))
    return c


_PSUM_PREFIX = ("pA", "pG", "pL", "pT", "pQ", "pKV", "pY", "pM", "pS", "pO", "pB", "pz", "pu", "pd", "pss")


class Buf:
    __slots__ = ("name", "w", "r", "psum")

    def __init__(self, name=""):
        self.name = name
        self.w = None
        self.r = []
        self.psum = name.startswith(_PSUM_PREFIX)


class Sched:
    NDMA = 24

    def __init__(self, nc):
        self.nc = nc
        self.ops = []
        self.eng = {"pe": nc.tensor, "act": nc.scalar, "dve": nc.vector, "pool": nc.gpsimd, "sp": nc.sync}

    def init_sems(self, es):
        self.sem = {e: es.enter_context(self.nc.semaphore("s_" + e)) for e in self.eng}
        self.dsem = [es.enter_context(self.nc.semaphore("d%d" % i)) for i in range(self.NDMA)]

    def op(self, eng, fn, reads=(), writes=()):
        self.ops.append(("c", eng, fn, tuple(reads), tuple(writes)))

    def dma(self, q, out, in_, reads=(), writes=()):
        eng = self.eng[q]
        self.ops.append(("d", q, (lambda: eng.dma_start(out=out, in_=in_)), tuple(reads), tuple(writes)))

    def barrier(self):
        self.ops.append(("b",))

    def finish(self):
        ops = self.ops
        n = len(ops)
        known = {e: {} for e in self.eng}
        vc = [None] * n
        waits = [None] * n
        signaled = [False] * n
        dma_use = [0] * self.NDMA
        dma_last = [None] * self.NDMA
        dma_ev = {}
        ndma = 0
        pend = {e: set() for e in self.eng}
        last_c = {}
        for i, o in enumerate(ops):
            if o[0] == "b":
                allp = set(last_c.values()) | set(x for x in dma_last if x is not None)
                for e in self.eng:
                    pend[e] |= allp
                waits[i] = {}
                vc[i] = {}
                continue
            kind, e = o[0], o[1]
            reads, writes = o[3], o[4]
            deps = set(pend[e])
            pend[e] = set()
            if kind == "c":
                last_c[e] = i
            for b in reads:
                if b.w is not None:
                    deps.add(b.w)
                if b.psum:
                    deps.update(x for x in b.r if ops[x][1] != e)
            for b in writes:
                if b.w is not None:
                    deps.add(b.w)
                deps.update(b.r)
            if kind == "d":
                s = ndma % self.NDMA
                ndma += 1
                if dma_last[s] is not None:
                    deps.add(dma_last[s])
                dma_use[s] += 1
                dma_ev[i] = (s, dma_use[s])
                dma_last[s] = i
            kn = known[e]
            w = {}
            for d in sorted(deps, reverse=True):
                od = ops[d]
                if od[0] == "c":
                    src = od[1]
                    if src == "pe" and e == "pe" and kind == "c":
                        continue
                    val = d
                else:
                    src = ("dma", dma_ev[d][0])
                    val = dma_ev[d][1]
                if kn.get(src, -1) >= val:
                    continue
                w[src] = max(w.get(src, -1), val)
                kn[src] = val
                for k2, v2 in vc[d].items():
                    if kn.get(k2, -1) < v2:
                        kn[k2] = v2
                if od[0] == "c":
                    signaled[d] = True
            waits[i] = w
            c = dict(kn)
            if kind == "c":
                c[e] = max(c.get(e, -1), i)
            else:
                c[("dma", dma_ev[i][0])] = dma_ev[i][1]
            vc[i] = c
            for b in reads:
                b.r.append(i)
            for b in writes:
                b.w = i
                b.r = []
        cnt = {e: 0 for e in self.eng}
        sigval = {}
        for i, o in enumerate(ops):
            if o[0] == "c" and signaled[i]:
                cnt[o[1]] += 1
                sigval[i] = cnt[o[1]]
        nw = 0
        self.trace = {e: [] for e in self.eng}
        for i, o in enumerate(ops):
            if o[0] == "b":
                continue
            e = o[1]
            self.trace[e].append((i, [((src, 16 * val) if isinstance(src, tuple) else (src, sigval[val])) for src, val in waits[i].items()],
                                  (("c", e, 1) if (o[0] == "c" and signaled[i]) else (("d", dma_ev[i][0], 16) if o[0] == "d" else None))))
            eng = self.eng[e]
            for src, val in waits[i].items():
                nw += 1
                if isinstance(src, tuple):
                    eng.wait_ge(self.dsem[src[1]], 16 * val)
                else:
                    eng.wait_ge(self.sem[src], sigval[val])
            ins = o[2]()
            if o[0] == "c":
                if signaled[i]:
                    ins.then_inc(self.sem[e], 1)
            else:
                ins.then_inc(self.dsem[dma_ev[i][0]], 16)
        sp = self.eng["sp"]
        for s in range(self.NDMA):
            if dma_use[s] > 0:
                sp.wait_ge(self.dsem[s], 16 * dma_use[s])
        self.stats = dict(nops=n, nwaits=nw, nsig=dict(cnt))


def build_program(dbg=False, stop_after=99):
    nc = bass.Bass("TRN2", target_bir_lowering=False)
    S = Sched(nc)
    E = S.eng

    def din(name, shape, dt=F32):
        return nc.dram_tensor(name, list(shape), dt, kind="ExternalInput").ap()

    def dscr(name, shape, dt):
        return nc.dram_tensor(name, list(shape), dt, kind=("ExternalOutput" if dbg else "Internal")).ap()

    xT = din("xT", [D, SEQ])
    pos = din("pos", [128, 64], I32)
    cst = din("cst", [128, NCST])
    memT = din("memT", [D, 256])
    w_in = din("w_in", [D, 1440])
    w_uq = din("w_uq", [256, 768])
    w_ukv = din("w_ukv", [128, 1024])
    w_out = din("w_out", [D, D])
    w_mq = din("w_mem_q", [D, D])
    w_mkv = din("w_mem_kv", [D, 2 * D])
    w_mo = din("w_mem_o", [D, D])
    w_up = din("w_up", [D, 2 * DFF])
    w_dn = din("w_down", [DFF, D])
    yT = nc.dram_tensor("yT", [D, NTOK], F32, kind="ExternalOutput").ap()
    QT_scr = dscr("QT_scr", [8, 96, NQ], BF16)
    KT_scr = dscr("KT_scr", [8, 96, SEQ], BF16)
    V_scr = dscr("V_scr", [8, 128, 64 * 128], BF16)
    x2_scr = dscr("x2_scr", [D, NQ], F32)
    dbg_out = {}
    if dbg:
        dbg_out["uT"] = nc.dram_tensor("dbg_uT", [128, 4, NQ + 30], BF16, kind="ExternalOutput").ap()
        dbg_out["ufT"] = nc.dram_tensor("dbg_ufT", [128, 4, NQ], BF16, kind="ExternalOutput").ap()
        dbg_out["attnT"] = nc.dram_tensor("dbg_attnT", [128, 4, NQ], BF16, kind="ExternalOutput").ap()

    xT_v = xT.rearrange("(kc p) t -> p kc t", p=128)
    x2_v = x2_scr.rearrange("(kc p) t -> p kc t", p=128)
    yT_v = yT.rearrange("(kc p) t -> p kc t", p=128)

    with contextlib.ExitStack() as es0:
        S.init_sems(es0)

        uid = [0]

        def mk(es):
            def sb(name, shape, dt):
                uid[0] += 1
                return es.enter_context(nc.sbuf_tensor("%s_%d" % (name, uid[0]), list(shape), dt))

            def ps(name, shape, dt=F32):
                uid[0] += 1
                return es.enter_context(nc.psum_tensor("%s_%d" % (name, uid[0]), list(shape), dt))
            return sb, ps

        sb0, _ = mk(es0)

        def ACT(out, in_, func, r, w, **kw):
            S.op("act", lambda: nc.scalar.activation(out, in_, func, **kw), r, w)

        def TT(eng, out, a, b, op, r, w):
            S.op(eng, lambda: E[eng].tensor_tensor(out, a, b, op), r, w)

        def TS(eng, out, a, s1, s2, op0, op1, r, w):
            if op1 is None:
                S.op(eng, lambda: E[eng].tensor_scalar(out, a, s1, None, op0), r, w)
            else:
                S.op(eng, lambda: E[eng].tensor_scalar(out, a, s1, s2, op0, op1), r, w)

        def STT(out, a, s, b, op0, op1, r, w):
            S.op("dve", lambda: nc.vector.scalar_tensor_tensor(out, a, s, b, op0, op1), r, w)

        def CP(eng, out, in_, r, w):
            if eng == "act":
                S.op("act", lambda: nc.scalar.copy(out, in_), r, w)
            else:
                S.op(eng, lambda: E[eng].tensor_copy(out, in_), r, w)

        def MM(out, lhsT, rhs, start, stop, r, w):
            S.op("pe", lambda: nc.tensor.matmul(out, lhsT, rhs, start=start, stop=stop), r, w)

        def TR(out, in_, ident, r, w):
            S.op("pe", lambda: nc.tensor.transpose(out, in_, ident), r, w)

        def RED(out, in_, r, w):
            S.op("dve", lambda: nc.vector.tensor_reduce(out, in_, AX.X, ALU.add), r, w)

        def RCP(out, in_, r, w):
            S.op("dve", lambda: nc.vector.reciprocal(out, in_), r, w)

        def MSET(eng, out, val, w):
            S.op(eng, lambda: E[eng].memset(out, val), (), w)

        cst_t = sb0("cst_t", [128, NCST], F32)
        Bc = Buf("cst")
        S.dma("sp", cst_t[:], cst, writes=[Bc])

        def cc(name, j=0, w=1):
            return cst_t[:, C[name] + j:C[name] + j + w]

        ident = sb0("ident", [128, 128], BF16)
        ones_bf = sb0("ones_bf", [128, 128], BF16)
        ones_f = sb0("ones_f", [128, 128], F32)
        Bid, Bob, Bof = Buf("ident"), Buf("ones_bf"), Buf("ones_f")
        CP("dve", ident[:], cc("ident", 0, 128), [Bc], [Bid])
        MSET("pool", ones_bf[:], 1.0, [Bob])
        MSET("pool", ones_f[:], 1.0, [Bof])

        esX = es0.enter_context(contextlib.ExitStack())
        uT = esX.enter_context(nc.sbuf_tensor("uT", [128, 4, NQ + 30], BF16))
        Buf_u = [Buf("u%d" % i) for i in range(9)]
        Bupad = Buf("upad")
        MSET("pool", uT[:, :, 0:30], 0.0, [Bupad])
        Buf_at = [Buf("at%d" % i) for i in range(9)]
        B_KTs = [Buf("KTs%d" % i) for i in range(16)]
        B_Vs = [Buf("Vs%d" % i) for i in range(16)]
        B_QTs = [Buf("QTs%d" % i) for i in range(9)]
        B_x2 = [Buf("x2s%d" % i) for i in range(9)]
        stgs = []

        @contextlib.contextmanager
        def staging():
            with contextlib.ExitStack() as esS:
                sbS, _ = mk(esS)
                stgs[:] = [(sbS("stg%d" % i, [128, 1024], F32), Buf("stg%d" % i)) for i in range(3)]
                yield
                S.barrier()
            stgs[:] = []

        stg_n = [0]

        def load_w(dst, name, src, K, N, gain=None):
            bufs = []
            for kc in range(K // 128):
                for c0 in range(0, N, 1024):
                    c1 = min(N, c0 + 1024)
                    wd = c1 - c0
                    i = stg_n[0]
                    stg_n[0] += 1
                    st, Bst = stgs[i % 3]
                    S.dma("sp", st[:, 0:wd], src[kc * 128:(kc + 1) * 128, c0:c1], writes=[Bst])
                    b = Buf(name)
                    bufs.append(b)
                    eng = ("act", "dve", "pool")[i % 3]
                    o = dst[:, kc, c0:c1]
                    if gain is not None:
                        g = cc(gain, kc)
                        if eng == "act":
                            ACT(o, st[:, 0:wd], AF.Copy, [Bst, Bc], [b], scale=g)
                        else:
                            TS(eng, o, st[:, 0:wd], g, None, ALU.mult, None, [Bst, Bc], [b])
                    else:
                        CP(eng, o, st[:, 0:wd], [Bst], [b])
            return bufs

        def rms_bcast(xin, Bx, n, sq, Bsq, pbank, Bp, rs, Brs, rstd, Brstd, nfeat=1024):
            kcs = nfeat // 128
            ACT(sq[:, 0:kcs, 0:n], xin, AF.Square, Bx, [Bsq])
            for kc in range(kcs):
                MM(pbank[:, 0:n], ones_bf[:], sq[:, kc, 0:n], kc == 0, kc == kcs - 1, [Bob, Bsq], [Bp])
            ACT(rs[:, 0:n], pbank[:, 0:n], AF.Sqrt, [Bp, Bc], [Brs], scale=1.0 / nfeat, bias=cc("eps"))
            RCP(rstd[:, 0:n], rs[:, 0:n], [Brs], [Brstd])

        with contextlib.ExitStack() as es1:
            if True:
                esA = es1
                sb, ps = mk(esA)
                cos2 = sb("cos2", [128, 64, 32], F32)
                sin2 = sb("sin2", [128, 64, 32], F32)
                Btab = Buf("tab")
                with contextlib.ExitStack() as esT:
                    sbt, _ = mk(esT)
                    pos_i = sbt("pos_i", [128, 64], I32)
                    posf = sbt("posf", [128, 64], F32)
                    ang = sbt("ang", [128, 64, 2, 16], F32)
                    tf = sbt("tf", [128, 2048], F32)
                    ti_ = sbt("ti_", [128, 2048], I32)
                    sc = sbt("sc", [128, 64, 2, 16], F32)
                    Bt = Buf("t")
                    angf = ang[:].rearrange("p a b c -> p (a b c)")
                    S.dma("sp", pos_i[:], pos, writes=[Bt])
                    CP("dve", posf[:], pos_i[:], [Bt], [Bt])
                    TT("dve", ang[:, :, 0, :], posf[:].unsqueeze(2).to_broadcast([128, 64, 16]),
                       cc("invf", 0, 16).unsqueeze(1).to_broadcast([128, 64, 16]), ALU.mult, [Bt, Bc], [Bt])
                    TS("dve", ang[:, :, 1, :], ang[:, :, 0, :], PI / 2, None, ALU.add, None, [Bt], [Bt])
                    C1 = 6.28125
                    C2 = 2 * np.pi - 6.28125
                    TS("dve", tf[:], angf, 1.0 / (2 * np.pi), None, ALU.mult, None, [Bt], [Bt])
                    CP("dve", ti_[:], tf[:], [Bt], [Bt])
                    CP("dve", tf[:], ti_[:], [Bt], [Bt])
                    STT(angf, tf[:], -C1, angf, ALU.mult, ALU.add, [Bt], [Bt])
                    STT(angf, tf[:], -C2, angf, ALU.mult, ALU.add, [Bt], [Bt])
                    TS("dve", tf[:], angf, PI, -2 * PI, ALU.is_gt, ALU.mult, [Bt], [Bt])
                    TT("dve", angf, angf, tf[:], ALU.add, [Bt], [Bt])
                    TS("dve", tf[:], angf, -PI, 2 * PI, ALU.is_lt, ALU.mult, [Bt], [Bt])
                    TT("dve", angf, angf, tf[:], ALU.add, [Bt], [Bt])
                    ACT(sc[:].rearrange("p a b c -> p (a b c)"), angf, AF.Sin, [Bt], [Bt])
                    CP("dve", cos2[:, :, 0:16], sc[:, :, 1, :], [Bt], [Btab])
                    CP("dve", cos2[:, :, 16:32], sc[:, :, 1, :], [Bt, Btab], [Btab])
                    CP("dve", sin2[:, :, 16:32], sc[:, :, 0, :], [Bt, Btab], [Btab])
                    TS("dve", sin2[:, :, 0:16], sc[:, :, 0, :], -1.0, None, ALU.mult, None, [Bt, Btab], [Btab])
                    S.barrier()

                w_in_bf = sb("w_in_bf", [128, 8, 1440], BF16)
                w_uq_bf = sb("w_uq_bf", [128, 2, 768], BF16)
                w_ukv_bf = sb("w_ukv_bf", [128, 1, 1024], BF16)
                with staging():
                    Bwin = load_w(w_in_bf, "w_in_bf", w_in, D, 1440, gain="mixg")
                    Bwuq = load_w(w_uq_bf, "w_uq_bf", w_uq, 256, 768)
                    Bwukv = load_w(w_ukv_bf, "w_ukv_bf", w_ukv, 128, 1024)

                xbuf = [(sb("xg%d" % i, [128, 8, 512], F32), Buf("xg%d" % i)) for i in range(2)]
                sq = sb("sq", [128, 8, 512], BF16)
                Bsq = Buf("sq")
                hT = sb("hT", [128, 8, 512], BF16)
                BhT = Buf("hT")
                rs = sb("rs", [128, 512], F32)
                Brs = Buf("rs")
                rstd = sb("rstd", [128, 512], F32)
                Brstd = Buf("rstd")
                sig = sb("sig", [128, 512], F32)
                Bsig = Buf("sig")
                sqLs = [(sb("sqL%d" % i, [128, 384], F32), Buf("sqL%d" % i)) for i in range(2)]
                st2s = [(sb("st2%d" % i, [128, 8], F32), Buf("st2%d" % i)) for i in range(2)]
                cns = [(sb("cn%d" % i, [128, 384], BF16), Buf("cn%d" % i)) for i in range(2)]
                cTs = [(sb("cT%d" % i, [128, 384], BF16), Buf("cT%d" % i)) for i in range(2)]
                sqq = sb("sqq", [128, 768], F32)
                Bsqq = Buf("sqq")
                stq = sb("stq", [128, 24], F32)
                Bstq = Buf("stq")
                qn = sb("qn", [128, 8, 96], F32)
                Bqn = Buf("qn")
                rt1 = sb("rt1", [128, 8, 32], F32)
                rt2 = sb("rt2", [128, 8, 32], F32)
                Brt = Buf("rt")
                qbf = sb("qbf", [128, 8, 96], BF16)
                Bqbf = Buf("qbf")
                sqk = sb("sqk", [128, 8, 64], F32)
                Bsqk = Buf("sqk")
                sqr = sb("sqr", [128, 32], F32)
                Bsqr = Buf("sqr")
                stk = sb("stk", [128, 32], F32)
                Bstk = Buf("stk")
                kn = sb("kn", [128, 8, 64], F32)
                Bkn = Buf("kn")
                kr = sb("kr", [128, 4, 32], F32)
                Bkr = Buf("kr")
                kbf = sb("kbf", [128, 8, 96], BF16)
                Bkbf = Buf("kbf")
                KTst = [(sb("KTst%d" % i, [96, 8, 512], BF16), Buf("KTst%d" % i)) for i in range(1)]
                QTst = [(sb("QTst%d" % i, [96, 8, 512], BF16), Buf("QTst%d" % i)) for i in range(1)]
                Vst = [(sb("Vst%d" % i, [128, 8, 4, 128], BF16), Buf("Vst%d" % i)) for i in range(1)]
                for (v, bv) in Vst:
                    MSET("pool", v[:, 0:8:2, :, 64:128], 1.0, [bv])
                    MSET("pool", v[:, 1:8:2, :, 0:64], 1.0, [bv])
                pA = ps("pA", [128, 512])
                pG = ps("pG", [128, 512])
                pTk = ps("pTk", [128, 1024], BF16)
                pT = ps("pT", [128, 1024], BF16)
                pQ = ps("pQ", [128, 1024])
                pKV = ps("pKV", [128, 1024])
                BpA, BpG, BpTk, BpT, BpQ, BpKV = [Buf(n) for n in "pA pG pTk pT pQ pKV".split()]
                pLs = [(pA, BpA), (pG, BpG)]
                KT_v = KT_scr.rearrange("h d t -> d h t")
                QT_v = QT_scr.rearrange("h d t -> d h t")
                V_v = V_scr.rearrange("h p (t e) -> p h t e", e=128)
                R96 = 96 ** -0.5

                S.dma("sp", xbuf[0][0][:], xT_v[:, :, 0:512], writes=[xbuf[0][1]])
                for G in range(16):
                    own = G >= 8
                    xg, Bxg = xbuf[G % 2]
                    if G + 1 < 16:
                        S.dma("sp", xbuf[(G + 1) % 2][0][:], xT_v[:, :, (G + 1) * 512:(G + 2) * 512],
                              writes=[xbuf[(G + 1) % 2][1]])
                    rms_bcast(xg[:], [Bxg], 512, sq, Bsq, pA, BpA, rs, Brs, rstd, Brstd)
                    TT("dve", hT[:], xg[:], rstd[:].unsqueeze(1).to_broadcast([128, 8, 512]), ALU.mult,
                       [Bxg, Brstd], [BhT])
                    if own or G == 7:
                        c0, n = (0, 512) if own else (384, 128)
                        qg = (G - 7) if own else 0
                        ucol = 30 + (128 + (G - 8) * 512 if own else 0)
                        for c in range(4):
                            for kc in range(8):
                                MM(pA[:, 0:n], w_in_bf[:, kc, c * 128:(c + 1) * 128], hT[:, kc, c0:c0 + n],
                                   kc == 0, kc == 7, Bwin + [BhT], [BpA])
                            for kc in range(8):
                                MM(pG[:, 0:n], w_in_bf[:, kc, 512 + c * 128:512 + (c + 1) * 128], hT[:, kc, c0:c0 + n],
                                   kc == 0, kc == 7, Bwin + [BhT], [BpG])
                            ACT(sig[:, 0:n], pG[:, 0:n], AF.Sigmoid, [BpG, Bc], [Bsig], bias=cc("bcin", 4 + c))
                            STT(uT[:, c, ucol:ucol + n], pA[:, 0:n], cc("bcin", c), sig[:, 0:n], ALU.add, ALU.mult,
                                [BpA, Bsig, Bc], [Buf_u[qg]])
                            if not own:
                                TS("dve", uT[:, c, ucol:ucol + n], uT[:, c, ucol:ucol + n], cc("flag"), None,
                                   ALU.mult, None, [Buf_u[qg], Bc], [Buf_u[qg]])
                    KTs, BKTs = KTst[0]
                    QTs, BQTs = QTst[0]
                    Vs, BVs = Vst[0]
                    def lat_chain(ti):
                        t = 4 * G + ti
                        hasq = own or t == 31
                        tsl = slice(ti * 128, (ti + 1) * 128)
                        l0 = 0 if hasq else 256
                        pLx, BpLx = pLs[ti % 2]
                        sqLx, BsqLx = sqLs[ti % 2]
                        st2x, Bst2x = st2s[ti % 2]
                        cnx, Bcnx = cns[ti % 2]
                        cTx, BcTx = cTs[ti % 2]
                        for kc in range(8):
                            MM(pLx[:, l0:416], hT[:, kc, tsl], w_in_bf[:, kc, 1024 + l0:1440], kc == 0, kc == 7,
                               Bwin + [BhT], [BpLx])
                        yield
                        if hasq:
                            ACT(sqLx[:, 0:256], pLx[:, 0:256], AF.Square, [BpLx], [BsqLx], scale=1.0 / 16)
                            yield
                        ACT(sqLx[:, 256:384], pLx[:, 256:384], AF.Square, [BpLx], [BsqLx], scale=128 ** -0.5)
                        yield
                        if hasq:
                            RED(st2x[:, 0:1], sqLx[:, 0:256], [BsqLx], [Bst2x])
                            yield
                        RED(st2x[:, 1:2], sqLx[:, 256:384], [BsqLx], [Bst2x])
                        yield
                        ACT(st2x[:, 2 + l0 // 256:4], st2x[:, l0 // 256:2], AF.Sqrt, [Bst2x, Bc], [Bst2x], bias=cc("eps"))
                        yield
                        RCP(st2x[:, 4 + l0 // 256:6], st2x[:, 2 + l0 // 256:4], [Bst2x], [Bst2x])
                        yield
                        if hasq:
                            STT(cnx[:, 0:256], pLx[:, 0:256], st2x[:, 4:5], cc("gql", 0, 256), ALU.mult, ALU.mult,
                                [BpLx, Bst2x, Bc], [Bcnx])
                            yield
                        STT(cnx[:, 256:384], pLx[:, 256:384], st2x[:, 5:6], cc("gkvl", 0, 128), ALU.mult, ALU.mult,
                            [BpLx, Bst2x, Bc], [Bcnx])
                        yield
                        for j in ((0, 1, 2) if hasq else (2,)):
                            TR(pT[:, j * 128:(j + 1) * 128], cnx[:, j * 128:(j + 1) * 128], ident[:], [Bcnx, Bid], [BpT])
                        yield
                        CP("act", cTx[:, l0:384], pT[:, l0:384], [BpT], [BcTx])
                        yield

                    def k_chain(ti):
                        t = 4 * G + ti
                        tsl = slice(ti * 128, (ti + 1) * 128)
                        pLx, BpLx = pLs[ti % 2]
                        cTx, BcTx = cTs[ti % 2]
                        for (a, b) in ((0, 512), (512, 1024)):
                            MM(pKV[:, a:b], cTx[:, 256:384], w_ukv_bf[:, 0, a:b], True, True, Bwukv + [BcTx], [BpKV])
                        yield
                        kvv = pKV[:].rearrange("p (h e) -> p h e", h=8)
                        TT("dve", kr[:, 0, :], pLx[:, 384:416], cc("gkn", 64, 32), ALU.mult, [BpLx, Bc], [Bkr])
                        yield
                        ACT(sqk[:], kvv[:, :, 0:64], AF.Square, [BpKV], [Bsqk], scale=R96)
                        yield
                        ACT(sqr[:], pLx[:, 384:416], AF.Square, [BpLx], [Bsqr], scale=R96)
                        yield
                        TT("pool", kr[:, 1, :], kr[:, 0, :], cos2[:, t, :], ALU.mult, [Bkr, Btab], [Bkr])
                        yield
                        RED(stk[:, 0:8], sqk[:], [Bsqk], [Bstk])
                        yield
                        RED(stk[:, 8:9], sqr[:], [Bsqr], [Bstk])
                        yield
                        TT("pool", kr[:, 2, 0:16], kr[:, 0, 16:32], sin2[:, t, 0:16], ALU.mult, [Bkr, Btab], [Bkr])
                        yield
                        TS("dve", stk[:, 9:10], stk[:, 8:9], EPS, None, ALU.add, None, [Bstk], [Bstk])
                        yield
                        CP("act", Vs[:, 0:8:2, ti, 0:64], kvv[:, 0:8:2, 64:128], [BpKV], [BVs])
                        yield
                        TT("pool", kr[:, 2, 16:32], kr[:, 0, 0:16], sin2[:, t, 16:32], ALU.mult, [Bkr, Btab], [Bkr])
                        yield
                        ACT(stk[:, 16:24], stk[:, 0:8], AF.Sqrt, [Bstk], [Bstk], bias=stk[:, 9:10])
                        yield
                        TT("pool", kr[:, 3, :], kr[:, 1, :], kr[:, 2, :], ALU.add, [Bkr], [Bkr])
                        yield
                        RCP(stk[:, 24:32], stk[:, 16:24], [Bstk], [Bstk])
                        yield
                        CP("act", Vs[:, 1:8:2, ti, 64:128], kvv[:, 1:8:2, 64:128], [BpKV], [BVs])
                        yield
                        rk = stk[:, 24:32]
                        TT("dve", kn[:], kvv[:, :, 0:64], rk.unsqueeze(2).to_broadcast([128, 8, 64]), ALU.mult,
                           [BpKV, Bstk], [Bkn])
                        yield
                        TT("dve", kbf[:, :, 64:96], kr[:, 3:4, :].to_broadcast([128, 8, 32]),
                           rk.unsqueeze(2).to_broadcast([128, 8, 32]), ALU.mult, [Bkr, Bstk], [Bkbf])
                        yield
                        TT("pool", kbf[:, :, 0:64], kn[:], cc("gkn", 0, 64).unsqueeze(1).to_broadcast([128, 8, 64]),
                           ALU.mult, [Bkn, Bc], [Bkbf])
                        yield
                        for h in range(8):
                            TR(pTk[0:96, h * 128:(h + 1) * 128], kbf[:, h, :], ident[:], [Bkbf, Bid], [BpTk])
                        yield
                        CP("act", KTs[:, :, tsl], pTk[0:96, :].rearrange("p (h t) -> p h t", h=8), [BpTk], [BKTs])
                        yield

                    def q_chain(ti):
                        t = 4 * G + ti
                        tsl = slice(ti * 128, (ti + 1) * 128)
                        cTx, BcTx = cTs[ti % 2]
                        for (a, b) in ((0, 512), (512, 768)):
                            for kc in range(2):
                                MM(pQ[:, a:b], cTx[:, kc * 128:(kc + 1) * 128], w_uq_bf[:, kc, a:b], kc == 0, kc == 1,
                                   Bwuq + [BcTx], [BpQ])
                        yield
                        ACT(sqq[:], pQ[:, 0:768], AF.Square, [BpQ], [Bsqq], scale=R96)
                        yield
                        RED(stq[:, 0:8], sqq[:].rearrange("p (h e) -> p h e", h=8), [Bsqq], [Bstq])
                        yield
                        ACT(stq[:, 8:16], stq[:, 0:8], AF.Sqrt, [Bstq, Bc], [Bstq], bias=cc("eps"))
                        yield
                        RCP(stq[:, 16:24], stq[:, 8:16], [Bstq], [Bstq])
                        yield
                        qv = pQ[:, 0:768].rearrange("p (h e) -> p h e", h=8)
                        TT("dve", qn[:], qv, stq[:, 16:24].unsqueeze(2).to_broadcast([128, 8, 96]), ALU.mult,
                           [BpQ, Bstq], [Bqn])
                        yield
                        TT("pool", qn[:], qn[:], cc("gqn", 0, 96).unsqueeze(1).to_broadcast([128, 8, 96]), ALU.mult,
                           [Bqn, Bc], [Bqn])
                        yield
                        TT("pool", rt1[:], qn[:, :, 64:96], cos2[:, t:t + 1, :].to_broadcast([128, 8, 32]), ALU.mult,
                           [Bqn, Btab], [Brt])
                        yield
                        TT("pool", rt2[:, :, 0:16], qn[:, :, 80:96], sin2[:, t:t + 1, 0:16].to_broadcast([128, 8, 16]),
                           ALU.mult, [Bqn, Btab, Brt], [Brt])
                        yield
                        TT("pool", rt2[:, :, 16:32], qn[:, :, 64:80], sin2[:, t:t + 1, 16:32].to_broadcast([128, 8, 16]),
                           ALU.mult, [Bqn, Btab, Brt], [Brt])
                        yield
                        TT("dve", qbf[:, :, 64:96], rt1[:], rt2[:], ALU.add, [Brt], [Bqbf])
                        yield
                        CP("act", qbf[:, :, 0:64], qn[:, :, 0:64], [Bqn], [Bqbf])
                        yield
                        for h in range(8):
                            TR(pT[0:96, h * 128:(h + 1) * 128], qbf[:, h, :], ident[:], [Bqbf, Bid], [BpT])
                        yield
                        qsl = tsl if own else slice(0, 128)
                        CP("dve", QTs[:, :, qsl], pT[0:96, :].rearrange("p (h t) -> p h t", h=8), [BpT], [BQTs])
                        yield
                        if not own:
                            S.dma("sp", QT_v[:, :, 0:128], QTs[:, :, 0:128], reads=[BQTs], writes=[B_QTs[0]])

                    def rr(*gens):
                        gens = [g_ for g_ in gens if g_ is not None]
                        while gens:
                            for g_ in list(gens):
                                try:
                                    next(g_)
                                except StopIteration:
                                    gens.remove(g_)

                    rr(lat_chain(0))
                    for ti in range(4):
                        hasq_ = own or (4 * G + ti) == 31
                        rr(k_chain(ti), q_chain(ti) if hasq_ else None, lat_chain(ti + 1) if ti < 3 else None)
                    S.dma("sp", KT_v[:, :, G * 512:(G + 1) * 512], KTs[:], reads=[BKTs], writes=[B_KTs[G]])
                    S.dma("sp", V_v[:, :, 4 * G:4 * G + 4, :], Vs[:], reads=[BVs], writes=[B_Vs[G]])
                    if own:
                        q0 = 128 + (G - 8) * 512
                        S.dma("sp", QT_v[:, :, q0:q0 + 512], QTs[:], reads=[BQTs], writes=[B_QTs[G - 7]])
                S.barrier()
        if dbg:
            S.dma("sp", dbg_out["uT"], uT[:], reads=Buf_u + [Bupad])
        attnT = esX.enter_context(nc.sbuf_tensor("attnT", [128, 4, NQ], BF16))
        if True:

            if stop_after >= 2:
                with contextlib.ExitStack() as esB:
                    sb, ps = mk(esB)
                    Dg = sb("Dg", [128, 4, 31, 128], BF16)
                    BDg = [Buf("Dg%d" % c) for c in range(4)]
                    for c in range(4):
                        for k in range(31):
                            TS(("dve", "pool")[k % 2], Dg[:, c, k, :], ident[:], cc("wdw", c * 31 + k), None, ALU.mult, None,
                               [Bid, Bc], [BDg[c]])
                    pY = [(ps("pY%d" % i, [128, 512]), Buf("pY%d" % i)) for i in range(2)]
                    pM = ps("pM", [128, 512])
                    pM2 = ps("pM2", [128, 512])
                    BpM, BpM2 = Buf("pM"), Buf("pM2")
                    ybf = sb("ybf", [128, 4, 512], BF16)
                    ysq = sb("ysq", [128, 4, 512], BF16)
                    Bybf = [Buf("ybf%d" % c) for c in range(4)]
                    Bysq = [Buf("ysq%d" % c) for c in range(4)]
                    mean = sb("mean", [128, 512], F32)
                    m2 = sb("m2", [128, 512], F32)
                    var = sb("var", [128, 512], F32)
                    rsd = sb("rsd", [128, 512], F32)
                    Bmean, Bm2, Bvar, Brsd = Buf("mean"), Buf("m2"), Buf("var"), Buf("rsd")
                    tmp = [(sb("lt%d" % i, [128, 512], F32), Buf("lt%d" % i)) for i in range(2)]
                    yi = 0
                    for g in reversed(range(9)):
                        n = 128 if g == 0 else 512
                        oc = 0 if g == 0 else 128 + (g - 1) * 512
                        rd = [Bupad] + Buf_u[max(0, g - 1):g + 1]
                        for c in range(4):
                            py, Bpy = pY[yi % 2]
                            yi += 1
                            for k in range(31):
                                MM(py[:, 0:n], Dg[:, c, k, :], uT[:, c, oc + k:oc + k + n], k == 0, k == 30, rd + [BDg[c]], [Bpy])
                            ACT(ybf[:, c, 0:n], py[:, 0:n], AF.Identity, [Bpy, Bc], [Bybf[c]], bias=cc("bdw", c))
                            ACT(ysq[:, c, 0:n], py[:, 0:n], AF.Square, [Bpy, Bc], [Bysq[c]], bias=cc("bdw", c))
                        for c in range(4):
                            MM(pM[:, 0:n], ones_bf[:], ybf[:, c, 0:n], c == 0, c == 3, [Bob, Bybf[c]], [BpM])
                        for c in range(4):
                            MM(pM2[:, 0:n], ones_bf[:], ysq[:, c, 0:n], c == 0, c == 3, [Bob, Bysq[c]], [BpM2])
                        TS("dve", mean[:, 0:n], pM[:, 0:n], 1.0 / 512, None, ALU.mult, None, [BpM], [Bmean])
                        TT("dve", m2[:, 0:n], mean[:, 0:n], mean[:, 0:n], ALU.mult, [Bmean], [Bm2])
                        STT(var[:, 0:n], pM2[:, 0:n], 1.0 / 512, m2[:, 0:n], ALU.mult, ALU.subtract, [BpM2, Bm2], [Bvar])
                        ACT(var[:, 0:n], var[:, 0:n], AF.Sqrt, [Bvar, Bc], [Bvar], bias=cc("eps"))
                        RCP(rsd[:, 0:n], var[:, 0:n], [Bvar], [Brsd])
                        for c in range(4):
                            tm, Btm = tmp[c % 2]
                            TT("dve", tm[:, 0:n], ybf[:, c, 0:n], mean[:, 0:n], ALU.subtract, [Bybf[c], Bmean], [Btm])
                            TT("pool", tm[:, 0:n], tm[:, 0:n], rsd[:, 0:n], ALU.mult, [Btm, Brsd], [Btm])
                            ACT(uT[:, c, 30 + oc:30 + oc + n], tm[:, 0:n], AF.Silu, [Btm, Bc], [Buf_u[g]],
                                scale=cc("lng", c), bias=cc("lnb", c))
                    S.barrier()
                if dbg:
                    S.dma("sp", dbg_out["ufT"], uT[:, :, 30:30 + NQ], reads=Buf_u)

        memK = esX.enter_context(nc.sbuf_tensor("memK", [128, 8, 256], BF16))
        memV = esX.enter_context(nc.sbuf_tensor("memV", [128, 2, 1024], BF16))
        BmK, BmV = Buf("memK"), Buf("memV")
        w_out_bf = esX.enter_context(nc.sbuf_tensor("w_out_bf", [128, 8, D], BF16))
        w_mq_bf = esX.enter_context(nc.sbuf_tensor("w_mq_bf", [128, 8, D], BF16))
        w_mo_bf = esX.enter_context(nc.sbuf_tensor("w_mo_bf", [128, 8, D], BF16))
        stgP = (esX.enter_context(nc.sbuf_tensor("stgP", [128, 512], F32)), Buf("stgP"))
        if stop_after >= 3:
            with contextlib.ExitStack() as esK:
                sb, ps = mk(esK)
                pz = [(ps("pzk%d" % i, [128, 512]), Buf("pz_k%d" % i)) for i in range(6)]
                sq = sb("sqm", [128, 8, 256], BF16)
                Bsq = Buf("sqm")
                rs = sb("rsm", [128, 512], F32)
                Brs = Buf("rsm")
                rstd = sb("rstdm", [128, 512], F32)
                Brstd = Buf("rstdm")
                with contextlib.ExitStack() as esM:
                    sbm, _ = mk(esM)
                    w_mkv_bf = sbm("w_mkv_bf", [128, 8, 2 * D], BF16)
                    with staging():
                        Bwmkv = load_w(w_mkv_bf, "w_mkv_bf", w_mkv, D, 2 * D, gain="memmg")
                    mx = sbm("mx", [128, 8, 256], F32)
                    Bmx = Buf("mx")
                    hm = sbm("hm", [128, 8, 256], BF16)
                    Bhm = Buf("hm")
                    S.dma("sp", mx[:], memT.rearrange("(kc p) t -> p kc t", p=128), writes=[Bmx])
                    rms_bcast(mx[:], [Bmx], 256, sq, Bsq, pz[0][0], pz[0][1], rs, Brs, rstd, Brstd)
                    TT("dve", hm[:], mx[:], rstd[:, 0:256].unsqueeze(1).to_broadcast([128, 8, 256]), ALU.mult, [Bmx, Brstd], [Bhm])
                    ksq = sbm("ksq", [128, 2, 256], BF16)
                    Bksq = Buf("ksq")
                    for hd in range(4):
                        for j in range(2):
                            ch = hd * 2 + j
                            p_, Bp_ = pz[1 + j]
                            for kc in range(8):
                                MM(p_[:, 0:256], w_mkv_bf[:, kc, ch * 128:(ch + 1) * 128], hm[:, kc, :], kc == 0, kc == 7,
                                   Bwmkv + [Bhm], [Bp_])
                            ACT(ksq[:, j, :], p_[:, 0:256], AF.Square, [Bp_], [Bksq], scale=1.0 / 16)
                        for j in range(2):
                            MM(pz[3][0][:, 0:256], ones_bf[:], ksq[:, j, :], j == 0, j == 1, [Bob, Bksq], [pz[3][1]])
                        ACT(rs[:, 0:256], pz[3][0][:, 0:256], AF.Sqrt, [pz[3][1], Bc], [Brs], bias=cc("eps"))
                        RCP(rstd[:, 0:256], rs[:, 0:256], [Brs], [Brstd])
                        for j in range(2):
                            STT(memK[:, hd * 2 + j, :], pz[1 + j][0][:, 0:256], cc("mkg", j), rstd[:, 0:256], ALU.mult, ALU.mult,
                                [pz[1 + j][1], Brstd, Bc], [BmK])
                    for kt in range(2):
                        for nb in range(2):
                            p_, Bp_ = pz[4 + nb]
                            for kc in range(8):
                                MM(p_[:], hm[:, kc, kt * 128:(kt + 1) * 128], w_mkv_bf[:, kc, 1024 + nb * 512:1024 + (nb + 1) * 512],
                                   kc == 0, kc == 7, Bwmkv + [Bhm], [Bp_])
                            CP("act", memV[:, kt, nb * 512:(nb + 1) * 512], p_[:], [Bp_], [BmV])
                    S.barrier()

        if stop_after >= 3:
            with contextlib.ExitStack() as es2:
                sb, ps = mk(es2)
                kT = [(sb("kT%d" % i, [96, SEQ], BF16), Buf("kT%d" % i)) for i in range(2)]
                vA = [(sb("vA%d" % i, [128, 64, 128], BF16), Buf("vA%d" % i)) for i in range(2)]
                qT = [(sb("qT%d" % i, [96, NQ], BF16), Buf("qT%d" % i)) for i in range(1)] * 2
                NP = 3
                Pb = [(sb("P%d" % i, [128, 512], BF16), Buf("P%d" % i)) for i in range(NP)]
                pS = [(ps("pS%d" % i, [128, 512]), Buf("pS%d" % i)) for i in range(4)]
                pO = [(ps("pO%d" % i, [128, 512]), Buf("pO%d" % i)) for i in range(2)]
                pB = ps("pB", [128, 512])
                BpB = Buf("pB")
                rrow = sb("rrow", [128, 512], F32)
                Brrow = Buf("rrow")
                bcs, Bbcs = rrow, Brrow
                Bscr = Buf("scr")
                SC = 96 ** -0.5

                def load_head(h):
                    hb = h % 2
                    S.dma("sp", kT[hb][0][:], KT_scr[h], reads=B_KTs, writes=[kT[hb][1]])
                    S.dma("sp", vA[hb][0][:], V_scr[h].rearrange("p (t e) -> p t e", e=128), reads=B_Vs, writes=[vA[hb][1]])

                def load_q(h):
                    S.dma("sp", qT[0][0][:], QT_scr[h], reads=B_QTs, writes=[qT[0][1]])

                load_head(0)
                load_q(0)
                sidx = 0
                oidx = 0
                Bwo, Bwmq, Bwmo = [], [], []
                pf = []
                for (dst_, src_, gain_, lst_) in ((w_out_bf, w_out, None, Bwo), (w_mq_bf, w_mq, "memxg", Bwmq),
                                                   (w_mo_bf, w_mo, None, Bwmo)):
                    for kc_ in range(8):
                        for hc_ in range(2):
                            pf.append((dst_, src_, gain_, lst_, kc_, hc_))

                def prefetch_piece():
                    if not pf:
                        return
                    dst_, src_, gain_, lst_, kc_, hc_ = pf.pop(0)
                    st_, Bst_ = stgP
                    cs_ = slice(hc_ * 512, (hc_ + 1) * 512)
                    S.dma("sp", st_[:], src_[kc_ * 128:(kc_ + 1) * 128, cs_], writes=[Bst_])
                    b_ = Buf("wpf")
                    lst_.append(b_)
                    eng_ = "pool" if len(pf) % 2 else "dve"
                    if gain_ is not None:
                        TS(eng_, dst_[:, kc_, cs_], st_[:], cc(gain_, kc_), None, ALU.mult, None, [Bst_, Bc], [b_])
                    else:
                        CP(eng_, dst_[:, kc_, cs_], st_[:], [Bst_], [b_])
                for h in range(8):
                    hb = h % 2
                    if h + 1 < 8:
                        load_head(h + 1)
                    k_t, Bk = kT[hb]
                    v_t, Bv = vA[hb]
                    q_t, Bq = qT[hb]
                    sr = 64 if h % 2 == 0 else 0
                    orow = 0 if h % 2 == 0 else 64
                    for g in range(9):
                        if g == 0:
                            N, q0, nk = 128, 0, 32
                            steps = [(t, 0, False) for t in range(31)] + [(31, 0, True)]
                        else:
                            N, q0, nk = 512, 128 + 512 * (g - 1), 32 + 4 * g
                            steps = [(t, 0, False) for t in range(nk - 4)] + [(nk - 4 + d, 128 * d, True) for d in range(4)]
                        po, Bpo = pO[oidx % 2]
                        oidx += 1
                        ns = len(steps)
                        prefetch_piece()

                        def emit_S(i):
                            t, cs, dg = steps[i]
                            p_s, Bps = pS[(sidx + i) % 4]
                            MM(p_s[:, cs:N], k_t[:, t * 128:(t + 1) * 128], q_t[:, q0 + cs:q0 + N], True, True, [Bk, Bq], [Bps])

                        for i in range(min(3, ns)):
                            emit_S(i)
                        for i in range(ns):
                            t, cs, dg = steps[i]
                            p_s, Bps = pS[(sidx + i) % 4]
                            pb, Bpb = Pb[(sidx + i) % NP]
                            ACT(pb[:, cs:N], p_s[:, cs:N], AF.Exp, [Bps, Bc], [Bpb], scale=SC, bias=cc("kbias", t))
                            if dg:
                                MSET("pool", pb[64:128, cs:cs + 64], 0.0, [Bpb])
                            MM(po[:, cs:N], v_t[:, t, :], pb[:, cs:N], i == 0, i == ns - 1, [Bv, Bpb], [Bpo])
                            if i + 3 < ns:
                                emit_S(i + 3)
                        sidx += ns
                        if g == 0:
                            TS("dve", rrow[sr:sr + 1, 0:N], po[sr:sr + 1, 0:N], 1e-30, None, ALU.max, None, [Bpo], [Brrow])
                            RCP(rrow[sr:sr + 1, 0:N], rrow[sr:sr + 1, 0:N], [Brrow], [Brrow])
                        else:
                            RCP(rrow[sr:sr + 1, 0:N], po[sr:sr + 1, 0:N], [Bpo], [Brrow])
                        MM(pB[:, 0:N], ones_f[sr:sr + 1, :], rrow[sr:sr + 1, 0:N], True, True, [Bof, Brrow], [BpB])
                        CP("dve", bcs[orow:orow + 64, 0:N], pB[orow:orow + 64, 0:N], [BpB], [Bbcs])
                        TT("dve", attnT[orow:orow + 64, h // 2, q0:q0 + N], po[orow:orow + 64, 0:N], bcs[orow:orow + 64, 0:N],
                           ALU.mult, [Bpo, Bbcs], [Buf_at[g]])
                    if h + 1 < 8:
                        load_q(h + 1)
                S.barrier()
            if dbg:
                S.dma("sp", dbg_out["attnT"], attnT[:], reads=Buf_at)

        if stop_after >= 4:
            with contextlib.ExitStack() as es3:
                sb, ps = mk(es3)
                pz = [(ps("pz%d" % i, [128, 512]), Buf("pz%d" % i)) for i in range(8)]
                hq = sb("hq", [128, 8, 512], BF16)
                Bhq = Buf("hq")
                sq, Bsq = hq, Bhq
                rs = sb("rs3", [128, 512], F32)
                Brs = Buf("rs3")
                rstd = sb("rstd3", [128, 512], F32)
                Brstd = Buf("rstd3")
                xb3 = [(sb("x3_%d" % i, [128, 8, 512], F32), Buf("x3_%d" % i)) for i in range(2)]
                qraws = [(sb("qraw%d" % i, [128, 2, 512], BF16), Buf("qraw%d" % i)) for i in range(2)]
                qsqs = [(sb("qsq%d" % i, [128, 2, 512], BF16), Buf("qsq%d" % i)) for i in range(2)]
                qmns = [(sb("qmn%d" % i, [128, 2, 512], BF16), Buf("qmn%d" % i)) for i in range(2)]
                pms = [[(sb("pm%d_%d" % (l_, i), [128, 512], BF16), Buf("pm%d_%d" % (l_, i))) for i in range(2)] for l_ in range(2)]
                rss = [(sb("rsx%d" % i, [128, 512], F32), Buf("rsx%d" % i)) for i in range(2)]
                rcss = [(sb("rcs%d" % i, [128, 512], F32), Buf("rcs%d" % i)) for i in range(2)]
                om = sb("om", [128, 8, 512], BF16)
                Bom = [Buf("om%d" % i) for i in range(4)]

                def xcols(g):
                    return (3968, 128) if g == 0 else (4096 + (g - 1) * 512, 512)

                c_, n_ = xcols(0)
                S.dma("sp", xb3[0][0][:, :, 0:n_], xT_v[:, :, c_:c_ + n_], writes=[xb3[0][1]])
                for g in range(9):
                    n = 128 if g == 0 else 512
                    oc = 0 if g == 0 else 128 + (g - 1) * 512
                    xg, Bxg = xb3[g % 2]
                    if g + 1 < 9:
                        c_, n_ = xcols(g + 1)
                        S.dma("sp", xb3[(g + 1) % 2][0][:, :, 0:n_], xT_v[:, :, c_:c_ + n_], writes=[xb3[(g + 1) % 2][1]])
                    for dc in range(8):
                        p_, Bp_ = pz[dc % 2]
                        for kc in range(8):
                            if kc < 4:
                                rhs_, bsrc = uT[:, kc, 30 + oc:30 + oc + n], Buf_u[g]
                            else:
                                rhs_, bsrc = attnT[:, kc - 4, oc:oc + n], Buf_at[g]
                            MM(p_[:, 0:n], w_out_bf[:, kc, dc * 128:(dc + 1) * 128], rhs_, kc == 0, kc == 7,
                               Bwo + [bsrc], [Bp_])
                        TT("dve", xg[:, dc, 0:n], xg[:, dc, 0:n], p_[:, 0:n], ALU.add, [Bxg, Bp_], [Bxg])
                    rms_bcast(xg[:, :, 0:n], [Bxg], n, sq, Bsq, pz[0][0], pz[0][1], rs, Brs, rstd, Brstd)
                    TT("dve", hq[:, :, 0:n], xg[:, :, 0:n], rstd[:, 0:n].unsqueeze(1).to_broadcast([128, 8, n]), ALU.mult,
                       [Bxg, Brstd], [Bhq])
                    def head_chain(hd):
                        L = hd % 2
                        pq_, Bpq_ = pz[2 + 3 * L]
                        pn_, Bpn_ = pz[3 + 3 * L]
                        psc, Bpsc = pz[4 + 3 * L]
                        qraw, Bqraw = qraws[L]
                        qsq, Bqsq = qsqs[L]
                        qmn, Bqmn = qmns[L]
                        rsx, Brsx = rss[L]
                        rcs, Brcs = rcss[L]
                        for j in range(2):
                            ch = hd * 2 + j
                            for kc in range(8):
                                MM(pq_[:, 0:n], w_mq_bf[:, kc, ch * 128:(ch + 1) * 128], hq[:, kc, 0:n], kc == 0, kc == 7,
                                   Bwmq + [Bhq], [Bpq_])
                            yield
                            ACT(qsq[:, j, 0:n], pq_[:, 0:n], AF.Square, [Bpq_], [Bqsq], scale=1.0 / 16)
                            yield
                            CP("dve", qraw[:, j, 0:n], pq_[:, 0:n], [Bpq_], [Bqraw])
                            yield
                        for j in range(2):
                            MM(pn_[:, 0:n], ones_bf[:], qsq[:, j, 0:n], j == 0, j == 1, [Bob, Bqsq], [Bpn_])
                        yield
                        ACT(rsx[:, 0:n], pn_[:, 0:n], AF.Sqrt, [Bpn_, Bc], [Brsx], bias=cc("eps"))
                        yield
                        RCP(rsx[:, 0:n], rsx[:, 0:n], [Brsx], [Brsx])
                        yield
                        for j in range(2):
                            STT(qmn[:, j, 0:n], qraw[:, j, 0:n], cc("mqg", j), rsx[:, 0:n], ALU.mult, ALU.mult,
                                [Bqraw, Brsx, Bc], [Bqmn])
                            yield
                        for kt in range(2):
                            for j in range(2):
                                MM(psc[:, 0:n], memK[:, hd * 2 + j, kt * 128:(kt + 1) * 128], qmn[:, j, 0:n], j == 0, j == 1,
                                   [BmK, Bqmn], [Bpsc])
                            yield
                            ACT(pms[L][kt][0][:, 0:n], psc[:, 0:n], AF.Exp, [Bpsc], [pms[L][kt][1]], scale=1.0 / 16)
                            yield
                        for kt in range(2):
                            MM(pn_[:, 0:n], ones_bf[:], pms[L][kt][0][:, 0:n], kt == 0, kt == 1, [Bob, pms[L][kt][1]], [Bpn_])
                        yield
                        RCP(rcs[:, 0:n], pn_[:, 0:n], [Bpn_], [Brcs])
                        yield
                        for j in range(2):
                            for kt in range(2):
                                MM(pq_[:, 0:n], memV[:, kt, (hd * 2 + j) * 128:(hd * 2 + j + 1) * 128], pms[L][kt][0][:, 0:n],
                                   kt == 0, kt == 1, [BmV, pms[L][kt][1]], [Bpq_])
                            yield
                            TT("dve", om[:, hd * 2 + j, 0:n], pq_[:, 0:n], rcs[:, 0:n], ALU.mult, [Bpq_, Brcs], [Bom[hd]])
                            yield

                    def rr3(*gens):
                        gens = list(gens)
                        while gens:
                            for g_ in list(gens):
                                try:
                                    next(g_)
                                except StopIteration:
                                    gens.remove(g_)

                    rr3(head_chain(0), head_chain(1))
                    rr3(head_chain(2), head_chain(3))
                    for dc in range(8):
                        p_, Bp_ = pz[dc % 2]
                        for kc in range(8):
                            MM(p_[:, 0:n], w_mo_bf[:, kc, dc * 128:(dc + 1) * 128], om[:, kc, 0:n], kc == 0, kc == 7,
                               Bwmo + [Bom[kc // 2]], [Bp_])
                        TT("dve", xg[:, dc, 0:n], xg[:, dc, 0:n], p_[:, 0:n], ALU.add, [Bxg, Bp_], [Bxg])
                    S.dma("sp", x2_v[:, :, oc:oc + n], xg[:, :, 0:n], reads=[Bxg], writes=[B_x2[g]])
                S.barrier()
        esX.close()

        if stop_after >= 5:
            with contextlib.ExitStack() as es4:
                sb, ps = mk(es4)
                w_up_bf = sb("w_up_bf", [128, 8, 2 * DFF], BF16)
                w_dn_bf = sb("w_dn_bf", [128, 22, D], BF16)
                stg4 = [(sb("stg4_%d" % i, [128, 512], F32), Buf("stg4_%d" % i)) for i in range(3)]
                x4 = sb("x4", [128, 8, 512], F32)
                Bx4 = Buf("x4")
                x4h = sb("x4h", [128, 8, 2], F32)
                Bx4h = Buf("x4h")
                h3 = sb("h3", [128, 8, 512], BF16)
                Bh3 = Buf("h3")
                h3h = sb("h3h", [128, 8, 2], BF16)
                Bh3h = Buf("h3h")
                sq, Bsq = h3, Bh3
                rs = sb("rs4", [128, 512], F32)
                Brs = Buf("rs4")
                rstd, Brstd = rs, Brs
                prev = sb("prev", [128, 44, 2], F32)
                Bprev = [Buf("prev%d" % r) for r in range(44)]
                upx = [(sb("upx%d" % i, [128, 514], F32), Buf("upx%d" % i)) for i in range(2)]
                yv = [(sb("yv%d" % i, [128, 512], F32), Buf("yv%d" % i)) for i in range(3)]
                actT = sb("actT", [128, 22, 512], BF16)
                Bact = [Buf("act%d" % j) for j in range(22)]
                ost = [(sb("ost%d" % i, [128, 512], F32), Buf("ost%d" % i)) for i in range(2)]
                pu = [(ps("pu%d" % i, [128, 512]), Buf("pu%d" % i)) for i in range(4)]
                pd = [(ps("pd%d" % i, [128, 512]), Buf("pd%d" % i)) for i in range(2)]
                pss = ps("pss", [128, 512])
                Bpss = Buf("pss")
                Bup = {}
                Bdn = {}
                s4 = [0]

                def load_up_block(c0_, c1_):
                    for c0 in range(c0_, c1_, 512):
                        _load_up_piece(c0, min(c1_, c0 + 512))

                def _load_up_piece(c0, c1):
                    for kc in range(8):
                        i = s4[0]
                        s4[0] += 1
                        st, Bst = stg4[i % 3]
                        S.dma("sp", st[:, 0:c1 - c0], w_up[kc * 128:(kc + 1) * 128, c0:c1], writes=[Bst])
                        b_ = Buf("wup")
                        eng = ("pool", "dve", "act")[i % 3]
                        o = w_up_bf[:, kc, c0:c1]
                        if eng == "act":
                            ACT(o, st[:, 0:c1 - c0], AF.Copy, [Bst, Bc], [b_], scale=cc("ffng", kc))
                        else:
                            TS(eng, o, st[:, 0:c1 - c0], cc("ffng", kc), None, ALU.mult, None, [Bst, Bc], [b_])
                        for r in range(c0 // 128, c1 // 128):
                            Bup.setdefault(r, []).append(b_)

                def load_dn(kc):
                    Bdn[kc] = []
                    for hc in range(2):
                        i = s4[0]
                        s4[0] += 1
                        st, Bst = stg4[i % 3]
                        S.dma("sp", st[:], w_dn[kc * 128:(kc + 1) * 128, hc * 512:(hc + 1) * 512], writes=[Bst])
                        b_ = Buf("wdn")
                        CP(("pool", "dve", "act")[i % 3], w_dn_bf[:, kc, hc * 512:(hc + 1) * 512], st[:], [Bst], [b_])
                        Bdn[kc].append(b_)

                S.dma("sp", x4h[:], x2_v[:, :, 126:128], reads=[B_x2[0]], writes=[Bx4h])
                S.dma("sp", x4[:], x2_v[:, :, 128:640], reads=[B_x2[1]], writes=[Bx4])
                load_up_block(0, 1024)
                load_up_block(2816, 3840)
                rms_bcast(x4h[:], [Bx4h], 2, h3h, Bh3h, pss, Bpss, rs, Brs, rstd, Brstd)
                TT("dve", h3h[:], x4h[:], rstd[:, 0:2].unsqueeze(1).to_broadcast([128, 8, 2]), ALU.mult,
                   [Bx4h, Brstd], [Bh3h])
                ui = 0
                dn_next = [0]
                for g in range(8):
                    oc = 128 + g * 512
                    if g > 0:
                        S.dma("sp", x4[:], x2_v[:, :, oc:oc + 512], reads=[B_x2[g + 1]], writes=[Bx4])
                    rms_bcast(x4[:], [Bx4], 512, sq, Bsq, pss, Bpss, rs, Brs, rstd, Brstd)
                    TT("dve", h3[:], x4[:], rstd[:].unsqueeze(1).to_broadcast([128, 8, 512]), ALU.mult, [Bx4, Brstd], [Bh3])
                    for j in range(22):
                        if g == 0:
                            if j == 0:
                                load_up_block(1024, 2048)
                                load_up_block(3840, 4864)
                            if j == 6:
                                load_up_block(2048, 2816)
                                load_up_block(4864, 5632)
                            if j >= 10:
                                for _ in range(2):
                                    if dn_next[0] < 22:
                                        load_dn(dn_next[0])
                                        dn_next[0] += 1
                        ys = []
                        for r in (j, 22 + j):
                            p_, Bp_ = pu[ui % 4]
                            ux, Bux = upx[ui % 2]
                            y_, By_ = yv[ui % 3]
                            ui += 1
                            if g == 0:
                                ph, Bph = pu[ui % 4]
                                for kc in range(8):
                                    MM(ph[:, 0:2], w_up_bf[:, kc, r * 128:(r + 1) * 128], h3h[:, kc, :], kc == 0, kc == 7,
                                       Bup[r] + [Bh3h], [Bph])
                                TS("dve", prev[:, r, :], ph[:, 0:2], cc("flag"), None, ALU.mult, None, [Bph, Bc], [Bprev[r]])
                            for kc in range(8):
                                MM(p_[:], w_up_bf[:, kc, r * 128:(r + 1) * 128], h3[:, kc, :], kc == 0, kc == 7, Bup[r] + [Bh3], [Bp_])
                            CP("act", ux[:, 2:514], p_[:], [Bp_], [Bux])
                            ACT(y_[:], p_[:], AF.Identity, [Bp_, Bc], [By_], scale=cc("wffn", r * 3 + 2), bias=cc("bffn", r))
                            CP("pool", ux[:, 0:2], prev[:, r, :], [Bprev[r], Bux], [Bux])
                            STT(y_[:], ux[:, 1:513], cc("wffn", r * 3 + 1), y_[:], ALU.mult, ALU.add, [Bux, By_, Bc], [By_])
                            STT(y_[:], ux[:, 0:512], cc("wffn", r * 3 + 0), y_[:], ALU.mult, ALU.add, [Bux, By_, Bc], [By_])
                            CP("pool", prev[:, r, :], ux[:, 512:514], [Bux], [Bprev[r]])
                            ys.append((y_, By_))
                        ACT(ys[0][0][:], ys[0][0][:], AF.Silu, [ys[0][1]], [ys[0][1]])
                        TT("dve", actT[:, j, :], ys[0][0][:], ys[1][0][:], ALU.mult, [ys[0][1], ys[1][1]], [Bact[j]])
                    for dc in range(8):
                        p_, Bp_ = pd[dc % 2]
                        o_, Bo_ = ost[dc % 2]
                        for kc in range(22):
                            MM(p_[:], w_dn_bf[:, kc, dc * 128:(dc + 1) * 128], actT[:, kc, :], kc == 0, kc == 21, Bdn[kc] + [Bact[kc]], [Bp_])
                        TT("dve", o_[:], x4[:, dc, :], p_[:], ALU.add, [Bx4, Bp_], [Bo_])
                        S.dma("sp", yT_v[:, dc, g * 512:(g + 1) * 512], o_[:], reads=[Bo_])
        S.finish()
    return nc, S


def make_in_maps(inputs):
    x = np.asarray(inputs["x"], np.float32)
    mem = np.asarray(inputs["mem"], np.float32)
    positions = np.asarray(inputs["positions"], np.int32)
    shared = {
        "w_in": np.ascontiguousarray(inputs["w_in"][0], np.float32),
        "w_uq": np.ascontiguousarray(inputs["w_uq"][0], np.float32),
        "w_ukv": np.ascontiguousarray(inputs["w_ukv"][0], np.float32),
        "w_out": np.ascontiguousarray(inputs["w_out"][0], np.float32),
        "w_mem_q": np.ascontiguousarray(inputs["w_mem_q"][0], np.float32),
        "w_mem_kv": np.ascontiguousarray(inputs["w_mem_kv"][0], np.float32),
        "w_mem_o": np.ascontiguousarray(inputs["w_mem_o"][0], np.float32),
        "w_up": np.ascontiguousarray(inputs["w_up"][0], np.float32),
        "w_down": np.ascontiguousarray(inputs["w_down"][0], np.float32),
    }
    csts = [pack_consts(inputs, 0), pack_consts(inputs, 1)]
    in_maps = []
    for core in range(8):
        b, half = core // 2, core % 2
        xT = np.zeros((D, SEQ), np.float32)
        p = np.zeros((SEQ,), np.int32)
        if half == 1:
            xT[:] = x[b].T
            p[:] = positions[b]
        else:
            xT[:, 4096:] = x[b, 0:4096].T
            p[4096:] = positions[b, 0:4096]
        m = dict(shared)
        m["xT"] = xT
        m["pos"] = np.ascontiguousarray(p.reshape(64, 128).T)
        m["cst"] = csts[half]
        m["memT"] = np.ascontiguousarray(mem[b].T)
        in_maps.append(m)
    return in_maps


_PROG = {}


def kernel(**inputs):
    if "nc" not in _PROG:
        _PROG["nc"] = build_program()[0]
    nc = _PROG["nc"]
    in_maps = make_in_maps(inputs)
    res = run_bass_kernel_spmd(nc, in_maps, core_ids=list(range(8)))
    out = np.empty((4, SEQ, D), np.float32)
    for core in range(8):
        b, half = core // 2, core % 2
        out[b, half * 4096:(half + 1) * 4096, :] = res.results[core]["yT"].T
    return out
```

```python
import contextlib
import numpy as np
import concourse.bass as bass
import concourse.mybir as mybir
from concourse.bass_utils import run_bass_kernel_spmd

F32 = mybir.dt.float32
BF16 = mybir.dt.bfloat16
I32 = mybir.dt.int32
AF = mybir.ActivationFunctionType
ALU = mybir.AluOpType
AX = mybir.AxisListType

D = 1024
SEQ = 8192
NTOK = 4096
NQ = 4224
EPS = 1e-6
DFF = 2816
PI = float(np.pi)

_COLS = [("mixg", 8), ("bcin", 8), ("wdw", 124), ("bdw", 4), ("lng", 4), ("lnb", 4), ("memxg", 8),
         ("memmg", 8), ("mqg", 2), ("mkg", 2), ("ffng", 8), ("wffn", 132), ("bffn", 44), ("eps", 1),
         ("flag", 1), ("kbias", 64), ("gql", 256), ("gkvl", 128), ("gqn", 96), ("gkn", 96),
         ("invf", 16), ("ident", 128)]
C = {}
_o = 0
for _n, _w in _COLS:
    C[_n] = _o
    _o += _w
NCST = _o


def _colmajor(v, k):
    return np.ascontiguousarray(np.asarray(v, np.float32).reshape(k, 128).T)


def pack_consts(inp, half):
    c = np.zeros((128, NCST), np.float32)

    def put(name, arr):
        arr = np.asarray(arr, np.float32)
        c[:, C[name]:C[name] + arr.shape[1]] = arr

    put("mixg", _colmajor(inp["mix_norm_g"][0], 8))
    put("bcin", _colmajor(inp["b_conv_in"][0], 8))
    wdw = np.asarray(inp["w_conv_dw"][0], np.float32)
    put("wdw", wdw.reshape(31, 4, 128).transpose(2, 1, 0).reshape(128, 124))
    put("bdw", _colmajor(inp["b_conv_dw"][0], 4))
    put("lng", _colmajor(inp["conv_ln_g"][0], 4))
    put("lnb", _colmajor(inp["conv_ln_b"][0], 4))
    put("memxg", _colmajor(inp["mem_norm_x_g"][0], 8))
    put("memmg", _colmajor(inp["mem_norm_m_g"][0], 8))
    put("mqg", _colmajor(inp["mem_q_norm_g"][0], 2))
    put("mkg", _colmajor(inp["mem_k_norm_g"][0], 2))
    put("ffng", _colmajor(inp["ffn_norm_g"][0], 8))
    wf = np.asarray(inp["w_ffn_dw"][0], np.float32)
    put("wffn", wf.reshape(3, 44, 128).transpose(2, 1, 0).reshape(128, 132))
    put("bffn", _colmajor(inp["b_ffn_dw"][0], 44))
    c[:, C["eps"]] = EPS
    c[:, C["flag"]] = 1.0 if half == 1 else 0.0
    if half == 0:
        c[:, C["kbias"]:C["kbias"] + 32] = -30000.0
    put("gql", np.tile(np.asarray(inp["q_lat_norm_g"][0], np.float32)[None, :], (128, 1)))
    put("gkvl", np.tile(np.asarray(inp["kv_lat_norm_g"][0], np.float32)[None, :], (128, 1)))
    put("gqn", np.tile(np.asarray(inp["q_norm_g"][0], np.float32)[None, :], (128, 1)))
    put("gkn", np.tile(np.asarray(inp["k_norm_g"][0], np.float32)[None, :], (128, 1)))
    invf = (np.float32(10000.0) ** (-np.arange(0, 32, 2, dtype=np.float32) / np.float32(32))).astype(np.float32)
    put("invf", np.tile(invf[None, :], (128, 1)))
    put("ident", np.eye(128, dtype=np.float32))
    return c


_PSUM_PREFIX = ("pA", "pG", "pL", "pT", "pQ", "pKV", "pY", "pM", "pS", "pO", "pB", "pz", "pu", "pd", "pss")


class Buf:
    __slots__ = ("name", "w", "r", "psum")

    def __init__(self, name=""):
        self.name = name
        self.w = None
        self.r = []
        self.psum = name.startswith(_PSUM_PREFIX)


class Sched:
    NDMA = 24

    def __init__(self, nc):
        self.nc = nc
        self.ops = []
        self.eng = {"pe": nc.tensor, "act": nc.scalar, "dve": nc.vector, "pool": nc.gpsimd, "sp": nc.sync}

    def init_sems(self, es):
        self.sem = {e: es.enter_context(self.nc.semaphore("s_" + e)) for e in self.eng}
        self.dsem = [es.enter_context(self.nc.semaphore("d%d" % i)) for i in range(self.NDMA)]

    def op(self, eng, fn, reads=(), writes=()):
        self.ops.append(("c", eng, fn, tuple(reads), tuple(writes)))

    def dma(self, q, out, in_, reads=(), writes=()):
        eng = self.eng[q]
        self.ops.append(("d", q, (lambda: eng.dma_start(out=out, in_=in_)), tuple(reads), tuple(writes)))

    def barrier(self):
        self.ops.append(("b",))

    def finish(self):
        ops = self.ops
        n = len(ops)
        known = {e: {} for e in self.eng}
        vc = [None] * n
        waits = [None] * n
        signaled = [False] * n
        dma_use = [0] * self.NDMA
        dma_last = [None] * self.NDMA
        dma_ev = {}
        ndma = 0
        pend = {e: set() for e in self.eng}
        last_c = {}
        for i, o in enumerate(ops):
            if o[0] == "b":
                allp = set(last_c.values()) | set(x for x in dma_last if x is not None)
                for e in self.eng:
                    pend[e] |= allp
                waits[i] = {}
                vc[i] = {}
                continue
            kind, e = o[0], o[1]
            reads, writes = o[3], o[4]
            deps = set(pend[e])
            pend[e] = set()
            if kind == "c":
                last_c[e] = i
            for b in reads:
                if b.w is not None:
                    deps.add(b.w)
                if b.psum:
                    deps.update(x for x in b.r if ops[x][1] != e)
            for b in writes:
                if b.w is not None:
                    deps.add(b.w)
                deps.update(b.r)
            if kind == "d":
                s = ndma % self.NDMA
                ndma += 1
                if dma_last[s] is not None:
                    deps.add(dma_last[s])
                dma_use[s] += 1
                dma_ev[i] = (s, dma_use[s])
                dma_last[s] = i
            kn = known[e]
            w = {}
            for d in sorted(deps, reverse=True):
                od = ops[d]
                if od[0] == "c":
                    src = od[1]
                    if src == "pe" and e == "pe" and kind == "c":
                        continue
                    val = d
                else:
                    src = ("dma", dma_ev[d][0])
                    val = dma_ev[d][1]
                if kn.get(src, -1) >= val:
                    continue
                w[src] = max(w.get(src, -1), val)
                kn[src] = val
                for k2, v2 in vc[d].items():
                    if kn.get(k2, -1) < v2:
                        kn[k2] = v2
                if od[0] == "c":
                    signaled[d] = True
            waits[i] = w
            c = dict(kn)
            if kind == "c":
                c[e] = max(c.get(e, -1), i)
            else:
                c[("dma", dma_ev[i][0])] = dma_ev[i][1]
            vc[i] = c
            for b in reads:
                b.r.append(i)
            for b in writes:
                b.w = i
                b.r = []
        cnt = {e: 0 for e in self.eng}
        sigval = {}
        for i, o in enumerate(ops):
            if o[0] == "c" and signaled[i]:
                cnt[o[1]] += 1
                sigval[i] = cnt[o[1]]
        nw = 0
        self.trace = {e: [] for e in self.eng}
        for i, o in enumerate(ops):
            if o[0] == "b":
                continue
            e = o[1]
            self.trace[e].append((i, [((src, 16 * val) if isinstance(src, tuple) else (src, sigval[val])) for src, val in waits[i].items()],
                                  (("c", e, 1) if (o[0] == "c" and signaled[i]) else (("d", dma_ev[i][0], 16) if o[0] == "d" else None))))
            eng = self.eng[e]
            for src, val in waits[i].items():
                nw += 1
                if isinstance(src, tuple):
                    eng.wait_ge(self.dsem[src[1]], 16 * val)
                else:
                    eng.wait_ge(self.sem[src], sigval[val])
            ins = o[2]()
            if o[0] == "c":
                if signaled[i]:
                    ins.then_inc(self.sem[e], 1)
            else:
                ins.then_inc(self.dsem[dma_ev[i][0]], 16)
        sp = self.eng["sp"]
        for s in range(self.NDMA):
            if dma_use[s] > 0:
                sp.wait_ge(self.dsem[s], 16 * dma_use[s])
        self.stats = dict(nops=n, nwaits=nw, nsig=dict(cnt))


def build_program(dbg=False, stop_after=99):
    nc = bass.Bass("TRN2", target_bir_lowering=False)
    S = Sched(nc)
    E = S.eng

    def din(name, shape, dt=F32):
        return nc.dram_tensor(name, list(shape), dt, kind="ExternalInput").ap()

    def dscr(name, shape, dt):
        return nc.dram_tensor(name, list(shape), dt, kind=("ExternalOutput" if dbg else "Internal")).ap()

    xT = din("xT", [D, SEQ])
    pos = din("pos", [128, 64], I32)
    cst = din("cst", [128, NCST])
    memT = din("memT", [D, 256])
    w_in = din("w_in", [D, 1440])
    w_uq = din("w_uq", [256, 768])
    w_ukv = din("w_ukv", [128, 1024])
    w_out = din("w_out", [D, D])
    w_mq = din("w_mem_q", [D, D])
    w_mkv = din("w_mem_kv", [D, 2 * D])
    w_mo = din("w_mem_o", [D, D])
    w_up = din("w_up", [D, 2 * DFF])
    w_dn = din("w_down", [DFF, D])
    yT = nc.dram_tensor("yT", [D, NTOK], F32, kind="ExternalOutput").ap()
    QT_scr = dscr("QT_scr", [8, 96, NQ], BF16)
    KT_scr = dscr("KT_scr", [8, 96, SEQ], BF16)
    V_scr = dscr("V_scr", [8, 128, 64 * 128], BF16)
    x2_scr = dscr("x2_scr", [D, NQ], F32)
    dbg_out = {}
    if dbg:
        dbg_out["uT"] = nc.dram_tensor("dbg_uT", [128, 4, NQ + 30], BF16, kind="ExternalOutput").ap()
        dbg_out["ufT"] = nc.dram_tensor("dbg_ufT", [128, 4, NQ], BF16, kind="ExternalOutput").ap()
        dbg_out["attnT"] = nc.dram_tensor("dbg_attnT", [128, 4, NQ], BF16, kind="ExternalOutput").ap()

    xT_v = xT.rearrange("(kc p) t -> p kc t", p=128)
    x2_v = x2_scr.rearrange("(kc p) t -> p kc t", p=128)
    yT_v = yT.rearrange("(kc p) t -> p kc t", p=128)

    with contextlib.ExitStack() as es0:
        S.init_sems(es0)

        uid = [0]

        def mk(es):
            def sb(name, shape, dt):
                uid[0] += 1
                return es.enter_context(nc.sbuf_tensor("%s_%d" % (name, uid[0]), list(shape), dt))

            def ps(name, shape, dt=F32):
                uid[0] += 1
                return es.enter_context(nc.psum_tensor("%s_%d" % (name, uid[0]), list(shape), dt))
            return sb, ps

        sb0, _ = mk(es0)

        def ACT(out, in_, func, r, w, **kw):
            S.op("act", lambda: nc.scalar.activation(out, in_, func, **kw), r, w)

        def TT(eng, out, a, b, op, r, w):
            S.op(eng, lambda: E[eng].tensor_tensor(out, a, b, op), r, w)

        def TS(eng, out, a, s1, s2, op0, op1, r, w):
            if op1 is None:
                S.op(eng, lambda: E[eng].tensor_scalar(out, a, s1, None, op0), r, w)
            else:
                S.op(eng, lambda: E[eng].tensor_scalar(out, a, s1, s2, op0, op1), r, w)

        def STT(out, a, s, b, op0, op1, r, w):
            S.op("dve", lambda: nc.vector.scalar_tensor_tensor(out, a, s, b, op0, op1), r, w)

        def CP(eng, out, in_, r, w):
            if eng == "act":
                S.op("act", lambda: nc.scalar.copy(out, in_), r, w)
            else:
                S.op(eng, lambda: E[eng].tensor_copy(out, in_), r, w)

        def MM(out, lhsT, rhs, start, stop, r, w):
            S.op("pe", lambda: nc.tensor.matmul(out, lhsT, rhs, start=start, stop=stop), r, w)

        def TR(out, in_, ident, r, w):
            S.op("pe", lambda: nc.tensor.transpose(out, in_, ident), r, w)

        def RED(out, in_, r, w):
            S.op("dve", lambda: nc.vector.tensor_reduce(out, in_, AX.X, ALU.add), r, w)

        def RCP(out, in_, r, w):
            S.op("dve", lambda: nc.vector.reciprocal(out, in_), r, w)

        def MSET(eng, out, val, w):
            S.op(eng, lambda: E[eng].memset(out, val), (), w)

        cst_t = sb0("cst_t", [128, NCST], F32)
        Bc = Buf("cst")
        S.dma("sp", cst_t[:], cst, writes=[Bc])

        def cc(name, j=0, w=1):
            return cst_t[:, C[name] + j:C[name] + j + w]

        ident = sb0("ident", [128, 128], BF16)
        ones_bf = sb0("ones_bf", [128, 128], BF16)
        ones_f = sb0("ones_f", [128, 128], F32)
        Bid, Bob, Bof = Buf("ident"), Buf("ones_bf"), Buf("ones_f")
        CP("dve", ident[:], cc("ident", 0, 128), [Bc], [Bid])
        MSET("pool", ones_bf[:], 1.0, [Bob])
        MSET("pool", ones_f[:], 1.0, [Bof])

        esX = es0.enter_context(contextlib.ExitStack())
        uT = esX.enter_context(nc.sbuf_tensor("uT", [128, 4, NQ + 30], BF16))
        Buf_u = [Buf("u%d" % i) for i in range(9)]
        Bupad = Buf("upad")
        MSET("pool", uT[:, :, 0:30], 0.0, [Bupad])
        Buf_at = [Buf("at%d" % i) for i in range(9)]
        B_KTs = [Buf("KTs%d" % i) for i in range(16)]
        B_Vs = [Buf("Vs%d" % i) for i in range(16)]
        B_QTs = [Buf("QTs%d" % i) for i in range(9)]
        B_x2 = [Buf("x2s%d" % i) for i in range(9)]
        stgs = []

        @contextlib.contextmanager
        def staging():
            with contextlib.ExitStack() as esS:
                sbS, _ = mk(esS)
                stgs[:] = [(sbS("stg%d" % i, [128, 1024], F32), Buf("stg%d" % i)) for i in range(3)]
                yield
                S.barrier()
            stgs[:] = []

        stg_n = [0]

        def load_w(dst, name, src, K, N, gain=None):
            bufs = []
            for kc in range(K // 128):
                for c0 in range(0, N, 1024):
                    c1 = min(N, c0 + 1024)
                    wd = c1 - c0
                    i = stg_n[0]
                    stg_n[0] += 1
                    st, Bst = stgs[i % 3]
                    S.dma("sp", st[:, 0:wd], src[kc * 128:(kc + 1) * 128, c0:c1], writes=[Bst])
                    b = Buf(name)
                    bufs.append(b)
                    eng = ("act", "dve", "pool")[i % 3]
                    o = dst[:, kc, c0:c1]
                    if gain is not None:
                        g = cc(gain, kc)
                        if eng == "act":
                            ACT(o, st[:, 0:wd], AF.Copy, [Bst, Bc], [b], scale=g)
                        else:
                            TS(eng, o, st[:, 0:wd], g, None, ALU.mult, None, [Bst, Bc], [b])
                    else:
                        CP(eng, o, st[:, 0:wd], [Bst], [b])
            return bufs

        def rms_bcast(xin, Bx, n, sq, Bsq, pbank, Bp, rs, Brs, rstd, Brstd, nfeat=1024):
            kcs = nfeat // 128
            ACT(sq[:, 0:kcs, 0:n], xin, AF.Square, Bx, [Bsq])
            for kc in range(kcs):
                MM(pbank[:, 0:n], ones_bf[:], sq[:, kc, 0:n], kc == 0, kc == kcs - 1, [Bob, Bsq], [Bp])
            ACT(rs[:, 0:n], pbank[:, 0:n], AF.Sqrt, [Bp, Bc], [Brs], scale=1.0 / nfeat, bias=cc("eps"))
            RCP(rstd[:, 0:n], rs[:, 0:n], [Brs], [Brstd])

        with contextlib.ExitStack() as es1:
            if True:
                esA = es1
                sb, ps = mk(esA)
                cos2 = sb("cos2", [128, 64, 32], F32)
                sin2 = sb("sin2", [128, 64, 32], F32)
                Btab = Buf("tab")
                with contextlib.ExitStack() as esT:
                    sbt, _ = mk(esT)
                    pos_i = sbt("pos_i", [128, 64], I32)
                    posf = sbt("posf", [128, 64], F32)
                    ang = sbt("ang", [128, 64, 2, 16], F32)
                    tf = sbt("tf", [128, 2048], F32)
                    ti_ = sbt("ti_", [128, 2048], I32)
                    sc = sbt("sc", [128, 64, 2, 16], F32)
                    Bt = Buf("t")
                    angf = ang[:].rearrange("p a b c -> p (a b c)")
                    S.dma("sp", pos_i[:], pos, writes=[Bt])
                    CP("dve", posf[:], pos_i[:], [Bt], [Bt])
                    TT("dve", ang[:, :, 0, :], posf[:].unsqueeze(2).to_broadcast([128, 64, 16]),
                       cc("invf", 0, 16).unsqueeze(1).to_broadcast([128, 64, 16]), ALU.mult, [Bt, Bc], [Bt])
                    TS("dve", ang[:, :, 1, :], ang[:, :, 0, :], PI / 2, None, ALU.add, None, [Bt], [Bt])
                    C1 = 6.28125
                    C2 = 2 * np.pi - 6.28125
                    TS("dve", tf[:], angf, 1.0 / (2 * np.pi), None, ALU.mult, None, [Bt], [Bt])
                    CP("dve", ti_[:], tf[:], [Bt], [Bt])
                    CP("dve", tf[:], ti_[:], [Bt], [Bt])
                    STT(angf, tf[:], -C1, angf, ALU.mult, ALU.add, [Bt], [Bt])
                    STT(angf, tf[:], -C2, angf, ALU.mult, ALU.add, [Bt], [Bt])
                    TS("dve", tf[:], angf, PI, -2 * PI, ALU.is_gt, ALU.mult, [Bt], [Bt])
                    TT("dve", angf, angf, tf[:], ALU.add, [Bt], [Bt])
                    TS("dve", tf[:], angf, -PI, 2 * PI, ALU.is_lt, ALU.mult, [Bt], [Bt])
                    TT("dve", angf, angf, tf[:], ALU.add, [Bt], [Bt])
                    ACT(sc[:].rearrange("p a b c -> p (a b c)"), angf, AF.Sin, [Bt], [Bt])
                    CP("dve", cos2[:, :, 0:16], sc[:, :, 1, :], [Bt], [Btab])
                    CP("dve", cos2[:, :, 16:32], sc[:, :, 1, :], [Bt, Btab], [Btab])
                    CP("dve", sin2[:, :, 16:32], sc[:, :, 0, :], [Bt, Btab], [Btab])
                    TS("dve", sin2[:, :, 0:16], sc[:, :, 0, :], -1.0, None, ALU.mult, None, [Bt, Btab], [Btab])
                    S.barrier()

                w_in_bf = sb("w_in_bf", [128, 8, 1440], BF16)
                w_uq_bf = sb("w_uq_bf", [128, 2, 768], BF16)
                w_ukv_bf = sb("w_ukv_bf", [128, 1, 1024], BF16)
                with staging():
                    Bwin = load_w(w_in_bf, "w_in_bf", w_in, D, 1440, gain="mixg")
                    Bwuq = load_w(w_uq_bf, "w_uq_bf", w_uq, 256, 768)
                    Bwukv = load_w(w_ukv_bf, "w_ukv_bf", w_ukv, 128, 1024)

                xbuf = [(sb("xg%d" % i, [128, 8, 512], F32), Buf("xg%d" % i)) for i in range(2)]
                sq = sb("sq", [128, 8, 512], BF16)
                Bsq = Buf("sq")
                hT = sb("hT", [128, 8, 512], BF16)
                BhT = Buf("hT")
                rs = sb("rs", [128, 512], F32)
                Brs = Buf("rs")
                rstd = sb("rstd", [128, 512], F32)
                Brstd = Buf("rstd")
                sig = sb("sig", [128, 512], F32)
                Bsig = Buf("sig")
                sqLs = [(sb("sqL%d" % i, [128, 384], F32), Buf("sqL%d" % i)) for i in range(2)]
                st2s = [(sb("st2%d" % i, [128, 8], F32), Buf("st2%d" % i)) for i in range(2)]
                cns = [(sb("cn%d" % i, [128, 384], BF16), Buf("cn%d" % i)) for i in range(2)]
                cTs = [(sb("cT%d" % i, [128, 384], BF16), Buf("cT%d" % i)) for i in range(2)]
                sqq = sb("sqq", [128, 768], F32)
                Bsqq = Buf("sqq")
                stq = sb("stq", [128, 24], F32)
                Bstq = Buf("stq")
                qn = sb("qn", [128, 8, 96], F32)
                Bqn = Buf("qn")
                rt1 = sb("rt1", [128, 8, 32], F32)
                rt2 = sb("rt2", [128, 8, 32], F32)
                Brt = Buf("rt")
                qbf = sb("qbf", [128, 8, 96], BF16)
                Bqbf = Buf("qbf")
                sqk = sb("sqk", [128, 8, 64], F32)
                Bsqk = Buf("sqk")
                sqr = sb("sqr", [128, 32], F32)
                Bsqr = Buf("sqr")
                stk = sb("stk", [128, 32], F32)
                Bstk = Buf("stk")
                kn = sb("kn", [128, 8, 64], F32)
                Bkn = Buf("kn")
                kr = sb("kr", [128, 4, 32], F32)
                Bkr = Buf("kr")
                kbf = sb("kbf", [128, 8, 96], BF16)
                Bkbf = Buf("kbf")
                KTst = [(sb("KTst%d" % i, [96, 8, 512], BF16), Buf("KTst%d" % i)) for i in range(1)]
                QTst = [(sb("QTst%d" % i, [96, 8, 512], BF16), Buf("QTst%d" % i)) for i in range(1)]
                Vst = [(sb("Vst%d" % i, [128, 8, 4, 128], BF16), Buf("Vst%d" % i)) for i in range(1)]
                for (v, bv) in Vst:
                    MSET("pool", v[:, 0:8:2, :, 64:128], 1.0, [bv])
                    MSET("pool", v[:, 1:8:2, :, 0:64], 1.0, [bv])
                pA = ps("pA", [128, 512])
                pG = ps("pG", [128, 512])
                pTk = ps("pTk", [128, 1024], BF16)
                pT = ps("pT", [128, 1024], BF16)
                pQ = ps("pQ", [128, 1024])
                pKV = ps("pKV", [128, 1024])
                BpA, BpG, BpTk, BpT, BpQ, BpKV = [Buf(n) for n in "pA pG pTk pT pQ pKV".split()]
                pLs = [(pA, BpA), (pG, BpG)]
                KT_v = KT_scr.rearrange("h d t -> d h t")
                QT_v = QT_scr.rearrange("h d t -> d h t")
                V_v = V_scr.rearrange("h p (t e) -> p h t e", e=128)
                R96 = 96 ** -0.5

                S.dma("sp", xbuf[0][0][:], xT_v[:, :, 0:512], writes=[xbuf[0][1]])
                for G in range(16):
                    own = G >= 8
                    xg, Bxg = xbuf[G % 2]
                    if G + 1 < 16:
                        S.dma("sp", xbuf[(G + 1) % 2][0][:], xT_v[:, :, (G + 1) * 512:(G + 2) * 512],
                              writes=[xbuf[(G + 1) % 2][1]])
                    rms_bcast(xg[:], [Bxg], 512, sq, Bsq, pA, BpA, rs, Brs, rstd, Brstd)
                    TT("dve", hT[:], xg[:], rstd[:].unsqueeze(1).to_broadcast([128, 8, 512]), ALU.mult,
                       [Bxg, Brstd], [BhT])
                    if own or G == 7:
                        c0, n = (0, 512) if own else (384, 128)
                        qg = (G - 7) if own else 0
                        ucol = 30 + (128 + (G - 8) * 512 if own else 0)
                        for c in range(4):
                            for kc in range(8):
                                MM(pA[:, 0:n], w_in_bf[:, kc, c * 128:(c + 1) * 128], hT[:, kc, c0:c0 + n],
                                   kc == 0, kc == 7, Bwin + [BhT], [BpA])
                            for kc in range(8):
                                MM(pG[:, 0:n], w_in_bf[:, kc, 512 + c * 128:512 + (c + 1) * 128], hT[:, kc, c0:c0 + n],
                                   kc == 0, kc == 7, Bwin + [BhT], [BpG])
                            ACT(sig[:, 0:n], pG[:, 0:n], AF.Sigmoid, [BpG, Bc], [Bsig], bias=cc("bcin", 4 + c))
                            STT(uT[:, c, ucol:ucol + n], pA[:, 0:n], cc("bcin", c), sig[:, 0:n], ALU.add, ALU.mult,
                                [BpA, Bsig, Bc], [Buf_u[qg]])
                            if not own:
                                TS("dve", uT[:, c, ucol:ucol + n], uT[:, c, ucol:ucol + n], cc("flag"), None,
                                   ALU.mult, None, [Buf_u[qg], Bc], [Buf_u[qg]])
                    KTs, BKTs = KTst[0]
                    QTs, BQTs = QTst[0]
                    Vs, BVs = Vst[0]
                    def lat_chain(ti):
                        t = 4 * G + ti
                        hasq = own or t == 31
                        tsl = slice(ti * 128, (ti + 1) * 128)
                        l0 = 0 if hasq else 256
                        pLx, BpLx = pLs[ti % 2]
                        sqLx, BsqLx = sqLs[ti % 2]
                        st2x, Bst2x = st2s[ti % 2]
                        cnx, Bcnx = cns[ti % 2]
                        cTx, BcTx = cTs[ti % 2]
                        for kc in range(8):
                            MM(pLx[:, l0:416], hT[:, kc, tsl], w_in_bf[:, kc, 1024 + l0:1440], kc == 0, kc == 7,
                               Bwin + [BhT], [BpLx])
                        yield
                        if hasq:
                            ACT(sqLx[:, 0:256], pLx[:, 0:256], AF.Square, [BpLx], [BsqLx], scale=1.0 / 16)
                            yield
                        ACT(sqLx[:, 256:384], pLx[:, 256:384], AF.Square, [BpLx], [BsqLx], scale=128 ** -0.5)
                        yield
                        if hasq:
                            RED(st2x[:, 0:1], sqLx[:, 0:256], [BsqLx], [Bst2x])
                            yield
                        RED(st2x[:, 1:2], sqLx[:, 256:384], [BsqLx], [Bst2x])
                        yield
                        ACT(st2x[:, 2 + l0 // 256:4], st2x[:, l0 // 256:2], AF.Sqrt, [Bst2x, Bc], [Bst2x], bias=cc("eps"))
                        yield
                        RCP(st2x[:, 4 + l0 // 256:6], st2x[:, 2 + l0 // 256:4], [Bst2x], [Bst2x])
                        yield
                        if hasq:
                            STT(cnx[:, 0:256], pLx[:, 0:256], st2x[:, 4:5], cc("gql", 0, 256), ALU.mult, ALU.mult,
                                [BpLx, Bst2x, Bc], [Bcnx])
                            yield
                        STT(cnx[:, 256:384], pLx[:, 256:384], st2x[:, 5:6], cc("gkvl", 0, 128), ALU.mult, ALU.mult,
                            [BpLx, Bst2x, Bc], [Bcnx])
                        yield
                        for j in ((0, 1, 2) if hasq else (2,)):
                            TR(pT[:, j * 128:(j + 1) * 128], cnx[:, j * 128:(j + 1) * 128], ident[:], [Bcnx, Bid], [BpT])
                        yield
                        CP("act", cTx[:, l0:384], pT[:, l0:384], [BpT], [BcTx])
                        yield

                    def k_chain(ti):
                        t = 4 * G + ti
                        tsl = slice(ti * 128, (ti + 1) * 128)
                        pLx, BpLx = pLs[ti % 2]
                        cTx, BcTx = cTs[ti % 2]
                        for (a, b) in ((0, 512), (512, 1024)):
                            MM(pKV[:, a:b], cTx[:, 256:384], w_ukv_bf[:, 0, a:b], True, True, Bwukv + [BcTx], [BpKV])
                        yield
                        kvv = pKV[:].rearrange("p (h e) -> p h e", h=8)
                        TT("dve", kr[:, 0, :], pLx[:, 384:416], cc("gkn", 64, 32), ALU.mult, [BpLx, Bc], [Bkr])
                        yield
                        ACT(sqk[:], kvv[:, :, 0:64], AF.Square, [BpKV], [Bsqk], scale=R96)
                        yield
                        ACT(sqr[:], pLx[:, 384:416], AF.Square, [BpLx], [Bsqr], scale=R96)
                        yield
                        TT("pool", kr[:, 1, :], kr[:, 0, :], cos2[:, t, :], ALU.mult, [Bkr, Btab], [Bkr])
                        yield
                        RED(stk[:, 0:8], sqk[:], [Bsqk], [Bstk])
                        yield
                        RED(stk[:, 8:9], sqr[:], [Bsqr], [Bstk])
                        yield
                        TT("pool", kr[:, 2, 0:16], kr[:, 0, 16:32], sin2[:, t, 0:16], ALU.mult, [Bkr, Btab], [Bkr])
                        yield
                        TS("dve", stk[:, 9:10], stk[:, 8:9], EPS, None, ALU.add, None, [Bstk], [Bstk])
                        yield
                        CP("act", Vs[:, 0:8:2, ti, 0:64], kvv[:, 0:8:2, 64:128], [BpKV], [BVs])
                        yield
                        TT("pool", kr[:, 2, 16:32], kr[:, 0, 0:16], sin2[:, t, 16:32], ALU.mult, [Bkr, Btab], [Bkr])
                        yield
                        ACT(stk[:, 16:24], stk[:, 0:8], AF.Sqrt, [Bstk], [Bstk], bias=stk[:, 9:10])
                        yield
                        TT("pool", kr[:, 3, :], kr[:, 1, :], kr[:, 2, :], ALU.add, [Bkr], [Bkr])
                        yield
                        RCP(stk[:, 24:32], stk[:, 16:24], [Bstk], [Bstk])
                        yield
                        CP("act", Vs[:, 1:8:2, ti, 64:128], kvv[:, 1:8:2, 64:128], [BpKV], [BVs])
                        yield
                        rk = stk[:, 24:32]
                        TT("dve", kn[:], kvv[:, :, 0:64], rk.unsqueeze(2).to_broadcast([128, 8, 64]), ALU.mult,
                           [BpKV, Bstk], [Bkn])
                        yield
                        TT("dve", kbf[:, :, 64:96], kr[:, 3:4, :].to_broadcast([128, 8, 32]),
                           rk.unsqueeze(2).to_broadcast([128, 8, 32]), ALU.mult, [Bkr, Bstk], [Bkbf])
                        yield
                        TT("pool", kbf[:, :, 0:64], kn[:], cc("gkn", 0, 64).unsqueeze(1).to_broadcast([128, 8, 64]),
                           ALU.mult, [Bkn, Bc], [Bkbf])
                        yield
                        for h in range(8):
                            TR(pTk[0:96, h * 128:(h + 1) * 128], kbf[:, h, :], ident[:], [Bkbf, Bid], [BpTk])
                        yield
                        CP("act", KTs[:, :, tsl], pTk[0:96, :].rearrange("p (h t) -> p h t", h=8), [BpTk], [BKTs])
                        yield

                    def q_chain(ti):
                        t = 4 * G + ti
                        tsl = slice(ti * 128, (ti + 1) * 128)
                        cTx, BcTx = cTs[ti % 2]
                        for (a, b) in ((0, 512), (512, 768)):
                            for kc in range(2):
                                MM(pQ[:, a:b], cTx[:, kc * 128:(kc + 1) * 128], w_uq_bf[:, kc, a:b], kc == 0, kc == 1,
                                   Bwuq + [BcTx], [BpQ])
                        yield
                        ACT(sqq[:], pQ[:, 0:768], AF.Square, [BpQ], [Bsqq], scale=R96)
                        yield
                        RED(stq[:, 0:8], sqq[:].rearrange("p (h e) -> p h e", h=8), [Bsqq], [Bstq])
                        yield
                        ACT(stq[:, 8:16], stq[:, 0:8], AF.Sqrt, [Bstq, Bc], [Bstq], bias=cc("eps"))
                        yield
                        RCP(stq[:, 16:24], stq[:, 8:16], [Bstq], [Bstq])
                        yield
                        qv = pQ[:, 0:768].rearrange("p (h e) -> p h e", h=8)
                        TT("dve", qn[:], qv, stq[:, 16:24].unsqueeze(2).to_broadcast([128, 8, 96]), ALU.mult,
                           [BpQ, Bstq], [Bqn])
                        yield
                        TT("pool", qn[:], qn[:], cc("gqn", 0, 96).unsqueeze(1).to_broadcast([128, 8, 96]), ALU.mult,
                           [Bqn, Bc], [Bqn])
                        yield
                        TT("pool", rt1[:], qn[:, :, 64:96], cos2[:, t:t + 1, :].to_broadcast([128, 8, 32]), ALU.mult,
                           [Bqn, Btab], [Brt])
                        yield
                        TT("pool", rt2[:, :, 0:16], qn[:, :, 80:96], sin2[:, t:t + 1, 0:16].to_broadcast([128, 8, 16]),
                           ALU.mult, [Bqn, Btab, Brt], [Brt])
                        yield
                        TT("pool", rt2[:, :, 16:32], qn[:, :, 64:80], sin2[:, t:t + 1, 16:32].to_broadcast([128, 8, 16]),
                           ALU.mult, [Bqn, Btab, Brt], [Brt])
                        yield
                        TT("dve", qbf[:, :, 64:96], rt1[:], rt2[:], ALU.add, [Brt], [Bqbf])
                        yield
                        CP("act", qbf[:, :, 0:64], qn[:, :, 0:64], [Bqn], [Bqbf])
                        yield
                        for h in range(8):
                            TR(pT[0:96, h * 128:(h + 1) * 128], qbf[:, h, :], ident[:], [Bqbf, Bid], [BpT])
                        yield
                        qsl = tsl if own else slice(0, 128)
                        CP("dve", QTs[:, :, qsl], pT[0:96, :].rearrange("p (h t) -> p h t", h=8), [BpT], [BQTs])
                        yield
                        if not own:
                            S.dma("sp", QT_v[:, :, 0:128], QTs[:, :, 0:128], reads=[BQTs], writes=[B_QTs[0]])

                    def rr(*gens):
                        gens = [g_ for g_ in gens if g_ is not None]
                        while gens:
                            for g_ in list(gens):
                                try:
                                    next(g_)
                                except StopIteration:
                                    gens.remove(g_)

                    rr(lat_chain(0))
                    for ti in range(4):
                        hasq_ = own or (4 * G + ti) == 31
                        rr(k_chain(ti), q_chain(ti) if hasq_ else None, lat_chain(ti + 1) if ti < 3 else None)
                    S.dma("sp", KT_v[:, :, G * 512:(G + 1) * 512], KTs[:], reads=[BKTs], writes=[B_KTs[G]])
                    S.dma("sp", V_v[:, :, 4 * G:4 * G + 4, :], Vs[:], reads=[BVs], writes=[B_Vs[G]])
                    if own:
                        q0 = 128 + (G - 8) * 512
                        S.dma("sp", QT_v[:, :, q0:q0 + 512], QTs[:], reads=[BQTs], writes=[B_QTs[G - 7]])
                S.barrier()
        if dbg:
            S.dma("sp", dbg_out["uT"], uT[:], reads=Buf_u + [Bupad])
        attnT = esX.enter_context(nc.sbuf_tensor("attnT", [128, 4, NQ], BF16))
        memK = esX.enter_context(nc.sbuf_tensor("memK", [128, 8, 256], BF16))
        memV = esX.enter_context(nc.sbuf_tensor("memV", [128, 2, 1024], BF16))
        BmK, BmV = Buf("memK"), Buf("memV")
        if True:

            if stop_after >= 2:
                with contextlib.ExitStack() as esB:
                    sb, ps = mk(esB)
                    Dg = sb("Dg", [128, 4, 31, 128], BF16)
                    BDg = [Buf("Dg%d" % c) for c in range(4)]
                    for c in range(4):
                        for k in range(31):
                            TS(("dve", "pool")[k % 2], Dg[:, c, k, :], ident[:], cc("wdw", c * 31 + k), None, ALU.mult, None,
                               [Bid, Bc], [BDg[c]])
                    pY = [(ps("pY%d" % i, [128, 512]), Buf("pY%d" % i)) for i in range(2)]
                    pM = ps("pM", [128, 512])
                    pM2 = ps("pM2", [128, 512])
                    BpM, BpM2 = Buf("pM"), Buf("pM2")
                    ybf = sb("ybf", [128, 4, 512], BF16)
                    ysq = sb("ysq", [128, 4, 512], BF16)
                    Bybf = [Buf("ybf%d" % c) for c in range(4)]
                    Bysq = [Buf("ysq%d" % c) for c in range(4)]
                    mean = sb("mean", [128, 512], F32)
                    m2 = sb("m2", [128, 512], F32)
                    var = sb("var", [128, 512], F32)
                    rsd = sb("rsd", [128, 512], F32)
                    Bmean, Bm2, Bvar, Brsd = Buf("mean"), Buf("m2"), Buf("var"), Buf("rsd")
                    tmp = [(sb("lt%d" % i, [128, 512], F32), Buf("lt%d" % i)) for i in range(2)]
                    w_mkv_bf = sb("w_mkv_bf", [128, 8, 2 * D], BF16)
                    stgM = [(sb("stgM%d" % i, [128, 1024], F32), Buf("stgM%d" % i)) for i in range(3)]
                    pzm = [(ps("pzm%d" % i, [128, 512]), Buf("pz_m%d" % i)) for i in range(4)]
                    msq = sb("sqm", [128, 8, 256], BF16)
                    Bmsq = Buf("sqm")
                    mrs = sb("rsm", [128, 512], F32)
                    Bmrs = Buf("rsm")
                    mrstd = sb("rstdm", [128, 512], F32)
                    Bmrstd = Buf("rstdm")
                    mx = sb("mx", [128, 8, 256], F32)
                    Bmx = Buf("mx")
                    hm = sb("hm", [128, 8, 256], BF16)
                    Bhm = Buf("hm")
                    ksq = sb("ksq", [128, 2, 256], BF16)
                    Bksq = Buf("ksq")
                    Bwmkv = []
                    mkv_pieces = [(kc, c0) for kc in range(8) for c0 in (0, 1024)]

                    def load_mkv_piece():
                        if not mkv_pieces:
                            return
                        kc, c0 = mkv_pieces.pop(0)
                        i = len(mkv_pieces)
                        st, Bst = stgM[i % 3]
                        S.dma("sp", st[:], w_mkv[kc * 128:(kc + 1) * 128, c0:c0 + 1024], writes=[Bst])
                        b_ = Buf("wmkv")
                        Bwmkv.append(b_)
                        TS(("dve", "pool")[i % 2], w_mkv_bf[:, kc, c0:c0 + 1024], st[:], cc("memmg", kc), None, ALU.mult, None,
                           [Bst, Bc], [b_])

                    S.dma("sp", mx[:], memT.rearrange("(kc p) t -> p kc t", p=128), writes=[Bmx])
                    yi = 0
                    for g in reversed(range(9)):
                        n = 128 if g == 0 else 512
                        oc = 0 if g == 0 else 128 + (g - 1) * 512
                        load_mkv_piece()
                        load_mkv_piece()
                        rd = [Bupad] + Buf_u[max(0, g - 1):g + 1]
                        for c in range(4):
                            py, Bpy = pY[yi % 2]
                            yi += 1
                            for k in range(31):
                                MM(py[:, 0:n], Dg[:, c, k, :], uT[:, c, oc + k:oc + k + n], k == 0, k == 30, rd + [BDg[c]], [Bpy])
                            ACT(ybf[:, c, 0:n], py[:, 0:n], AF.Identity, [Bpy, Bc], [Bybf[c]], bias=cc("bdw", c))
                            ACT(ysq[:, c, 0:n], py[:, 0:n], AF.Square, [Bpy, Bc], [Bysq[c]], bias=cc("bdw", c))
                        for c in range(4):
                            MM(pM[:, 0:n], ones_bf[:], ybf[:, c, 0:n], c == 0, c == 3, [Bob, Bybf[c]], [BpM])
                        for c in range(4):
                            MM(pM2[:, 0:n], ones_bf[:], ysq[:, c, 0:n], c == 0, c == 3, [Bob, Bysq[c]], [BpM2])
                        TS("dve", mean[:, 0:n], pM[:, 0:n], 1.0 / 512, None, ALU.mult, None, [BpM], [Bmean])
                        TT("dve", m2[:, 0:n], mean[:, 0:n], mean[:, 0:n], ALU.mult, [Bmean], [Bm2])
                        STT(var[:, 0:n], pM2[:, 0:n], 1.0 / 512, m2[:, 0:n], ALU.mult, ALU.subtract, [BpM2, Bm2], [Bvar])
                        ACT(var[:, 0:n], var[:, 0:n], AF.Sqrt, [Bvar, Bc], [Bvar], bias=cc("eps"))
                        RCP(rsd[:, 0:n], var[:, 0:n], [Bvar], [Brsd])
                        for c in range(4):
                            tm, Btm = tmp[c % 2]
                            TT("dve", tm[:, 0:n], ybf[:, c, 0:n], mean[:, 0:n], ALU.subtract, [Bybf[c], Bmean], [Btm])
                            TT("pool", tm[:, 0:n], tm[:, 0:n], rsd[:, 0:n], ALU.mult, [Btm, Brsd], [Btm])
                            ACT(uT[:, c, 30 + oc:30 + oc + n], tm[:, 0:n], AF.Silu, [Btm, Bc], [Buf_u[g]],
                                scale=cc("lng", c), bias=cc("lnb", c))
                    if stop_after >= 3:
                        while mkv_pieces:
                            load_mkv_piece()
                        rms_bcast(mx[:], [Bmx], 256, msq, Bmsq, pzm[0][0], pzm[0][1], mrs, Bmrs, mrstd, Bmrstd)
                        TT("dve", hm[:], mx[:], mrstd[:, 0:256].unsqueeze(1).to_broadcast([128, 8, 256]), ALU.mult,
                           [Bmx, Bmrstd], [Bhm])
                        for hd in range(4):
                            for j in range(2):
                                ch = hd * 2 + j
                                p_, Bp_ = pzm[1 + j]
                                for kc in range(8):
                                    MM(p_[:, 0:256], w_mkv_bf[:, kc, ch * 128:(ch + 1) * 128], hm[:, kc, :], kc == 0, kc == 7,
                                       Bwmkv + [Bhm], [Bp_])
                                ACT(ksq[:, j, :], p_[:, 0:256], AF.Square, [Bp_], [Bksq], scale=1.0 / 16)
                            for j in range(2):
                                MM(pzm[3][0][:, 0:256], ones_bf[:], ksq[:, j, :], j == 0, j == 1, [Bob, Bksq], [pzm[3][1]])
                            ACT(mrs[:, 0:256], pzm[3][0][:, 0:256], AF.Sqrt, [pzm[3][1], Bc], [Bmrs], bias=cc("eps"))
                            RCP(mrstd[:, 0:256], mrs[:, 0:256], [Bmrs], [Bmrstd])
                            for j in range(2):
                                STT(memK[:, hd * 2 + j, :], pzm[1 + j][0][:, 0:256], cc("mkg", j), mrstd[:, 0:256], ALU.mult, ALU.mult,
                                    [pzm[1 + j][1], Bmrstd, Bc], [BmK])
                        for kt in range(2):
                            for nb in range(2):
                                p_, Bp_ = pzm[(kt * 2 + nb) % 2 + 1]
                                for kc in range(8):
                                    MM(p_[:], hm[:, kc, kt * 128:(kt + 1) * 128], w_mkv_bf[:, kc, 1024 + nb * 512:1024 + (nb + 1) * 512],
                                       kc == 0, kc == 7, Bwmkv + [Bhm], [Bp_])
                                CP("act", memV[:, kt, nb * 512:(nb + 1) * 512], p_[:], [Bp_], [BmV])
                    S.barrier()
                if dbg:
                    S.dma("sp", dbg_out["ufT"], uT[:, :, 30:30 + NQ], reads=Buf_u)

        w_out_bf = esX.enter_context(nc.sbuf_tensor("w_out_bf", [128, 8, D], BF16))
        w_mq_bf = esX.enter_context(nc.sbuf_tensor("w_mq_bf", [128, 8, D], BF16))
        w_mo_bf = esX.enter_context(nc.sbuf_tensor("w_mo_bf", [128, 8, D], BF16))
        stgP = (esX.enter_context(nc.sbuf_tensor("stgP", [128, 512], F32)), Buf("stgP"))
        if stop_after >= 3:
            with contextlib.ExitStack() as es2:
                sb, ps = mk(es2)
                kT = [(sb("kT%d" % i, [96, SEQ], BF16), Buf("kT%d" % i)) for i in range(2)]
                vA = [(sb("vA%d" % i, [128, 64, 128], BF16), Buf("vA%d" % i)) for i in range(2)]
                qT = [(sb("qT%d" % i, [96, NQ], BF16), Buf("qT%d" % i)) for i in range(1)] * 2
                NP = 3
                Pb = [(sb("P%d" % i, [128, 512], BF16), Buf("P%d" % i)) for i in range(NP)]
                pS = [(ps("pS%d" % i, [128, 512]), Buf("pS%d" % i)) for i in range(4)]
                pO = [(ps("pO%d" % i, [128, 512]), Buf("pO%d" % i)) for i in range(2)]
                pB = ps("pB", [128, 512])
                BpB = Buf("pB")
                rrow = sb("rrow", [128, 512], F32)
                Brrow = Buf("rrow")
                bcs, Bbcs = rrow, Brrow
                Bscr = Buf("scr")
                SC = 96 ** -0.5

                def load_head(h):
                    hb = h % 2
                    S.dma("sp", kT[hb][0][:], KT_scr[h], reads=B_KTs, writes=[kT[hb][1]])
                    S.dma("sp", vA[hb][0][:], V_scr[h].rearrange("p (t e) -> p t e", e=128), reads=B_Vs, writes=[vA[hb][1]])

                def load_q(h):
                    S.dma("sp", qT[0][0][:], QT_scr[h], reads=B_QTs, writes=[qT[0][1]])

                load_head(0)
                load_q(0)
                sidx = 0
                oidx = 0
                Bwo, Bwmq, Bwmo = [], [], []
                pf = []
                for (dst_, src_, gain_, lst_) in ((w_out_bf, w_out, None, Bwo), (w_mq_bf, w_mq, "memxg", Bwmq),
                                                   (w_mo_bf, w_mo, None, Bwmo)):
                    for kc_ in range(8):
                        for hc_ in range(2):
                            pf.append((dst_, src_, gain_, lst_, kc_, hc_))

                def prefetch_piece():
                    if not pf:
                        return
                    dst_, src_, gain_, lst_, kc_, hc_ = pf.pop(0)
                    st_, Bst_ = stgP
                    cs_ = slice(hc_ * 512, (hc_ + 1) * 512)
                    S.dma("sp", st_[:], src_[kc_ * 128:(kc_ + 1) * 128, cs_], writes=[Bst_])
                    b_ = Buf("wpf")
                    lst_.append(b_)
                    eng_ = "pool" if len(pf) % 2 else "dve"
                    if gain_ is not None:
                        TS(eng_, dst_[:, kc_, cs_], st_[:], cc(gain_, kc_), None, ALU.mult, None, [Bst_, Bc], [b_])
                    else:
                        CP(eng_, dst_[:, kc_, cs_], st_[:], [Bst_], [b_])
                for h in range(8):
                    hb = h % 2
                    if h + 1 < 8:
                        load_head(h + 1)
                    k_t, Bk = kT[hb]
                    v_t, Bv = vA[hb]
                    q_t, Bq = qT[hb]
                    sr = 64 if h % 2 == 0 else 0
                    orow = 0 if h % 2 == 0 else 64
                    for g in range(9):
                        if g == 0:
                            N, q0, nk = 128, 0, 32
                            steps = [(t, 0, False) for t in range(31)] + [(31, 0, True)]
                        else:
                            N, q0, nk = 512, 128 + 512 * (g - 1), 32 + 4 * g
                            steps = [(t, 0, False) for t in range(nk - 4)] + [(nk - 4 + d, 128 * d, True) for d in range(4)]
                        po, Bpo = pO[oidx % 2]
                        oidx += 1
                        ns = len(steps)
                        prefetch_piece()

                        def emit_S(i):
                            t, cs, dg = steps[i]
                            p_s, Bps = pS[(sidx + i) % 4]
                            MM(p_s[:, cs:N], k_t[:, t * 128:(t + 1) * 128], q_t[:, q0 + cs:q0 + N], True, True, [Bk, Bq], [Bps])

                        for i in range(min(3, ns)):
                            emit_S(i)
                        for i in range(ns):
                            t, cs, dg = steps[i]
                            p_s, Bps = pS[(sidx + i) % 4]
                            pb, Bpb = Pb[(sidx + i) % NP]
                            ACT(pb[:, cs:N], p_s[:, cs:N], AF.Exp, [Bps, Bc], [Bpb], scale=SC, bias=cc("kbias", t))
                            if dg:
                                MSET("pool", pb[64:128, cs:cs + 64], 0.0, [Bpb])
                            MM(po[:, cs:N], v_t[:, t, :], pb[:, cs:N], i == 0, i == ns - 1, [Bv, Bpb], [Bpo])
                            if i + 3 < ns:
                                emit_S(i + 3)
                        sidx += ns
                        if g == 0:
                            TS("dve", rrow[sr:sr + 1, 0:N], po[sr:sr + 1, 0:N], 1e-30, None, ALU.max, None, [Bpo], [Brrow])
                            RCP(rrow[sr:sr + 1, 0:N], rrow[sr:sr + 1, 0:N], [Brrow], [Brrow])
                        else:
                            RCP(rrow[sr:sr + 1, 0:N], po[sr:sr + 1, 0:N], [Bpo], [Brrow])
                        MM(pB[:, 0:N], ones_f[sr:sr + 1, :], rrow[sr:sr + 1, 0:N], True, True, [Bof, Brrow], [BpB])
                        CP("dve", bcs[orow:orow + 64, 0:N], pB[orow:orow + 64, 0:N], [BpB], [Bbcs])
                        TT("dve", attnT[orow:orow + 64, h // 2, q0:q0 + N], po[orow:orow + 64, 0:N], bcs[orow:orow + 64, 0:N],
                           ALU.mult, [Bpo, Bbcs], [Buf_at[g]])
                    if h + 1 < 8:
                        load_q(h + 1)
                S.barrier()
            if dbg:
                S.dma("sp", dbg_out["attnT"], attnT[:], reads=Buf_at)

        if stop_after >= 4:
            with contextlib.ExitStack() as es3:
                sb, ps = mk(es3)
                pz = [(ps("pz%d" % i, [128, 512]), Buf("pz%d" % i)) for i in range(8)]
                hq = sb("hq", [128, 8, 512], BF16)
                Bhq = Buf("hq")
                sq, Bsq = hq, Bhq
                rs = sb("rs3", [128, 512], F32)
                Brs = Buf("rs3")
                rstd = sb("rstd3", [128, 512], F32)
                Brstd = Buf("rstd3")
                xb3 = [(sb("x3_%d" % i, [128, 8, 512], F32), Buf("x3_%d" % i)) for i in range(2)]
                qraws = [(sb("qraw%d" % i, [128, 2, 512], BF16), Buf("qraw%d" % i)) for i in range(2)]
                qsqs = [(sb("qsq%d" % i, [128, 2, 512], BF16), Buf("qsq%d" % i)) for i in range(2)]
                qmns = [(sb("qmn%d" % i, [128, 2, 512], BF16), Buf("qmn%d" % i)) for i in range(2)]
                pms = [[(sb("pm%d_%d" % (l_, i), [128, 512], BF16), Buf("pm%d_%d" % (l_, i))) for i in range(2)] for l_ in range(2)]
                rss = [(sb("rsx%d" % i, [128, 512], F32), Buf("rsx%d" % i)) for i in range(2)]
                rcss = [(sb("rcs%d" % i, [128, 512], F32), Buf("rcs%d" % i)) for i in range(2)]
                om = sb("om", [128, 8, 512], BF16)
                Bom = [Buf("om%d" % i) for i in range(4)]

                def xcols(g):
                    return (3968, 128) if g == 0 else (4096 + (g - 1) * 512, 512)

                c_, n_ = xcols(0)
                S.dma("sp", xb3[0][0][:, :, 0:n_], xT_v[:, :, c_:c_ + n_], writes=[xb3[0][1]])
                for g in range(9):
                    n = 128 if g == 0 else 512
                    oc = 0 if g == 0 else 128 + (g - 1) * 512
                    xg, Bxg = xb3[g % 2]
                    if g + 1 < 9:
                        c_, n_ = xcols(g + 1)
                        S.dma("sp", xb3[(g + 1) % 2][0][:, :, 0:n_], xT_v[:, :, c_:c_ + n_], writes=[xb3[(g + 1) % 2][1]])
                    for dc in range(8):
                        p_, Bp_ = pz[dc % 2]
                        for kc in range(8):
                            if kc < 4:
                                rhs_, bsrc = uT[:, kc, 30 + oc:30 + oc + n], Buf_u[g]
                            else:
                                rhs_, bsrc = attnT[:, kc - 4, oc:oc + n], Buf_at[g]
                            MM(p_[:, 0:n], w_out_bf[:, kc, dc * 128:(dc + 1) * 128], rhs_, kc == 0, kc == 7,
                               Bwo + [bsrc], [Bp_])
                        TT("dve", xg[:, dc, 0:n], xg[:, dc, 0:n], p_[:, 0:n], ALU.add, [Bxg, Bp_], [Bxg])
                    rms_bcast(xg[:, :, 0:n], [Bxg], n, sq, Bsq, pz[0][0], pz[0][1], rs, Brs, rstd, Brstd)
                    TT("dve", hq[:, :, 0:n], xg[:, :, 0:n], rstd[:, 0:n].unsqueeze(1).to_broadcast([128, 8, n]), ALU.mult,
                       [Bxg, Brstd], [Bhq])
                    def head_chain(hd):
                        L = hd % 2
                        pq_, Bpq_ = pz[2 + 3 * L]
                        pn_, Bpn_ = pz[3 + 3 * L]
                        psc, Bpsc = pz[4 + 3 * L]
                        qraw, Bqraw = qraws[L]
                        qsq, Bqsq = qsqs[L]
                        qmn, Bqmn = qmns[L]
                        rsx, Brsx = rss[L]
                        rcs, Brcs = rcss[L]
                        for j in range(2):
                            ch = hd * 2 + j
                            for kc in range(8):
                                MM(pq_[:, 0:n], w_mq_bf[:, kc, ch * 128:(ch + 1) * 128], hq[:, kc, 0:n], kc == 0, kc == 7,
                                   Bwmq + [Bhq], [Bpq_])
                            yield
                            ACT(qsq[:, j, 0:n], pq_[:, 0:n], AF.Square, [Bpq_], [Bqsq], scale=1.0 / 16)
                            yield
                            CP("dve", qraw[:, j, 0:n], pq_[:, 0:n], [Bpq_], [Bqraw])
                            yield
                        for j in range(2):
                            MM(pn_[:, 0:n], ones_bf[:], qsq[:, j, 0:n], j == 0, j == 1, [Bob, Bqsq], [Bpn_])
                        yield
                        ACT(rsx[:, 0:n], pn_[:, 0:n], AF.Sqrt, [Bpn_, Bc], [Brsx], bias=cc("eps"))
                        yield
                        RCP(rsx[:, 0:n], rsx[:, 0:n], [Brsx], [Brsx])
                        yield
                        for j in range(2):
                            STT(qmn[:, j, 0:n], qraw[:, j, 0:n], cc("mqg", j), rsx[:, 0:n], ALU.mult, ALU.mult,
                                [Bqraw, Brsx, Bc], [Bqmn])
                            yield
                        for kt in range(2):
                            for j in range(2):
                                MM(psc[:, 0:n], memK[:, hd * 2 + j, kt * 128:(kt + 1) * 128], qmn[:, j, 0:n], j == 0, j == 1,
                                   [BmK, Bqmn], [Bpsc])
                            yield
                            ACT(pms[L][kt][0][:, 0:n], psc[:, 0:n], AF.Exp, [Bpsc], [pms[L][kt][1]], scale=1.0 / 16)
                            yield
                        for kt in range(2):
                            MM(pn_[:, 0:n], ones_bf[:], pms[L][kt][0][:, 0:n], kt == 0, kt == 1, [Bob, pms[L][kt][1]], [Bpn_])
                        yield
                        RCP(rcs[:, 0:n], pn_[:, 0:n], [Bpn_], [Brcs])
                        yield
                        for j in range(2):
                            for kt in range(2):
                                MM(pq_[:, 0:n], memV[:, kt, (hd * 2 + j) * 128:(hd * 2 + j + 1) * 128], pms[L][kt][0][:, 0:n],
                                   kt == 0, kt == 1, [BmV, pms[L][kt][1]], [Bpq_])
                            yield
                            TT("dve", om[:, hd * 2 + j, 0:n], pq_[:, 0:n], rcs[:, 0:n], ALU.mult, [Bpq_, Brcs], [Bom[hd]])
                            yield

                    def rr3(*gens):
                        gens = list(gens)
                        while gens:
                            for g_ in list(gens):
                                try:
                                    next(g_)
                                except StopIteration:
                                    gens.remove(g_)

                    rr3(head_chain(0), head_chain(1))
                    rr3(head_chain(2), head_chain(3))
                    for dc in range(8):
                        p_, Bp_ = pz[dc % 2]
                        for kc in range(8):
                            MM(p_[:, 0:n], w_mo_bf[:, kc, dc * 128:(dc + 1) * 128], om[:, kc, 0:n], kc == 0, kc == 7,
                               Bwmo + [Bom[kc // 2]], [Bp_])
                        TT("dve", xg[:, dc, 0:n], xg[:, dc, 0:n], p_[:, 0:n], ALU.add, [Bxg, Bp_], [Bxg])
                    S.dma("sp", x2_v[:, :, oc:oc + n], xg[:, :, 0:n], reads=[Bxg], writes=[B_x2[g]])
                S.barrier()
        esX.close()

        if stop_after >= 5:
            with contextlib.ExitStack() as es4:
                sb, ps = mk(es4)
                w_up_bf = sb("w_up_bf", [128, 8, 2 * DFF], BF16)
                w_dn_bf = sb("w_dn_bf", [128, 22, D], BF16)
                stg4 = [(sb("stg4_%d" % i, [128, 512], F32), Buf("stg4_%d" % i)) for i in range(3)]
                x4 = sb("x4", [128, 8, 512], F32)
                Bx4 = Buf("x4")
                x4h = sb("x4h", [128, 8, 2], F32)
                Bx4h = Buf("x4h")
                h3 = sb("h3", [128, 8, 512], BF16)
                Bh3 = Buf("h3")
                h3h = sb("h3h", [128, 8, 2], BF16)
                Bh3h = Buf("h3h")
                sq, Bsq = h3, Bh3
                rs = sb("rs4", [128, 512], F32)
                Brs = Buf("rs4")
                rstd, Brstd = rs, Brs
                prev = sb("prev", [128, 44, 2], F32)
                Bprev = [Buf("prev%d" % r) for r in range(44)]
                upx = [(sb("upx%d" % i, [128, 514], F32), Buf("upx%d" % i)) for i in range(2)]
                yv = [(sb("yv%d" % i, [128, 512], F32), Buf("yv%d" % i)) for i in range(3)]
                actT = sb("actT", [128, 22, 512], BF16)
                Bact = [Buf("act%d" % j) for j in range(22)]
                ost = [(sb("ost%d" % i, [128, 512], F32), Buf("ost%d" % i)) for i in range(2)]
                pu = [(ps("pu%d" % i, [128, 512]), Buf("pu%d" % i)) for i in range(4)]
                pd = [(ps("pd%d" % i, [128, 512]), Buf("pd%d" % i)) for i in range(2)]
                pss = ps("pss", [128, 512])
                Bpss = Buf("pss")
                Bup = {}
                Bdn = {}
                s4 = [0]

                def load_up_block(c0_, c1_):
                    for c0 in range(c0_, c1_, 512):
                        _load_up_piece(c0, min(c1_, c0 + 512))

                def _load_up_piece(c0, c1):
                    for kc in range(8):
                        i = s4[0]
                        s4[0] += 1
                        st, Bst = stg4[i % 3]
                        S.dma("sp", st[:, 0:c1 - c0], w_up[kc * 128:(kc + 1) * 128, c0:c1], writes=[Bst])
                        b_ = Buf("wup")
                        eng = ("pool", "dve", "act")[i % 3]
                        o = w_up_bf[:, kc, c0:c1]
                        if eng == "act":
                            ACT(o, st[:, 0:c1 - c0], AF.Copy, [Bst, Bc], [b_], scale=cc("ffng", kc))
                        else:
                            TS(eng, o, st[:, 0:c1 - c0], cc("ffng", kc), None, ALU.mult, None, [Bst, Bc], [b_])
                        for r in range(c0 // 128, c1 // 128):
                            Bup.setdefault(r, []).append(b_)

                def load_dn(kc):
                    Bdn[kc] = []
                    for hc in range(2):
                        i = s4[0]
                        s4[0] += 1
                        st, Bst = stg4[i % 3]
                        S.dma("sp", st[:], w_dn[kc * 128:(kc + 1) * 128, hc * 512:(hc + 1) * 512], writes=[Bst])
                        b_ = Buf("wdn")
                        CP(("pool", "dve", "act")[i % 3], w_dn_bf[:, kc, hc * 512:(hc + 1) * 512], st[:], [Bst], [b_])
                        Bdn[kc].append(b_)

                S.dma("sp", x4h[:], x2_v[:, :, 126:128], reads=[B_x2[0]], writes=[Bx4h])
                S.dma("sp", x4[:], x2_v[:, :, 128:640], reads=[B_x2[1]], writes=[Bx4])
                load_up_block(0, 1024)
                load_up_block(2816, 3840)
                rms_bcast(x4h[:], [Bx4h], 2, h3h, Bh3h, pss, Bpss, rs, Brs, rstd, Brstd)
                TT("dve", h3h[:], x4h[:], rstd[:, 0:2].unsqueeze(1).to_broadcast([128, 8, 2]), ALU.mult,
                   [Bx4h, Brstd], [Bh3h])
                ui = 0
                dn_next = [0]
                for g in range(8):
                    oc = 128 + g * 512
                    if g > 0:
                        S.dma("sp", x4[:], x2_v[:, :, oc:oc + 512], reads=[B_x2[g + 1]], writes=[Bx4])
                    rms_bcast(x4[:], [Bx4], 512, sq, Bsq, pss, Bpss, rs, Brs, rstd, Brstd)
                    TT("dve", h3[:], x4[:], rstd[:].unsqueeze(1).to_broadcast([128, 8, 512]), ALU.mult, [Bx4, Brstd], [Bh3])
                    for j in range(22):
                        if g == 0:
                            if j == 0:
                                load_up_block(1024, 2048)
                                load_up_block(3840, 4864)
                            if j == 6:
                                load_up_block(2048, 2816)
                                load_up_block(4864, 5632)
                            if j >= 10:
                                for _ in range(2):
                                    if dn_next[0] < 22:
                                        load_dn(dn_next[0])
                                        dn_next[0] += 1
                        ys = []
                        for r in (j, 22 + j):
                            p_, Bp_ = pu[ui % 4]
                            ux, Bux = upx[ui % 2]
                            y_, By_ = yv[ui % 3]
                            ui += 1
                            if g == 0:
                                ph, Bph = pu[ui % 4]
                                for kc in range(8):
                                    MM(ph[:, 0:2], w_up_bf[:, kc, r * 128:(r + 1) * 128], h3h[:, kc, :], kc == 0, kc == 7,
                                       Bup[r] + [Bh3h], [Bph])
                                TS("dve", prev[:, r, :], ph[:, 0:2], cc("flag"), None, ALU.mult, None, [Bph, Bc], [Bprev[r]])
                            for kc in range(8):
                                MM(p_[:], w_up_bf[:, kc, r * 128:(r + 1) * 128], h3[:, kc, :], kc == 0, kc == 7, Bup[r] + [Bh3], [Bp_])
                            CP("act", ux[:, 2:514], p_[:], [Bp_], [Bux])
                            ACT(y_[:], p_[:], AF.Identity, [Bp_, Bc], [By_], scale=cc("wffn", r * 3 + 2), bias=cc("bffn", r))
                            CP("pool", ux[:, 0:2], prev[:, r, :], [Bprev[r], Bux], [Bux])
                            STT(y_[:], ux[:, 1:513], cc("wffn", r * 3 + 1), y_[:], ALU.mult, ALU.add, [Bux, By_, Bc], [By_])
                            STT(y_[:], ux[:, 0:512], cc("wffn", r * 3 + 0), y_[:], ALU.mult, ALU.add, [Bux, By_, Bc], [By_])
                            CP("pool", prev[:, r, :], ux[:, 512:514], [Bux], [Bprev[r]])
                            ys.append((y_, By_))
                        ACT(ys[0][0][:], ys[0][0][:], AF.Silu, [ys[0][1]], [ys[0][1]])
                        TT("dve", actT[:, j, :], ys[0][0][:], ys[1][0][:], ALU.mult, [ys[0][1], ys[1][1]], [Bact[j]])
                    for dc in range(8):
                        p_, Bp_ = pd[dc % 2]
                        o_, Bo_ = ost[dc % 2]
                        for kc in range(22):
                            MM(p_[:], w_dn_bf[:, kc, dc * 128:(dc + 1) * 128], actT[:, kc, :], kc == 0, kc == 21, Bdn[kc] + [Bact[kc]], [Bp_])
                        TT("dve", o_[:], x4[:, dc, :], p_[:], ALU.add, [Bx4, Bp_], [Bo_])
                        S.dma("sp", yT_v[:, dc, g * 512:(g + 1) * 512], o_[:], reads=[Bo_])
        S.finish()
    return nc, S


def make_in_maps(inputs):
    x = np.asarray(inputs["x"], np.float32)
    mem = np.asarray(inputs["mem"], np.float32)
    positions = np.asarray(inputs["positions"], np.int32)
    shared = {
        "w_in": np.ascontiguousarray(inputs["w_in"][0], np.float32),
        "w_uq": np.ascontiguousarray(inputs["w_uq"][0], np.float32),
        "w_ukv": np.ascontiguousarray(inputs["w_ukv"][0], np.float32),
        "w_out": np.ascontiguousarray(inputs["w_out"][0], np.float32),
        "w_mem_q": np.ascontiguousarray(inputs["w_mem_q"][0], np.float32),
        "w_mem_kv": np.ascontiguousarray(inputs["w_mem_kv"][0], np.float32),
        "w_mem_o": np.ascontiguousarray(inputs["w_mem_o"][0], np.float32),
        "w_up": np.ascontiguousarray(inputs["w_up"][0], np.float32),
        "w_down": np.ascontiguousarray(inputs["w_down"][0], np.float32),
    }
    csts = [pack_consts(inputs, 0), pack_consts(inputs, 1)]
    in_maps = []
    for core in range(8):
        b, half = core // 2, core % 2
        xT = np.zeros((D, SEQ), np.float32)
        p = np.zeros((SEQ,), np.int32)
        if half == 1:
            xT[:] = x[b].T
            p[:] = positions[b]
        else:
            xT[:, 4096:] = x[b, 0:4096].T
            p[4096:] = positions[b, 0:4096]
        m = dict(shared)
        m["xT"] = xT
        m["pos"] = np.ascontiguousarray(p.reshape(64, 128).T)
        m["cst"] = csts[half]
        m["memT"] = np.ascontiguousarray(mem[b].T)
        in_maps.append(m)
    return in_maps


_PROG = {}


def kernel(**inputs):
    if "nc" not in _PROG:
        _PROG["nc"] = build_program()[0]
    nc = _PROG["nc"]
    in_maps = make_in_maps(inputs)
    res = run_bass_kernel_spmd(nc, in_maps, core_ids=list(range(8)))
    out = np.empty((4, SEQ, D), np.float32)
    for core in range(8):
        b, half = core // 2, core % 2
        out[b, half * 4096:(half + 1) * 4096, :] = res.results[core]["yT"].T
    return out
```

```python
import contextlib
import numpy as np
import concourse.bass as bass
import concourse.mybir as mybir
from concourse.bass_utils import run_bass_kernel_spmd

F32 = mybir.dt.float32
BF16 = mybir.dt.bfloat16
I32 = mybir.dt.int32
AF = mybir.ActivationFunctionType
ALU = mybir.AluOpType
AX = mybir.AxisListType

D = 1024
SEQ = 8192
NTOK = 4096
NQ = 4224
EPS = 1e-6
DFF = 2816
PI = float(np.pi)

_COLS = [("mixg", 8), ("bcin", 8), ("wdw", 124), ("bdw", 4), ("lng", 4), ("lnb", 4), ("memxg", 8),
         ("memmg", 8), ("mqg", 2), ("mkg", 2), ("ffng", 8), ("wffn", 132), ("bffn", 44), ("eps", 1),
         ("flag", 1), ("kbias", 64), ("gql", 256), ("gkvl", 128), ("gqn", 96), ("gkn", 96),
         ("invf", 16), ("ident", 128)]
C = {}
_o = 0
for _n, _w in _COLS:
    C[_n] = _o
    _o += _w
NCST = _o


def _colmajor(v, k):
    return np.ascontiguousarray(np.asarray(v, np.float32).reshape(k, 128).T)


def pack_consts(inp, half):
    c = np.zeros((128, NCST), np.float32)

    def put(name, arr):
        arr = np.asarray(arr, np.float32)
        c[:, C[name]:C[name] + arr.shape[1]] = arr

    put("mixg", _colmajor(inp["mix_norm_g"][0], 8))
    put("bcin", _colmajor(inp["b_conv_in"][0], 8))
    wdw = np.asarray(inp["w_conv_dw"][0], np.float32)
    put("wdw", wdw.reshape(31, 4, 128).transpose(2, 1, 0).reshape(128, 124))
    put("bdw", _colmajor(inp["b_conv_dw"][0], 4))
    put("lng", _colmajor(inp["conv_ln_g"][0], 4))
    put("lnb", _colmajor(inp["conv_ln_b"][0], 4))
    put("memxg", _colmajor(inp["mem_norm_x_g"][0], 8))
    put("memmg", _colmajor(inp["mem_norm_m_g"][0], 8))
    put("mqg", _colmajor(inp["mem_q_norm_g"][0], 2))
    put("mkg", _colmajor(inp["mem_k_norm_g"][0], 2))
    put("ffng", _colmajor(inp["ffn_norm_g"][0], 8))
    wf = np.asarray(inp["w_ffn_dw"][0], np.float32)
    put("wffn", wf.reshape(3, 44, 128).transpose(2, 1, 0).reshape(128, 132))
    put("bffn", _colmajor(inp["b_ffn_dw"][0], 44))
    c[:, C["eps"]] = EPS
    c[:, C["flag"]] = 1.0 if half == 1 else 0.0
    if half == 0:
        c[:, C["kbias"]:C["kbias"] + 32] = -30000.0
    put("gql", np.tile(np.asarray(inp["q_lat_norm_g"][0], np.float32)[None, :], (128, 1)))
    put("gkvl", np.tile(np.asarray(inp["kv_lat_norm_g"][0], np.float32)[None, :], (128, 1)))
    put("gqn", np.tile(np.asarray(inp["q_norm_g"][0], np.float32)[None, :], (128, 1)))
    put("gkn", np.tile(np.asarray(inp["k_norm_g"][0], np.float32)[None, :], (128, 1)))
    invf = (np.float32(10000.0) ** (-np.arange(0, 32, 2, dtype=np.float32) / np.float32(32))).astype(np.float32)
    put("invf", np.tile(invf[None, :], (128, 1)))
    put("ident", np.eye(128, dtype=np.float32))
    return c


_PSUM_PREFIX = ("pA", "pG", "pL", "pT", "pQ", "pKV", "pY", "pM", "pS", "pO", "pB", "pz", "pu", "pd", "pss")


class Buf:
    __slots__ = ("name", "w", "r", "psum")

    def __init__(self, name=""):
        self.name = name
        self.w = None
        self.r = []
        self.psum = name.startswith(_PSUM_PREFIX)


class Sched:
    NDMA = 24

    def __init__(self, nc):
        self.nc = nc
        self.ops = []
        self.eng = {"pe": nc.tensor, "act": nc.scalar, "dve": nc.vector, "pool": nc.gpsimd, "sp": nc.sync}

    def init_sems(self, es):
        self.sem = {e: es.enter_context(self.nc.semaphore("s_" + e)) for e in self.eng}
        self.dsem = [es.enter_context(self.nc.semaphore("d%d" % i)) for i in range(self.NDMA)]

    def op(self, eng, fn, reads=(), writes=()):
        self.ops.append(("c", eng, fn, tuple(reads), tuple(writes)))

    def dma(self, q, out, in_, reads=(), writes=()):
        eng = self.eng[q]
        self.ops.append(("d", q, (lambda: eng.dma_start(out=out, in_=in_)), tuple(reads), tuple(writes)))

    def barrier(self):
        self.ops.append(("b",))

    def finish(self):
        ops = self.ops
        n = len(ops)
        known = {e: {} for e in self.eng}
        vc = [None] * n
        waits = [None] * n
        signaled = [False] * n
        dma_use = [0] * self.NDMA
        dma_last = [None] * self.NDMA
        dma_ev = {}
        ndma = 0
        pend = {e: set() for e in self.eng}
        last_c = {}
        for i, o in enumerate(ops):
            if o[0] == "b":
                allp = set(last_c.values()) | set(x for x in dma_last if x is not None)
                for e in self.eng:
                    pend[e] |= allp
                waits[i] = {}
                vc[i] = {}
                continue
            kind, e = o[0], o[1]
            reads, writes = o[3], o[4]
            deps = set(pend[e])
            pend[e] = set()
            if kind == "c":
                last_c[e] = i
            for b in reads:
                if b.w is not None:
                    deps.add(b.w)
                if b.psum:
                    deps.update(x for x in b.r if ops[x][1] != e)
            for b in writes:
                if b.w is not None:
                    deps.add(b.w)
                deps.update(b.r)
            if kind == "d":
                s = ndma % self.NDMA
                ndma += 1
                if dma_last[s] is not None:
                    deps.add(dma_last[s])
                dma_use[s] += 1
                dma_ev[i] = (s, dma_use[s])
                dma_last[s] = i
            kn = known[e]
            w = {}
            for d in sorted(deps, reverse=True):
                od = ops[d]
                if od[0] == "c":
                    src = od[1]
                    if src == "pe" and e == "pe" and kind == "c":
                        continue
                    val = d
                else:
                    src = ("dma", dma_ev[d][0])
                    val = dma_ev[d][1]
                if kn.get(src, -1) >= val:
                    continue
                w[src] = max(w.get(src, -1), val)
                kn[src] = val
                for k2, v2 in vc[d].items():
                    if kn.get(k2, -1) < v2:
                        kn[k2] = v2
                if od[0] == "c":
                    signaled[d] = True
            waits[i] = w
            c = dict(kn)
            if kind == "c":
                c[e] = max(c.get(e, -1), i)
            else:
                c[("dma", dma_ev[i][0])] = dma_ev[i][1]
            vc[i] = c
            for b in reads:
                b.r.append(i)
            for b in writes:
                b.w = i
                b.r = []
        cnt = {e: 0 for e in self.eng}
        sigval = {}
        for i, o in enumerate(ops):
            if o[0] == "c" and signaled[i]:
                cnt[o[1]] += 1
                sigval[i] = cnt[o[1]]
        nw = 0
        self.trace = {e: [] for e in self.eng}
        for i, o in enumerate(ops):
            if o[0] == "b":
                continue
            e = o[1]
            self.trace[e].append((i, [((src, 16 * val) if isinstance(src, tuple) else (src, sigval[val])) for src, val in waits[i].items()],
                                  (("c", e, 1) if (o[0] == "c" and signaled[i]) else (("d", dma_ev[i][0], 16) if o[0] == "d" else None))))
            eng = self.eng[e]
            for src, val in waits[i].items():
                nw += 1
                if isinstance(src, tuple):
                    eng.wait_ge(self.dsem[src[1]], 16 * val)
                else:
                    eng.wait_ge(self.sem[src], sigval[val])
            ins = o[2]()
            if o[0] == "c":
                if signaled[i]:
                    ins.then_inc(self.sem[e], 1)
            else:
                ins.then_inc(self.dsem[dma_ev[i][0]], 16)
        sp = self.eng["sp"]
        for s in range(self.NDMA):
            if dma_use[s] > 0:
                sp.wait_ge(self.dsem[s], 16 * dma_use[s])
        self.stats = dict(nops=n, nwaits=nw, nsig=dict(cnt))


def build_program(dbg=False, stop_after=99):
    nc = bass.Bass("TRN2", target_bir_lowering=False)
    S = Sched(nc)
    E = S.eng

    def din(name, shape, dt=F32):
        return nc.dram_tensor(name, list(shape), dt, kind="ExternalInput").ap()

    def dscr(name, shape, dt):
        return nc.dram_tensor(name, list(shape), dt, kind=("ExternalOutput" if dbg else "Internal")).ap()

    xT = din("xT", [D, SEQ])
    pos = din("pos", [128, 64], I32)
    cst = din("cst", [128, NCST])
    memT = din("memT", [D, 256])
    w_in = din("w_in", [D, 1440])
    w_uq = din("w_uq", [256, 768])
    w_ukv = din("w_ukv", [128, 1024])
    w_out = din("w_out", [D, D])
    w_mq = din("w_mem_q", [D, D])
    w_mkv = din("w_mem_kv", [D, 2 * D])
    w_mo = din("w_mem_o", [D, D])
    w_up = din("w_up", [D, 2 * DFF])
    w_dn = din("w_down", [DFF, D])
    yT = nc.dram_tensor("yT", [D, NTOK], F32, kind="ExternalOutput").ap()
    QT_scr = dscr("QT_scr", [8, 96, NQ], BF16)
    KT_scr = dscr("KT_scr", [8, 96, SEQ], BF16)
    V_scr = dscr("V_scr", [8, 128, 64 * 128], BF16)
    x2_scr = dscr("x2_scr", [D, NQ], F32)
    dbg_out = {}
    if dbg:
        dbg_out["uT"] = nc.dram_tensor("dbg_uT", [128, 4, NQ + 30], BF16, kind="ExternalOutput").ap()
        dbg_out["ufT"] = nc.dram_tensor("dbg_ufT", [128, 4, NQ], BF16, kind="ExternalOutput").ap()
        dbg_out["attnT"] = nc.dram_tensor("dbg_attnT", [128, 4, NQ], BF16, kind="ExternalOutput").ap()

    xT_v = xT.rearrange("(kc p) t -> p kc t", p=128)
    x2_v = x2_scr.rearrange("(kc p) t -> p kc t", p=128)
    yT_v = yT.rearrange("(kc p) t -> p kc t", p=128)

    with contextlib.ExitStack() as es0:
        S.init_sems(es0)

        uid = [0]

        def mk(es):
            def sb(name, shape, dt):
                uid[0] += 1
                return es.enter_context(nc.sbuf_tensor("%s_%d" % (name, uid[0]), list(shape), dt))

            def ps(name, shape, dt=F32):
                uid[0] += 1
                return es.enter_context(nc.psum_tensor("%s_%d" % (name, uid[0]), list(shape), dt))
            return sb, ps

        sb0, _ = mk(es0)

        def ACT(out, in_, func, r, w, **kw):
            S.op("act", lambda: nc.scalar.activation(out, in_, func, **kw), r, w)

        def TT(eng, out, a, b, op, r, w):
            S.op(eng, lambda: E[eng].tensor_tensor(out, a, b, op), r, w)

        def TS(eng, out, a, s1, s2, op0, op1, r, w):
            if op1 is None:
                S.op(eng, lambda: E[eng].tensor_scalar(out, a, s1, None, op0), r, w)
            else:
                S.op(eng, lambda: E[eng].tensor_scalar(out, a, s1, s2, op0, op1), r, w)

        def STT(out, a, s, b, op0, op1, r, w):
            S.op("dve", lambda: nc.vector.scalar_tensor_tensor(out, a, s, b, op0, op1), r, w)

        def CP(eng, out, in_, r, w):
            if eng == "act":
                S.op("act", lambda: nc.scalar.copy(out, in_), r, w)
            else:
                S.op(eng, lambda: E[eng].tensor_copy(out, in_), r, w)

        def MM(out, lhsT, rhs, start, stop, r, w):
            S.op("pe", lambda: nc.tensor.matmul(out, lhsT, rhs, start=start, stop=stop), r, w)

        def TR(out, in_, ident, r, w):
            S.op("pe", lambda: nc.tensor.transpose(out, in_, ident), r, w)

        def RED(out, in_, r, w):
            S.op("dve", lambda: nc.vector.tensor_reduce(out, in_, AX.X, ALU.add), r, w)

        def RCP(out, in_, r, w):
            S.op("dve", lambda: nc.vector.reciprocal(out, in_), r, w)

        def MSET(eng, out, val, w):
            S.op(eng, lambda: E[eng].memset(out, val), (), w)

        cst_t = sb0("cst_t", [128, NCST], F32)
        Bc = Buf("cst")
        S.dma("sp", cst_t[:], cst, writes=[Bc])

        def cc(name, j=0, w=1):
            return cst_t[:, C[name] + j:C[name] + j + w]

        ident = sb0("ident", [128, 128], BF16)
        ones_bf = sb0("ones_bf", [128, 128], BF16)
        ones_f = sb0("ones_f", [128, 128], F32)
        Bid, Bob, Bof = Buf("ident"), Buf("ones_bf"), Buf("ones_f")
        CP("dve", ident[:], cc("ident", 0, 128), [Bc], [Bid])
        MSET("pool", ones_bf[:], 1.0, [Bob])
        MSET("pool", ones_f[:], 1.0, [Bof])

        esX = es0.enter_context(contextlib.ExitStack())
        uT = esX.enter_context(nc.sbuf_tensor("uT", [128, 4, NQ + 30], BF16))
        Buf_u = [Buf("u%d" % i) for i in range(9)]
        Bupad = Buf("upad")
        MSET("pool", uT[:, :, 0:30], 0.0, [Bupad])
        Buf_at = [Buf("at%d" % i) for i in range(9)]
        B_KTs = [Buf("KTs%d" % i) for i in range(16)]
        B_Vs = [Buf("Vs%d" % i) for i in range(16)]
        B_QTs = [Buf("QTs%d" % i) for i in range(9)]
        B_x2 = [Buf("x2s%d" % i) for i in range(9)]
        stgs = []

        @contextlib.contextmanager
        def staging():
            with contextlib.ExitStack() as esS:
                sbS, _ = mk(esS)
                stgs[:] = [(sbS("stg%d" % i, [128, 1024], F32), Buf("stg%d" % i)) for i in range(3)]
                yield
                S.barrier()
            stgs[:] = []

        stg_n = [0]

        def load_w(dst, name, src, K, N, gain=None):
            bufs = []
            for kc in range(K // 128):
                for c0 in range(0, N, 1024):
                    c1 = min(N, c0 + 1024)
                    wd = c1 - c0
                    i = stg_n[0]
                    stg_n[0] += 1
                    st, Bst = stgs[i % 3]
                    S.dma("sp", st[:, 0:wd], src[kc * 128:(kc + 1) * 128, c0:c1], writes=[Bst])
                    b = Buf(name)
                    bufs.append(b)
                    eng = ("act", "dve", "pool")[i % 3]
                    o = dst[:, kc, c0:c1]
                    if gain is not None:
                        g = cc(gain, kc)
                        if eng == "act":
                            ACT(o, st[:, 0:wd], AF.Copy, [Bst, Bc], [b], scale=g)
                        else:
                            TS(eng, o, st[:, 0:wd], g, None, ALU.mult, None, [Bst, Bc], [b])
                    else:
                        CP(eng, o, st[:, 0:wd], [Bst], [b])
            return bufs

        def rms_bcast(xin, Bx, n, sq, Bsq, pbank, Bp, rs, Brs, rstd, Brstd, nfeat=1024):
            kcs = nfeat // 128
            ACT(sq[:, 0:kcs, 0:n], xin, AF.Square, Bx, [Bsq])
            for kc in range(kcs):
                MM(pbank[:, 0:n], ones_bf[:], sq[:, kc, 0:n], kc == 0, kc == kcs - 1, [Bob, Bsq], [Bp])
            ACT(rs[:, 0:n], pbank[:, 0:n], AF.Sqrt, [Bp, Bc], [Brs], scale=1.0 / nfeat, bias=cc("eps"))
            RCP(rstd[:, 0:n], rs[:, 0:n], [Brs], [Brstd])

        with contextlib.ExitStack() as es1:
            if True:
                esA = es1
                sb, ps = mk(esA)
                cos2 = sb("cos2", [128, 64, 32], F32)
                sin2 = sb("sin2", [128, 64, 32], F32)
                Btab = Buf("tab")
                with contextlib.ExitStack() as esT:
                    sbt, _ = mk(esT)
                    pos_i = sbt("pos_i", [128, 64], I32)
                    posf = sbt("posf", [128, 64], F32)
                    ang = sbt("ang", [128, 64, 2, 16], F32)
                    tf = sbt("tf", [128, 2048], F32)
                    ti_ = sbt("ti_", [128, 2048], I32)
                    sc = sbt("sc", [128, 64, 2, 16], F32)
                    Bt = Buf("t")
                    angf = ang[:].rearrange("p a b c -> p (a b c)")
                    S.dma("sp", pos_i[:], pos, writes=[Bt])
                    CP("dve", posf[:], pos_i[:], [Bt], [Bt])
                    TT("dve", ang[:, :, 0, :], posf[:].unsqueeze(2).to_broadcast([128, 64, 16]),
                       cc("invf", 0, 16).unsqueeze(1).to_broadcast([128, 64, 16]), ALU.mult, [Bt, Bc], [Bt])
                    TS("dve", ang[:, :, 1, :], ang[:, :, 0, :], PI / 2, None, ALU.add, None, [Bt], [Bt])
                    C1 = 6.28125
                    C2 = 2 * np.pi - 6.28125
                    TS("dve", tf[:], angf, 1.0 / (2 * np.pi), None, ALU.mult, None, [Bt], [Bt])
                    CP("dve", ti_[:], tf[:], [Bt], [Bt])
                    CP("dve", tf[:], ti_[:], [Bt], [Bt])
                    STT(angf, tf[:], -C1, angf, ALU.mult, ALU.add, [Bt], [Bt])
                    STT(angf, tf[:], -C2, angf, ALU.mult, ALU.add, [Bt], [Bt])
                    TS("dve", tf[:], angf, PI, -2 * PI, ALU.is_gt, ALU.mult, [Bt], [Bt])
                    TT("dve", angf, angf, tf[:], ALU.add, [Bt], [Bt])
                    TS("dve", tf[:], angf, -PI, 2 * PI, ALU.is_lt, ALU.mult, [Bt], [Bt])
                    TT("dve", angf, angf, tf[:], ALU.add, [Bt], [Bt])
                    ACT(sc[:].rearrange("p a b c -> p (a b c)"), angf, AF.Sin, [Bt], [Bt])
                    CP("dve", cos2[:, :, 0:16], sc[:, :, 1, :], [Bt], [Btab])
                    CP("dve", cos2[:, :, 16:32], sc[:, :, 1, :], [Bt, Btab], [Btab])
                    CP("dve", sin2[:, :, 16:32], sc[:, :, 0, :], [Bt, Btab], [Btab])
                    TS("dve", sin2[:, :, 0:16], sc[:, :, 0, :], -1.0, None, ALU.mult, None, [Bt, Btab], [Btab])
                    S.barrier()

                w_in_bf = sb("w_in_bf", [128, 8, 1440], BF16)
                w_uq_bf = sb("w_uq_bf", [128, 2, 768], BF16)
                w_ukv_bf = sb("w_ukv_bf", [128, 1, 1024], BF16)
                with staging():
                    Bwin = load_w(w_in_bf, "w_in_bf", w_in, D, 1440, gain="mixg")
                    Bwuq = load_w(w_uq_bf, "w_uq_bf", w_uq, 256, 768)
                    Bwukv = load_w(w_ukv_bf, "w_ukv_bf", w_ukv, 128, 1024)

                xbuf = [(sb("xg%d" % i, [128, 8, 512], F32), Buf("xg%d" % i)) for i in range(2)]
                sq = sb("sq", [128, 8, 512], BF16)
                Bsq = Buf("sq")
                hT = sb("hT", [128, 8, 512], BF16)
                BhT = Buf("hT")
                rs = sb("rs", [128, 512], F32)
                Brs = Buf("rs")
                rstd = sb("rstd", [128, 512], F32)
                Brstd = Buf("rstd")
                sig = sb("sig", [128, 512], F32)
                Bsig = Buf("sig")
                sqLs = [(sb("sqL%d" % i, [128, 384], F32), Buf("sqL%d" % i)) for i in range(2)]
                st2s = [(sb("st2%d" % i, [128, 8], F32), Buf("st2%d" % i)) for i in range(2)]
                cns = [(sb("cn%d" % i, [128, 384], BF16), Buf("cn%d" % i)) for i in range(2)]
                cTs = [(sb("cT%d" % i, [128, 384], BF16), Buf("cT%d" % i)) for i in range(2)]
                sqq = sb("sqq", [128, 768], F32)
                Bsqq = Buf("sqq")
                stqs = [(sb("stq%d" % i, [128, 24], F32), Buf("stq%d" % i)) for i in range(2)]
                rts = [(sb("rt1_%d" % i, [128, 8, 32], F32), sb("rt2_%d" % i, [128, 8, 32], F32), Buf("rt%d" % i)) for i in range(2)]
                qbfs = [(sb("qbf%d" % i, [128, 8, 96], BF16), Buf("qbf%d" % i)) for i in range(2)]
                kvs = [sb("kvs%d" % i, [128, 1056], F32) for i in range(2)]
                Bkv_k = [Buf("kvk%d" % i) for i in range(2)]
                Bkv_v = [Buf("kvv%d" % i) for i in range(2)]
                Bkv_r = [Buf("kvr%d" % i) for i in range(2)]
                qs = [sb("qs%d" % i, [128, 768], F32) for i in range(2)]
                Bqs = [Buf("qs%d" % i) for i in range(2)]
                sqk = sb("sqk", [128, 8, 64], F32)
                Bsqk = Buf("sqk")
                sqr = sb("sqr", [128, 32], F32)
                Bsqr = Buf("sqr")
                stks = [(sb("stk%d" % i, [128, 32], F32), Buf("stk%d" % i)) for i in range(2)]
                krs = [(sb("kr%d" % i, [128, 4, 32], F32), Buf("kr%d" % i)) for i in range(2)]
                kbfs = [(sb("kbf%d" % i, [128, 8, 96], BF16), Buf("kbf%d" % i)) for i in range(2)]
                KTst = [(sb("KTst%d" % i, [96, 8, 512], BF16), Buf("KTst%d" % i)) for i in range(1)]
                QTst = [(sb("QTst%d" % i, [96, 8, 512], BF16), Buf("QTst%d" % i)) for i in range(1)]
                Vst = [(sb("Vst%d" % i, [128, 8, 4, 128], BF16), Buf("Vst%d" % i)) for i in range(1)]
                for (v, bv) in Vst:
                    MSET("pool", v[:, 0:8:2, :, 64:128], 1.0, [bv])
                    MSET("pool", v[:, 1:8:2, :, 0:64], 1.0, [bv])
                pA = ps("pA", [128, 512])
                pG = ps("pG", [128, 512])
                pTk = ps("pTk", [128, 1024], BF16)
                pT = ps("pT", [128, 1024], BF16)
                pQ = ps("pQ", [128, 1024])
                pKV = ps("pKV", [128, 1024])
                BpA, BpG, BpTk, BpT, BpQ, BpKV = [Buf(n) for n in "pA pG pTk pT pQ pKV".split()]
                pLs = [(pA, BpA), (pG, BpG)]
                KT_v = KT_scr.rearrange("h d t -> d h t")
                QT_v = QT_scr.rearrange("h d t -> d h t")
                V_v = V_scr.rearrange("h p (t e) -> p h t e", e=128)
                R96 = 96 ** -0.5

                S.dma("sp", xbuf[0][0][:], xT_v[:, :, 0:512], writes=[xbuf[0][1]])
                for G in range(16):
                    own = G >= 8
                    xg, Bxg = xbuf[G % 2]
                    if G + 1 < 16:
                        S.dma("sp", xbuf[(G + 1) % 2][0][:], xT_v[:, :, (G + 1) * 512:(G + 2) * 512],
                              writes=[xbuf[(G + 1) % 2][1]])
                    rms_bcast(xg[:], [Bxg], 512, sq, Bsq, pA, BpA, rs, Brs, rstd, Brstd)
                    TT("dve", hT[:], xg[:], rstd[:].unsqueeze(1).to_broadcast([128, 8, 512]), ALU.mult,
                       [Bxg, Brstd], [BhT])
                    if own or G == 7:
                        c0, n = (0, 512) if own else (384, 128)
                        qg = (G - 7) if own else 0
                        ucol = 30 + (128 + (G - 8) * 512 if own else 0)
                        for c in range(4):
                            for kc in range(8):
                                MM(pA[:, 0:n], w_in_bf[:, kc, c * 128:(c + 1) * 128], hT[:, kc, c0:c0 + n],
                                   kc == 0, kc == 7, Bwin + [BhT], [BpA])
                            for kc in range(8):
                                MM(pG[:, 0:n], w_in_bf[:, kc, 512 + c * 128:512 + (c + 1) * 128], hT[:, kc, c0:c0 + n],
                                   kc == 0, kc == 7, Bwin + [BhT], [BpG])
                            ACT(sig[:, 0:n], pG[:, 0:n], AF.Sigmoid, [BpG, Bc], [Bsig], bias=cc("bcin", 4 + c))
                            STT(uT[:, c, ucol:ucol + n], pA[:, 0:n], cc("bcin", c), sig[:, 0:n], ALU.add, ALU.mult,
                                [BpA, Bsig, Bc], [Buf_u[qg]])
                            if not own:
                                TS("dve", uT[:, c, ucol:ucol + n], uT[:, c, ucol:ucol + n], cc("flag"), None,
                                   ALU.mult, None, [Buf_u[qg], Bc], [Buf_u[qg]])
                    KTs, BKTs = KTst[0]
                    QTs, BQTs = QTst[0]
                    Vs, BVs = Vst[0]
                    def lat_chain(ti):
                        t = 4 * G + ti
                        hasq = own or t == 31
                        tsl = slice(ti * 128, (ti + 1) * 128)
                        l0 = 0 if hasq else 256
                        pLx, BpLx = pLs[ti % 2]
                        sqLx, BsqLx = sqLs[ti % 2]
                        st2x, Bst2x = st2s[ti % 2]
                        cnx, Bcnx = cns[ti % 2]
                        cTx, BcTx = cTs[ti % 2]
                        for kc in range(8):
                            MM(pLx[:, l0:416], hT[:, kc, tsl], w_in_bf[:, kc, 1024 + l0:1440], kc == 0, kc == 7,
                               Bwin + [BhT], [BpLx])
                        yield
                        if hasq:
                            ACT(sqLx[:, 0:256], pLx[:, 0:256], AF.Square, [BpLx], [BsqLx], scale=1.0 / 16)
                            yield
                        ACT(sqLx[:, 256:384], pLx[:, 256:384], AF.Square, [BpLx], [BsqLx], scale=128 ** -0.5)
                        yield
                        if hasq:
                            RED(st2x[:, 0:1], sqLx[:, 0:256], [BsqLx], [Bst2x])
                            yield
                        RED(st2x[:, 1:2], sqLx[:, 256:384], [BsqLx], [Bst2x])
                        yield
                        ACT(st2x[:, 2 + l0 // 256:4], st2x[:, l0 // 256:2], AF.Sqrt, [Bst2x, Bc], [Bst2x], bias=cc("eps"))
                        yield
                        RCP(st2x[:, 4 + l0 // 256:6], st2x[:, 2 + l0 // 256:4], [Bst2x], [Bst2x])
                        yield
                        if hasq:
                            STT(cnx[:, 0:256], pLx[:, 0:256], st2x[:, 4:5], cc("gql", 0, 256), ALU.mult, ALU.mult,
                                [BpLx, Bst2x, Bc], [Bcnx])
                            yield
                        STT(cnx[:, 256:384], pLx[:, 256:384], st2x[:, 5:6], cc("gkvl", 0, 128), ALU.mult, ALU.mult,
                            [BpLx, Bst2x, Bc], [Bcnx])
                        yield
                        for j in ((0, 1, 2) if hasq else (2,)):
                            TR(pT[:, j * 128:(j + 1) * 128], cnx[:, j * 128:(j + 1) * 128], ident[:], [Bcnx, Bid], [BpT])
                        yield
                        CP("act", cTx[:, l0:384], pT[:, l0:384], [BpT], [BcTx])
                        yield

                    def kq_mm(ti):
                        t = 4 * G + ti
                        p = ti % 2
                        hasq = own or t == 31
                        pLx, BpLx = pLs[p]
                        cTx, BcTx = cTs[p]
                        for (a, b) in ((0, 512), (512, 1024)):
                            MM(pKV[:, a:b], cTx[:, 256:384], w_ukv_bf[:, 0, a:b], True, True, Bwukv + [BcTx], [BpKV])
                        yield
                        CP("act", kvs[p][:, 0:1024], pKV[:], [BpKV], [Bkv_k[p], Bkv_v[p]])
                        yield
                        CP("dve", kvs[p][:, 1024:1056], pLx[:, 384:416], [BpLx], [Bkv_r[p]])
                        yield
                        if hasq:
                            for (a, b) in ((0, 512), (512, 768)):
                                for kc in range(2):
                                    MM(pQ[:, a:b], cTx[:, kc * 128:(kc + 1) * 128], w_uq_bf[:, kc, a:b], kc == 0, kc == 1,
                                       Bwuq + [BcTx], [BpQ])
                            yield
                            CP("dve", qs[p][:], pQ[:, 0:768], [BpQ], [Bqs[p]])
                            yield

                    def k_chain(ti):
                        t = 4 * G + ti
                        p = ti % 2
                        tsl = slice(ti * 128, (ti + 1) * 128)
                        kvv = kvs[p][:, 0:1024].rearrange("p (h e) -> p h e", h=8)
                        krr = kvs[p][:, 1024:1056]
                        stkx, Bstkx = stks[p]
                        krx, Bkrx = krs[p]
                        kbfx, Bkbfx = kbfs[p]
                        TT("pool", krx[:, 0, :], krr, cc("gkn", 64, 32), ALU.mult, [Bkv_r[p], Bc], [Bkrx])
                        yield
                        ACT(sqk[:], kvv[:, :, 0:64], AF.Square, [Bkv_k[p]], [Bsqk], scale=R96)
                        yield
                        ACT(sqr[:], krr, AF.Square, [Bkv_r[p]], [Bsqr], scale=R96)
                        yield
                        TT("pool", krx[:, 1, :], krx[:, 0, :], cos2[:, t, :], ALU.mult, [Bkrx, Btab], [Bkrx])
                        yield
                        RED(stkx[:, 0:8], sqk[:], [Bsqk], [Bstkx])
                        yield
                        RED(stkx[:, 8:9], sqr[:], [Bsqr], [Bstkx])
                        yield
                        TT("pool", krx[:, 2, 0:16], krx[:, 0, 16:32], sin2[:, t, 0:16], ALU.mult, [Bkrx, Btab], [Bkrx])
                        yield
                        TS("dve", stkx[:, 9:10], stkx[:, 8:9], EPS, None, ALU.add, None, [Bstkx], [Bstkx])
                        yield
                        CP("act", Vs[:, 0:8:2, ti, 0:64], kvv[:, 0:8:2, 64:128], [Bkv_v[p]], [BVs])
                        yield
                        TT("pool", krx[:, 2, 16:32], krx[:, 0, 0:16], sin2[:, t, 16:32], ALU.mult, [Bkrx, Btab], [Bkrx])
                        yield
                        ACT(stkx[:, 16:24], stkx[:, 0:8], AF.Sqrt, [Bstkx], [Bstkx], bias=stkx[:, 9:10])
                        yield
                        TT("pool", krx[:, 3, :], krx[:, 1, :], krx[:, 2, :], ALU.add, [Bkrx], [Bkrx])
                        yield
                        RCP(stkx[:, 24:32], stkx[:, 16:24], [Bstkx], [Bstkx])
                        yield
                        CP("act", Vs[:, 1:8:2, ti, 64:128], kvv[:, 1:8:2, 64:128], [Bkv_v[p]], [BVs])
                        yield
                        rk = stkx[:, 24:32]
                        TT("dve", kvv[:, :, 0:64], kvv[:, :, 0:64], rk.unsqueeze(2).to_broadcast([128, 8, 64]), ALU.mult,
                           [Bkv_k[p], Bstkx], [Bkv_k[p]])
                        yield
                        TT("dve", kbfx[:, :, 64:96], krx[:, 3:4, :].to_broadcast([128, 8, 32]),
                           rk.unsqueeze(2).to_broadcast([128, 8, 32]), ALU.mult, [Bkrx, Bstkx], [Bkbfx])
                        yield
                        TT("pool", kbfx[:, :, 0:64], kvv[:, :, 0:64], cc("gkn", 0, 64).unsqueeze(1).to_broadcast([128, 8, 64]),
                           ALU.mult, [Bkv_k[p], Bc], [Bkbfx])
                        yield
                        for h in range(8):
                            TR(pTk[0:96, h * 128:(h + 1) * 128], kbfx[:, h, :], ident[:], [Bkbfx, Bid], [BpTk])
                        yield
                        CP("act", KTs[:, :, tsl], pTk[0:96, :].rearrange("p (h t) -> p h t", h=8), [BpTk], [BKTs])
                        yield

                    def q_chain(ti):
                        t = 4 * G + ti
                        p = ti % 2
                        tsl = slice(ti * 128, (ti + 1) * 128)
                        stqx, Bstqx = stqs[p]
                        rt1x, rt2x, Brtx = rts[p]
                        qbfx, Bqbfx = qbfs[p]
                        qv = qs[p][:].rearrange("p (h e) -> p h e", h=8)
                        ACT(sqq[:], qs[p][:], AF.Square, [Bqs[p]], [Bsqq], scale=R96)
                        yield
                        RED(stqx[:, 0:8], sqq[:].rearrange("p (h e) -> p h e", h=8), [Bsqq], [Bstqx])
                        yield
                        ACT(stqx[:, 8:16], stqx[:, 0:8], AF.Sqrt, [Bstqx, Bc], [Bstqx], bias=cc("eps"))
                        yield
                        RCP(stqx[:, 16:24], stqx[:, 8:16], [Bstqx], [Bstqx])
                        yield
                        TT("dve", qv, qv, stqx[:, 16:24].unsqueeze(2).to_broadcast([128, 8, 96]), ALU.mult,
                           [Bqs[p], Bstqx], [Bqs[p]])
                        yield
                        TT("pool", qv, qv, cc("gqn", 0, 96).unsqueeze(1).to_broadcast([128, 8, 96]), ALU.mult,
                           [Bqs[p], Bc], [Bqs[p]])
                        yield
                        TT("pool", rt1x[:], qv[:, :, 64:96], cos2[:, t:t + 1, :].to_broadcast([128, 8, 32]), ALU.mult,
                           [Bqs[p], Btab], [Brtx])
                        yield
                        TT("dve", rt2x[:, :, 0:16], qv[:, :, 80:96], sin2[:, t:t + 1, 0:16].to_broadcast([128, 8, 16]),
                           ALU.mult, [Bqs[p], Btab, Brtx], [Brtx])
                        yield
                        TT("pool", rt2x[:, :, 16:32], qv[:, :, 64:80], sin2[:, t:t + 1, 16:32].to_broadcast([128, 8, 16]),
                           ALU.mult, [Bqs[p], Btab, Brtx], [Brtx])
                        yield
                        CP("act", qbfx[:, :, 0:64], qv[:, :, 0:64], [Bqs[p]], [Bqbfx])
                        yield
                        TT("dve", qbfx[:, :, 64:96], rt1x[:], rt2x[:], ALU.add, [Brtx], [Bqbfx])
                        yield
                        for h in range(8):
                            TR(pT[0:96, h * 128:(h + 1) * 128], qbfx[:, h, :], ident[:], [Bqbfx, Bid], [BpT])
                        yield
                        qsl = tsl if own else slice(0, 128)
                        CP("dve", QTs[:, :, qsl], pT[0:96, :].rearrange("p (h t) -> p h t", h=8), [BpT], [BQTs])
                        yield
                        if not own:
                            S.dma("sp", QT_v[:, :, 0:128], QTs[:, :, 0:128], reads=[BQTs], writes=[B_QTs[0]])

                    def run_dyn():
                        done = set()
                        active = []
                        pending = []

                        def wrap(name, gen):
                            yield from gen
                            done.add(name)

                        for ti_ in range(4):
                            hq_ = own or (4 * G + ti_) == 31
                            prev2 = set()
                            if ti_ >= 2:
                                prev2.add(("k", ti_ - 2))
                                if own or (4 * G + ti_ - 2) == 31:
                                    prev2.add(("q", ti_ - 2))
                            lat_need = set()
                            if ti_ >= 1:
                                lat_need.add(("lat", ti_ - 1))
                            if ti_ >= 2:
                                lat_need.add(("kq", ti_ - 2))
                            pending.append((lat_need, ("lat", ti_), lat_chain(ti_)))
                            pending.append(({("lat", ti_)} | prev2, ("kq", ti_), kq_mm(ti_)))
                            pending.append(({("kq", ti_)}, ("k", ti_), k_chain(ti_)))
                            if hq_:
                                pending.append(({("kq", ti_)}, ("q", ti_), q_chain(ti_)))
                        while active or pending:
                            for it in list(pending):
                                if it[0] <= done:
                                    pending.remove(it)
                                    active.append(wrap(it[1], it[2]))
                            for g_ in list(active):
                                try:
                                    next(g_)
                                except StopIteration:
                                    active.remove(g_)

                    run_dyn()
                    S.dma("sp", KT_v[:, :, G * 512:(G + 1) * 512], KTs[:], reads=[BKTs], writes=[B_KTs[G]])
                    S.dma("sp", V_v[:, :, 4 * G:4 * G + 4, :], Vs[:], reads=[BVs], writes=[B_Vs[G]])
                    if own:
                        q0 = 128 + (G - 8) * 512
                        S.dma("sp", QT_v[:, :, q0:q0 + 512], QTs[:], reads=[BQTs], writes=[B_QTs[G - 7]])
                S.barrier()
        if dbg:
            S.dma("sp", dbg_out["uT"], uT[:], reads=Buf_u + [Bupad])
        attnT = esX.enter_context(nc.sbuf_tensor("attnT", [128, 4, NQ], BF16))
        memK = esX.enter_context(nc.sbuf_tensor("memK", [128, 8, 256], BF16))
        memV = esX.enter_context(nc.sbuf_tensor("memV", [128, 2, 1024], BF16))
        BmK, BmV = Buf("memK"), Buf("memV")
        if True:

            if stop_after >= 2:
                with contextlib.ExitStack() as esB:
                    sb, ps = mk(esB)
                    Dg = sb("Dg", [128, 4, 31, 128], BF16)
                    BDg = [Buf("Dg%d" % c) for c in range(4)]
                    for c in range(4):
                        for k in range(31):
                            TS(("dve", "pool")[k % 2], Dg[:, c, k, :], ident[:], cc("wdw", c * 31 + k), None, ALU.mult, None,
                               [Bid, Bc], [BDg[c]])
                    pY = [(ps("pY%d" % i, [128, 512]), Buf("pY%d" % i)) for i in range(2)]
                    pM = ps("pM", [128, 512])
                    pM2 = ps("pM2", [128, 512])
                    BpM, BpM2 = Buf("pM"), Buf("pM2")
                    ybf = sb("ybf", [128, 4, 512], BF16)
                    ysq = sb("ysq", [128, 4, 512], BF16)
                    Bybf = [Buf("ybf%d" % c) for c in range(4)]
                    Bysq = [Buf("ysq%d" % c) for c in range(4)]
                    mean = sb("mean", [128, 512], F32)
                    m2 = sb("m2", [128, 512], F32)
                    var = sb("var", [128, 512], F32)
                    rsd = sb("rsd", [128, 512], F32)
                    Bmean, Bm2, Bvar, Brsd = Buf("mean"), Buf("m2"), Buf("var"), Buf("rsd")
                    tmp = [(sb("lt%d" % i, [128, 512], F32), Buf("lt%d" % i)) for i in range(2)]
                    w_mkv_bf = sb("w_mkv_bf", [128, 8, 2 * D], BF16)
                    stgM = [(sb("stgM%d" % i, [128, 1024], F32), Buf("stgM%d" % i)) for i in range(3)]
                    pzm = [(ps("pzm%d" % i, [128, 512]), Buf("pz_m%d" % i)) for i in range(4)]
                    msq = sb("sqm", [128, 8, 256], BF16)
                    Bmsq = Buf("sqm")
                    mrs = sb("rsm", [128, 512], F32)
                    Bmrs = Buf("rsm")
                    mrstd = sb("rstdm", [128, 512], F32)
                    Bmrstd = Buf("rstdm")
                    mx = sb("mx", [128, 8, 256], F32)
                    Bmx = Buf("mx")
                    hm = sb("hm", [128, 8, 256], BF16)
                    Bhm = Buf("hm")
                    ksq = sb("ksq", [128, 2, 256], BF16)
                    Bksq = Buf("ksq")
                    Bwmkv = []
                    mkv_pieces = [(kc, c0) for kc in range(8) for c0 in (0, 1024)]

                    def load_mkv_piece():
                        if not mkv_pieces:
                            return
                        kc, c0 = mkv_pieces.pop(0)
                        i = len(mkv_pieces)
                        st, Bst = stgM[i % 3]
                        S.dma("sp", st[:], w_mkv[kc * 128:(kc + 1) * 128, c0:c0 + 1024], writes=[Bst])
                        b_ = Buf("wmkv")
                        Bwmkv.append(b_)
                        TS(("dve", "pool")[i % 2], w_mkv_bf[:, kc, c0:c0 + 1024], st[:], cc("memmg", kc), None, ALU.mult, None,
                           [Bst, Bc], [b_])

                    S.dma("sp", mx[:], memT.rearrange("(kc p) t -> p kc t", p=128), writes=[Bmx])
                    yi = 0
                    for g in reversed(range(9)):
                        n = 128 if g == 0 else 512
                        oc = 0 if g == 0 else 128 + (g - 1) * 512
                        load_mkv_piece()
                        load_mkv_piece()
                        rd = [Bupad] + Buf_u[max(0, g - 1):g + 1]
                        for c in range(4):
                            py, Bpy = pY[yi % 2]
                            yi += 1
                            for k in range(31):
                                MM(py[:, 0:n], Dg[:, c, k, :], uT[:, c, oc + k:oc + k + n], k == 0, k == 30, rd + [BDg[c]], [Bpy])
                            ACT(ybf[:, c, 0:n], py[:, 0:n], AF.Identity, [Bpy, Bc], [Bybf[c]], bias=cc("bdw", c))
                            ACT(ysq[:, c, 0:n], py[:, 0:n], AF.Square, [Bpy, Bc], [Bysq[c]], bias=cc("bdw", c))
                        for c in range(4):
                            MM(pM[:, 0:n], ones_bf[:], ybf[:, c, 0:n], c == 0, c == 3, [Bob, Bybf[c]], [BpM])
                        for c in range(4):
                            MM(pM2[:, 0:n], ones_bf[:], ysq[:, c, 0:n], c == 0, c == 3, [Bob, Bysq[c]], [BpM2])
                        TS("dve", mean[:, 0:n], pM[:, 0:n], 1.0 / 512, None, ALU.mult, None, [BpM], [Bmean])
                        TT("dve", m2[:, 0:n], mean[:, 0:n], mean[:, 0:n], ALU.mult, [Bmean], [Bm2])
                        STT(var[:, 0:n], pM2[:, 0:n], 1.0 / 512, m2[:, 0:n], ALU.mult, ALU.subtract, [BpM2, Bm2], [Bvar])
                        ACT(var[:, 0:n], var[:, 0:n], AF.Sqrt, [Bvar, Bc], [Bvar], bias=cc("eps"))
                        RCP(rsd[:, 0:n], var[:, 0:n], [Bvar], [Brsd])
                        for c in range(4):
                            tm, Btm = tmp[c % 2]
                            TT("dve", tm[:, 0:n], ybf[:, c, 0:n], mean[:, 0:n], ALU.subtract, [Bybf[c], Bmean], [Btm])
                            TT("pool", tm[:, 0:n], tm[:, 0:n], rsd[:, 0:n], ALU.mult, [Btm, Brsd], [Btm])
                            ACT(uT[:, c, 30 + oc:30 + oc + n], tm[:, 0:n], AF.Silu, [Btm, Bc], [Buf_u[g]],
                                scale=cc("lng", c), bias=cc("lnb", c))
                    if stop_after >= 3:
                        while mkv_pieces:
                            load_mkv_piece()
                        rms_bcast(mx[:], [Bmx], 256, msq, Bmsq, pzm[0][0], pzm[0][1], mrs, Bmrs, mrstd, Bmrstd)
                        TT("dve", hm[:], mx[:], mrstd[:, 0:256].unsqueeze(1).to_broadcast([128, 8, 256]), ALU.mult,
                           [Bmx, Bmrstd], [Bhm])
                        for hd in range(4):
                            for j in range(2):
                                ch = hd * 2 + j
                                p_, Bp_ = pzm[1 + j]
                                for kc in range(8):
                                    MM(p_[:, 0:256], w_mkv_bf[:, kc, ch * 128:(ch + 1) * 128], hm[:, kc, :], kc == 0, kc == 7,
                                       Bwmkv + [Bhm], [Bp_])
                                ACT(ksq[:, j, :], p_[:, 0:256], AF.Square, [Bp_], [Bksq], scale=1.0 / 16)
                            for j in range(2):
                                MM(pzm[3][0][:, 0:256], ones_bf[:], ksq[:, j, :], j == 0, j == 1, [Bob, Bksq], [pzm[3][1]])
                            ACT(mrs[:, 0:256], pzm[3][0][:, 0:256], AF.Sqrt, [pzm[3][1], Bc], [Bmrs], bias=cc("eps"))
                            RCP(mrstd[:, 0:256], mrs[:, 0:256], [Bmrs], [Bmrstd])
                            for j in range(2):
                                STT(memK[:, hd * 2 + j, :], pzm[1 + j][0][:, 0:256], cc("mkg", j), mrstd[:, 0:256], ALU.mult, ALU.mult,
                                    [pzm[1 + j][1], Bmrstd, Bc], [BmK])
                        for kt in range(2):
                            for nb in range(2):
                                p_, Bp_ = pzm[(kt * 2 + nb) % 2 + 1]
                                for kc in range(8):
                                    MM(p_[:], hm[:, kc, kt * 128:(kt + 1) * 128], w_mkv_bf[:, kc, 1024 + nb * 512:1024 + (nb + 1) * 512],
                                       kc == 0, kc == 7, Bwmkv + [Bhm], [Bp_])
                                CP("act", memV[:, kt, nb * 512:(nb + 1) * 512], p_[:], [Bp_], [BmV])
                    S.barrier()
                if dbg:
                    S.dma("sp", dbg_out["ufT"], uT[:, :, 30:30 + NQ], reads=Buf_u)

        w_out_bf = esX.enter_context(nc.sbuf_tensor("w_out_bf", [128, 8, D], BF16))
        w_mq_bf = esX.enter_context(nc.sbuf_tensor("w_mq_bf", [128, 8, D], BF16))
        w_mo_bf = esX.enter_context(nc.sbuf_tensor("w_mo_bf", [128, 8, D], BF16))
        stgP = (esX.enter_context(nc.sbuf_tensor("stgP", [128, 512], F32)), Buf("stgP"))
        if stop_after >= 3:
            with contextlib.ExitStack() as es2:
                sb, ps = mk(es2)
                kT = [(sb("kT%d" % i, [96, SEQ], BF16), Buf("kT%d" % i)) for i in range(2)]
                vA = [(sb("vA%d" % i, [128, 64, 128], BF16), Buf("vA%d" % i)) for i in range(2)]
                qT = [(sb("qT%d" % i, [96, NQ], BF16), Buf("qT%d" % i)) for i in range(1)] * 2
                NP = 3
                Pb = [(sb("P%d" % i, [128, 512], BF16), Buf("P%d" % i)) for i in range(NP)]
                pS = [(ps("pS%d" % i, [128, 512]), Buf("pS%d" % i)) for i in range(4)]
                pO = [(ps("pO%d" % i, [128, 512]), Buf("pO%d" % i)) for i in range(2)]
                pB = ps("pB", [128, 512])
                BpB = Buf("pB")
                rrow = sb("rrow", [128, 512], F32)
                Brrow = Buf("rrow")
                bcs, Bbcs = rrow, Brrow
                Bscr = Buf("scr")
                SC = 96 ** -0.5

                def load_head(h):
                    hb = h % 2
                    S.dma("sp", kT[hb][0][:], KT_scr[h], reads=B_KTs, writes=[kT[hb][1]])
                    S.dma("sp", vA[hb][0][:], V_scr[h].rearrange("p (t e) -> p t e", e=128), reads=B_Vs, writes=[vA[hb][1]])

                def load_q(h):
                    S.dma("sp", qT[0][0][:], QT_scr[h], reads=B_QTs, writes=[qT[0][1]])

                load_head(0)
                load_q(0)
                sidx = 0
                oidx = 0
                Bwo, Bwmq, Bwmo = [], [], []
                pf = []
                for (dst_, src_, gain_, lst_) in ((w_out_bf, w_out, None, Bwo), (w_mq_bf, w_mq, "memxg", Bwmq),
                                                   (w_mo_bf, w_mo, None, Bwmo)):
                    for kc_ in range(8):
                        for hc_ in range(2):
                            pf.append((dst_, src_, gain_, lst_, kc_, hc_))

                def prefetch_piece():
                    if not pf:
                        return
                    dst_, src_, gain_, lst_, kc_, hc_ = pf.pop(0)
                    st_, Bst_ = stgP
                    cs_ = slice(hc_ * 512, (hc_ + 1) * 512)
                    S.dma("sp", st_[:], src_[kc_ * 128:(kc_ + 1) * 128, cs_], writes=[Bst_])
                    b_ = Buf("wpf")
                    lst_.append(b_)
                    eng_ = "pool" if len(pf) % 2 else "dve"
                    if gain_ is not None:
                        TS(eng_, dst_[:, kc_, cs_], st_[:], cc(gain_, kc_), None, ALU.mult, None, [Bst_, Bc], [b_])
                    else:
                        CP(eng_, dst_[:, kc_, cs_], st_[:], [Bst_], [b_])
                for h in range(8):
                    hb = h % 2
                    if h + 1 < 8:
                        load_head(h + 1)
                    k_t, Bk = kT[hb]
                    v_t, Bv = vA[hb]
                    q_t, Bq = qT[hb]
                    sr = 64 if h % 2 == 0 else 0
                    orow = 0 if h % 2 == 0 else 64
                    for g in range(9):
                        if g == 0:
                            N, q0, nk = 128, 0, 32
                            steps = [(t, 0, False) for t in range(31)] + [(31, 0, True)]
                        else:
                            N, q0, nk = 512, 128 + 512 * (g - 1), 32 + 4 * g
                            steps = [(t, 0, False) for t in range(nk - 4)] + [(nk - 4 + d, 128 * d, True) for d in range(4)]
                        po, Bpo = pO[oidx % 2]
                        oidx += 1
                        ns = len(steps)
                        prefetch_piece()

                        def emit_S(i):
                            t, cs, dg = steps[i]
                            p_s, Bps = pS[(sidx + i) % 4]
                            MM(p_s[:, cs:N], k_t[:, t * 128:(t + 1) * 128], q_t[:, q0 + cs:q0 + N], True, True, [Bk, Bq], [Bps])

                        for i in range(min(3, ns)):
                            emit_S(i)
                        for i in range(ns):
                            t, cs, dg = steps[i]
                            p_s, Bps = pS[(sidx + i) % 4]
                            pb, Bpb = Pb[(sidx + i) % NP]
                            ACT(pb[:, cs:N], p_s[:, cs:N], AF.Exp, [Bps, Bc], [Bpb], scale=SC, bias=cc("kbias", t))
                            if dg:
                                MSET("pool", pb[64:128, cs:cs + 64], 0.0, [Bpb])
                            MM(po[:, cs:N], v_t[:, t, :], pb[:, cs:N], i == 0, i == ns - 1, [Bv, Bpb], [Bpo])
                            if i + 3 < ns:
                                emit_S(i + 3)
                        sidx += ns
                        if g == 0:
                            TS("dve", rrow[sr:sr + 1, 0:N], po[sr:sr + 1, 0:N], 1e-30, None, ALU.max, None, [Bpo], [Brrow])
                            RCP(rrow[sr:sr + 1, 0:N], rrow[sr:sr + 1, 0:N], [Brrow], [Brrow])
                        else:
                            RCP(rrow[sr:sr + 1, 0:N], po[sr:sr + 1, 0:N], [Bpo], [Brrow])
                        MM(pB[:, 0:N], ones_f[sr:sr + 1, :], rrow[sr:sr + 1, 0:N], True, True, [Bof, Brrow], [BpB])
                        CP("dve", bcs[orow:orow + 64, 0:N], pB[orow:orow + 64, 0:N], [BpB], [Bbcs])
                        TT("dve", attnT[orow:orow + 64, h // 2, q0:q0 + N], po[orow:orow + 64, 0:N], bcs[orow:orow + 64, 0:N],
                           ALU.mult, [Bpo, Bbcs], [Buf_at[g]])
                    if h + 1 < 8:
                        load_q(h + 1)
                S.barrier()
            if dbg:
                S.dma("sp", dbg_out["attnT"], attnT[:], reads=Buf_at)

        if stop_after >= 4:
            with contextlib.ExitStack() as es3:
                sb, ps = mk(es3)
                pz = [(ps("pz%d" % i, [128, 512]), Buf("pz%d" % i)) for i in range(8)]
                hq = sb("hq", [128, 8, 512], BF16)
                Bhq = Buf("hq")
                sq, Bsq = hq, Bhq
                rs = sb("rs3", [128, 512], F32)
                Brs = Buf("rs3")
                rstd = sb("rstd3", [128, 512], F32)
                Brstd = Buf("rstd3")
                xb3 = [(sb("x3_%d" % i, [128, 8, 512], F32), Buf("x3_%d" % i)) for i in range(2)]
                qraws = [(sb("qraw%d" % i, [128, 2, 512], BF16), Buf("qraw%d" % i)) for i in range(2)]
                qsqs = [(sb("qsq%d" % i, [128, 2, 512], BF16), Buf("qsq%d" % i)) for i in range(2)]
                qmns = [(sb("qmn%d" % i, [128, 2, 512], BF16), Buf("qmn%d" % i)) for i in range(2)]
                pms = [[(sb("pm%d_%d" % (l_, i), [128, 512], BF16), Buf("pm%d_%d" % (l_, i))) for i in range(2)] for l_ in range(2)]
                rss = [(sb("rsx%d" % i, [128, 512], F32), Buf("rsx%d" % i)) for i in range(2)]
                rcss = [(sb("rcs%d" % i, [128, 512], F32), Buf("rcs%d" % i)) for i in range(2)]
                om = sb("om", [128, 8, 512], BF16)
                Bom = [Buf("om%d" % i) for i in range(4)]

                def xcols(g):
                    return (3968, 128) if g == 0 else (4096 + (g - 1) * 512, 512)

                c_, n_ = xcols(0)
                S.dma("sp", xb3[0][0][:, :, 0:n_], xT_v[:, :, c_:c_ + n_], writes=[xb3[0][1]])
                for g in range(9):
                    n = 128 if g == 0 else 512
                    oc = 0 if g == 0 else 128 + (g - 1) * 512
                    xg, Bxg = xb3[g % 2]
                    if g + 1 < 9:
                        c_, n_ = xcols(g + 1)
                        S.dma("sp", xb3[(g + 1) % 2][0][:, :, 0:n_], xT_v[:, :, c_:c_ + n_], writes=[xb3[(g + 1) % 2][1]])
                    for dc in range(8):
                        p_, Bp_ = pz[dc % 2]
                        for kc in range(8):
                            if kc < 4:
                                rhs_, bsrc = uT[:, kc, 30 + oc:30 + oc + n], Buf_u[g]
                            else:
                                rhs_, bsrc = attnT[:, kc - 4, oc:oc + n], Buf_at[g]
                            MM(p_[:, 0:n], w_out_bf[:, kc, dc * 128:(dc + 1) * 128], rhs_, kc == 0, kc == 7,
                               Bwo + [bsrc], [Bp_])
                        TT("dve", xg[:, dc, 0:n], xg[:, dc, 0:n], p_[:, 0:n], ALU.add, [Bxg, Bp_], [Bxg])
                    rms_bcast(xg[:, :, 0:n], [Bxg], n, sq, Bsq, pz[0][0], pz[0][1], rs, Brs, rstd, Brstd)
                    TT("dve", hq[:, :, 0:n], xg[:, :, 0:n], rstd[:, 0:n].unsqueeze(1).to_broadcast([128, 8, n]), ALU.mult,
                       [Bxg, Brstd], [Bhq])
                    def head_chain(hd):
                        L = hd % 2
                        pq_, Bpq_ = pz[2 + 3 * L]
                        pn_, Bpn_ = pz[3 + 3 * L]
                        psc, Bpsc = pz[4 + 3 * L]
                        qraw, Bqraw = qraws[L]
                        qsq, Bqsq = qsqs[L]
                        qmn, Bqmn = qmns[L]
                        rsx, Brsx = rss[L]
                        rcs, Brcs = rcss[L]
                        for j in range(2):
                            ch = hd * 2 + j
                            for kc in range(8):
                                MM(pq_[:, 0:n], w_mq_bf[:, kc, ch * 128:(ch + 1) * 128], hq[:, kc, 0:n], kc == 0, kc == 7,
                                   Bwmq + [Bhq], [Bpq_])
                            yield
                            ACT(qsq[:, j, 0:n], pq_[:, 0:n], AF.Square, [Bpq_], [Bqsq], scale=1.0 / 16)
                            yield
                            CP("dve", qraw[:, j, 0:n], pq_[:, 0:n], [Bpq_], [Bqraw])
                            yield
                        for j in range(2):
                            MM(pn_[:, 0:n], ones_bf[:], qsq[:, j, 0:n], j == 0, j == 1, [Bob, Bqsq], [Bpn_])
                        yield
                        ACT(rsx[:, 0:n], pn_[:, 0:n], AF.Sqrt, [Bpn_, Bc], [Brsx], bias=cc("eps"))
                        yield
                        RCP(rsx[:, 0:n], rsx[:, 0:n], [Brsx], [Brsx])
                        yield
                        for j in range(2):
                            STT(qmn[:, j, 0:n], qraw[:, j, 0:n], cc("mqg", j), rsx[:, 0:n], ALU.mult, ALU.mult,
                                [Bqraw, Brsx, Bc], [Bqmn])
                            yield
                        for kt in range(2):
                            for j in range(2):
                                MM(psc[:, 0:n], memK[:, hd * 2 + j, kt * 128:(kt + 1) * 128], qmn[:, j, 0:n], j == 0, j == 1,
                                   [BmK, Bqmn], [Bpsc])
                            yield
                            ACT(pms[L][kt][0][:, 0:n], psc[:, 0:n], AF.Exp, [Bpsc], [pms[L][kt][1]], scale=1.0 / 16)
                            yield
                        for kt in range(2):
                            MM(pn_[:, 0:n], ones_bf[:], pms[L][kt][0][:, 0:n], kt == 0, kt == 1, [Bob, pms[L][kt][1]], [Bpn_])
                        yield
                        RCP(rcs[:, 0:n], pn_[:, 0:n], [Bpn_], [Brcs])
                        yield
                        for j in range(2):
                            for kt in range(2):
                                MM(pq_[:, 0:n], memV[:, kt, (hd * 2 + j) * 128:(hd * 2 + j + 1) * 128], pms[L][kt][0][:, 0:n],
                                   kt == 0, kt == 1, [BmV, pms[L][kt][1]], [Bpq_])
                            yield
                            TT("dve", om[:, hd * 2 + j, 0:n], pq_[:, 0:n], rcs[:, 0:n], ALU.mult, [Bpq_, Brcs], [Bom[hd]])
                            yield

                    def rr3(*gens):
                        gens = list(gens)
                        while gens:
                            for g_ in list(gens):
                                try:
                                    next(g_)
                                except StopIteration:
                                    gens.remove(g_)

                    rr3(head_chain(0), head_chain(1))
                    rr3(head_chain(2), head_chain(3))
                    for dc in range(8):
                        p_, Bp_ = pz[dc % 2]
                        for kc in range(8):
                            MM(p_[:, 0:n], w_mo_bf[:, kc, dc * 128:(dc + 1) * 128], om[:, kc, 0:n], kc == 0, kc == 7,
                               Bwmo + [Bom[kc // 2]], [Bp_])
                        TT("dve", xg[:, dc, 0:n], xg[:, dc, 0:n], p_[:, 0:n], ALU.add, [Bxg, Bp_], [Bxg])
                    S.dma("sp", x2_v[:, :, oc:oc + n], xg[:, :, 0:n], reads=[Bxg], writes=[B_x2[g]])
                S.barrier()
        esX.close()

        if stop_after >= 5:
            with contextlib.ExitStack() as es4:
                sb, ps = mk(es4)
                w_up_bf = sb("w_up_bf", [128, 8, 2 * DFF], BF16)
                w_dn_bf = sb("w_dn_bf", [128, 22, D], BF16)
                stg4 = [(sb("stg4_%d" % i, [128, 512], F32), Buf("stg4_%d" % i)) for i in range(3)]
                x4 = sb("x4", [128, 8, 512], F32)
                Bx4 = Buf("x4")
                x4h = sb("x4h", [128, 8, 2], F32)
                Bx4h = Buf("x4h")
                h3 = sb("h3", [128, 8, 512], BF16)
                Bh3 = Buf("h3")
                h3h = sb("h3h", [128, 8, 2], BF16)
                Bh3h = Buf("h3h")
                sq, Bsq = h3, Bh3
                rs = sb("rs4", [128, 512], F32)
                Brs = Buf("rs4")
                rstd, Brstd = rs, Brs
                prev = sb("prev", [128, 44, 2], F32)
                Bprev = [Buf("prev%d" % r) for r in range(44)]
                upx = [(sb("upx%d" % i, [128, 514], F32), Buf("upx%d" % i)) for i in range(2)]
                yv = [(sb("yv%d" % i, [128, 512], F32), Buf("yv%d" % i)) for i in range(3)]
                actT = sb("actT", [128, 22, 512], BF16)
                Bact = [Buf("act%d" % j) for j in range(22)]
                ost = [(sb("ost%d" % i, [128, 512], F32), Buf("ost%d" % i)) for i in range(2)]
                pu = [(ps("pu%d" % i, [128, 512]), Buf("pu%d" % i)) for i in range(4)]
                pd = [(ps("pd%d" % i, [128, 512]), Buf("pd%d" % i)) for i in range(2)]
                pss = ps("pss", [128, 512])
                Bpss = Buf("pss")
                Bup = {}
                Bdn = {}
                s4 = [0]

                def load_up_block(c0_, c1_):
                    for c0 in range(c0_, c1_, 512):
                        _load_up_piece(c0, min(c1_, c0 + 512))

                def _load_up_piece(c0, c1):
                    for kc in range(8):
                        i = s4[0]
                        s4[0] += 1
                        st, Bst = stg4[i % 3]
                        S.dma("sp", st[:, 0:c1 - c0], w_up[kc * 128:(kc + 1) * 128, c0:c1], writes=[Bst])
                        b_ = Buf("wup")
                        eng = ("pool", "dve", "act")[i % 3]
                        o = w_up_bf[:, kc, c0:c1]
                        if eng == "act":
                            ACT(o, st[:, 0:c1 - c0], AF.Copy, [Bst, Bc], [b_], scale=cc("ffng", kc))
                        else:
                            TS(eng, o, st[:, 0:c1 - c0], cc("ffng", kc), None, ALU.mult, None, [Bst, Bc], [b_])
                        for r in range(c0 // 128, c1 // 128):
                            Bup.setdefault(r, []).append(b_)

                def load_dn(kc):
                    Bdn[kc] = []
                    for hc in range(2):
                        i = s4[0]
                        s4[0] += 1
                        st, Bst = stg4[i % 3]
                        S.dma("sp", st[:], w_dn[kc * 128:(kc + 1) * 128, hc * 512:(hc + 1) * 512], writes=[Bst])
                        b_ = Buf("wdn")
                        CP(("pool", "dve", "act")[i % 3], w_dn_bf[:, kc, hc * 512:(hc + 1) * 512], st[:], [Bst], [b_])
                        Bdn[kc].append(b_)

                S.dma("sp", x4h[:], x2_v[:, :, 126:128], reads=[B_x2[0]], writes=[Bx4h])
                S.dma("sp", x4[:], x2_v[:, :, 128:640], reads=[B_x2[1]], writes=[Bx4])
                load_up_block(0, 1024)
                load_up_block(2816, 3840)
                rms_bcast(x4h[:], [Bx4h], 2, h3h, Bh3h, pss, Bpss, rs, Brs, rstd, Brstd)
                TT("dve", h3h[:], x4h[:], rstd[:, 0:2].unsqueeze(1).to_broadcast([128, 8, 2]), ALU.mult,
                   [Bx4h, Brstd], [Bh3h])
                ui = 0
                dn_next = [0]
                for g in range(8):
                    oc = 128 + g * 512
                    if g > 0:
                        S.dma("sp", x4[:], x2_v[:, :, oc:oc + 512], reads=[B_x2[g + 1]], writes=[Bx4])
                    rms_bcast(x4[:], [Bx4], 512, sq, Bsq, pss, Bpss, rs, Brs, rstd, Brstd)
                    TT("dve", h3[:], x4[:], rstd[:].unsqueeze(1).to_broadcast([128, 8, 512]), ALU.mult, [Bx4, Brstd], [Bh3])
                    for j in range(22):
                        if g == 0:
                            if j == 0:
                                load_up_block(1024, 2048)
                                load_up_block(3840, 4864)
                            if j == 6:
                                load_up_block(2048, 2816)
                                load_up_block(4864, 5632)
                            if j >= 10:
                                for _ in range(2):
                                    if dn_next[0] < 22:
                                        load_dn(dn_next[0])
                                        dn_next[0] += 1
                        ys = []
                        for r in (j, 22 + j):
                            p_, Bp_ = pu[ui % 4]
                            ux, Bux = upx[ui % 2]
                            y_, By_ = yv[ui % 3]
                            ui += 1
                            if g == 0:
                                ph, Bph = pu[ui % 4]
                                for kc in range(8):
                                    MM(ph[:, 0:2], w_up_bf[:, kc, r * 128:(r + 1) * 128], h3h[:, kc, :], kc == 0, kc == 7,
                                       Bup[r] + [Bh3h], [Bph])
                                TS("dve", prev[:, r, :], ph[:, 0:2], cc("flag"), None, ALU.mult, None, [Bph, Bc], [Bprev[r]])
                            for kc in range(8):
                                MM(p_[:], w_up_bf[:, kc, r * 128:(r + 1) * 128], h3[:, kc, :], kc == 0, kc == 7, Bup[r] + [Bh3], [Bp_])
                            CP("act", ux[:, 2:514], p_[:], [Bp_], [Bux])
                            ACT(y_[:], p_[:], AF.Identity, [Bp_, Bc], [By_], scale=cc("wffn", r * 3 + 2), bias=cc("bffn", r))
                            CP("pool", ux[:, 0:2], prev[:, r, :], [Bprev[r], Bux], [Bux])
                            STT(y_[:], ux[:, 1:513], cc("wffn", r * 3 + 1), y_[:], ALU.mult, ALU.add, [Bux, By_, Bc], [By_])
                            STT(y_[:], ux[:, 0:512], cc("wffn", r * 3 + 0), y_[:], ALU.mult, ALU.add, [Bux, By_, Bc], [By_])
                            CP("pool", prev[:, r, :], ux[:, 512:514], [Bux], [Bprev[r]])
                            ys.append((y_, By_))
                        ACT(ys[0][0][:], ys[0][0][:], AF.Silu, [ys[0][1]], [ys[0][1]])
                        TT("dve", actT[:, j, :], ys[0][0][:], ys[1][0][:], ALU.mult, [ys[0][1], ys[1][1]], [Bact[j]])
                    for dc in range(8):
                        p_, Bp_ = pd[dc % 2]
                        o_, Bo_ = ost[dc % 2]
                        for kc in range(22):
                            MM(p_[:], w_dn_bf[:, kc, dc * 128:(dc + 1) * 128], actT[:, kc, :], kc == 0, kc == 21, Bdn[kc] + [Bact[kc]], [Bp_])
                        TT("dve", o_[:], x4[:, dc, :], p_[:], ALU.add, [Bx4, Bp_], [Bo_])
                        S.dma("sp", yT_v[:, dc, g * 512:(g + 1) * 512], o_[:], reads=[Bo_])
        S.finish()
    return nc, S


def make_in_maps(inputs):
    x = np.asarray(inputs["x"], np.float32)
    mem = np.asarray(inputs["mem"], np.float32)
    positions = np.asarray(inputs["positions"], np.int32)
    shared = {
        "w_in": np.ascontiguousarray(inputs["w_in"][0], np.float32),
        "w_uq": np.ascontiguousarray(inputs["w_uq"][0], np.float32),
        "w_ukv": np.ascontiguousarray(inputs["w_ukv"][0], np.float32),
        "w_out": np.ascontiguousarray(inputs["w_out"][0], np.float32),
        "w_mem_q": np.ascontiguousarray(inputs["w_mem_q"][0], np.float32),
        "w_mem_kv": np.ascontiguousarray(inputs["w_mem_kv"][0], np.float32),
        "w_mem_o": np.ascontiguousarray(inputs["w_mem_o"][0], np.float32),
        "w_up": np.ascontiguousarray(inputs["w_up"][0], np.float32),
        "w_down": np.ascontiguousarray(inputs["w_down"][0], np.float32),
    }
    csts = [pack_consts(inputs, 0), pack_consts(inputs, 1)]
    in_maps = []
    for core in range(8):
        b, half = core // 2, core % 2
        xT = np.zeros((D, SEQ), np.float32)
        p = np.zeros((SEQ,), np.int32)
        if half == 1:
            xT[:] = x[b].T
            p[:] = positions[b]
        else:
            xT[:, 4096:] = x[b, 0:4096].T
            p[4096:] = positions[b, 0:4096]
        m = dict(shared)
        m["xT"] = xT
        m["pos"] = np.ascontiguousarray(p.reshape(64, 128).T)
        m["cst"] = csts[half]
        m["memT"] = np.ascontiguousarray(mem[b].T)
        in_maps.append(m)
    return in_maps


_PROG = {}


def kernel(**inputs):
    if "nc" not in _PROG:
        _PROG["nc"] = build_program()[0]
    nc = _PROG["nc"]
    in_maps = make_in_maps(inputs)
    res = run_bass_kernel_spmd(nc, in_maps, core_ids=list(range(8)))
    out = np.empty((4, SEQ, D), np.float32)
    for core in range(8):
        b, half = core // 2, core % 2
        out[b, half * 4096:(half + 1) * 4096, :] = res.results[core]["yT"].T
    return out
```

```python
import contextlib
import numpy as np
import concourse.bass as bass
import concourse.mybir as mybir
from concourse.bass_utils import run_bass_kernel_spmd

F32 = mybir.dt.float32
BF16 = mybir.dt.bfloat16
I32 = mybir.dt.int32
AF = mybir.ActivationFunctionType
ALU = mybir.AluOpType
AX = mybir.AxisListType

D = 1024
SEQ = 8192
NTOK = 4096
NQ = 4224
EPS = 1e-6
DFF = 2816
PI = float(np.pi)

_COLS = [("mixg", 8), ("bcin", 8), ("wdw", 124), ("bdw", 4), ("lng", 4), ("lnb", 4), ("memxg", 8),
         ("memmg", 8), ("mqg", 2), ("mkg", 2), ("ffng", 8), ("wffn", 132), ("bffn", 44), ("eps", 1),
         ("flag", 1), ("kbias", 64), ("gql", 256), ("gkvl", 128), ("gqn", 96), ("gkn", 96),
         ("invf", 16), ("ident", 128)]
C = {}
_o = 0
for _n, _w in _COLS:
    C[_n] = _o
    _o += _w
NCST = _o


def _colmajor(v, k):
    return np.ascontiguousarray(np.asarray(v, np.float32).reshape(k, 128).T)


def pack_consts(inp, half):
    c = np.zeros((128, NCST), np.float32)

    def put(name, arr):
        arr = np.asarray(arr, np.float32)
        c[:, C[name]:C[name] + arr.shape[1]] = arr

    put("mixg", _colmajor(inp["mix_norm_g"][0], 8))
    put("bcin", _colmajor(inp["b_conv_in"][0], 8))
    wdw = np.asarray(inp["w_conv_dw"][0], np.float32)
    put("wdw", wdw.reshape(31, 4, 128).transpose(2, 1, 0).reshape(128, 124))
    put("bdw", _colmajor(inp["b_conv_dw"][0], 4))
    put("lng", _colmajor(inp["conv_ln_g"][0], 4))
    put("lnb", _colmajor(inp["conv_ln_b"][0], 4))
    put("memxg", _colmajor(inp["mem_norm_x_g"][0], 8))
    put("memmg", _colmajor(inp["mem_norm_m_g"][0], 8))
    put("mqg", _colmajor(inp["mem_q_norm_g"][0], 2))
    put("mkg", _colmajor(inp["mem_k_norm_g"][0], 2))
    put("ffng", _colmajor(inp["ffn_norm_g"][0], 8))
    wf = np.asarray(inp["w_ffn_dw"][0], np.float32)
    put("wffn", wf.reshape(3, 44, 128).transpose(2, 1, 0).reshape(128, 132))
    put("bffn", _colmajor(inp["b_ffn_dw"][0], 44))
    c[:, C["eps"]] = EPS
    c[:, C["flag"]] = 1.0 if half == 1 else 0.0
    if half == 0:
        c[:, C["kbias"]:C["kbias"] + 32] = -30000.0
    put("gql", np.tile(np.asarray(inp["q_lat_norm_g"][0], np.float32)[None, :], (128, 1)))
    put("gkvl", np.tile(np.asarray(inp["kv_lat_norm_g"][0], np.float32)[None, :], (128, 1)))
    put("gqn", np.tile(np.asarray(inp["q_norm_g"][0], np.float32)[None, :], (128, 1)))
    put("gkn", np.tile(np.asarray(inp["k_norm_g"][0], np.float32)[None, :], (128, 1)))
    invf = (np.float32(10000.0) ** (-np.arange(0, 32, 2, dtype=np.float32) / np.float32(32))).astype(np.float32)
    put("invf", np.tile(invf[None, :], (128, 1)))
    put("ident", np.eye(128, dtype=np.float32))
    return c


_PSUM_PREFIX = ("pA", "pG", "pL", "pT", "pQ", "pKV", "pY", "pM", "pS", "pO", "pB", "pz", "pu", "pd", "pss")


class Buf:
    __slots__ = ("name", "w", "r", "psum")

    def __init__(self, name=""):
        self.name = name
        self.w = None
        self.r = []
        self.psum = name.startswith(_PSUM_PREFIX)


class Sched:
    NDMA = 24

    def __init__(self, nc):
        self.nc = nc
        self.ops = []
        self.eng = {"pe": nc.tensor, "act": nc.scalar, "dve": nc.vector, "pool": nc.gpsimd, "sp": nc.sync}

    def init_sems(self, es):
        self.sem = {e: es.enter_context(self.nc.semaphore("s_" + e)) for e in self.eng}
        self.dsem = [es.enter_context(self.nc.semaphore("d%d" % i)) for i in range(self.NDMA)]

    def op(self, eng, fn, reads=(), writes=()):
        self.ops.append(("c", eng, fn, tuple(reads), tuple(writes)))

    def dma(self, q, out, in_, reads=(), writes=()):
        eng = self.eng[q]
        self.ops.append(("d", q, (lambda: eng.dma_start(out=out, in_=in_)), tuple(reads), tuple(writes)))

    def barrier(self):
        self.ops.append(("b",))

    def finish(self):
        ops = self.ops
        n = len(ops)
        known = {e: {} for e in self.eng}
        vc = [None] * n
        waits = [None] * n
        signaled = [False] * n
        dma_use = [0] * self.NDMA
        dma_last = [None] * self.NDMA
        dma_ev = {}
        ndma = 0
        pend = {e: set() for e in self.eng}
        last_c = {}
        for i, o in enumerate(ops):
            if o[0] == "b":
                allp = set(last_c.values()) | set(x for x in dma_last if x is not None)
                for e in self.eng:
                    pend[e] |= allp
                waits[i] = {}
                vc[i] = {}
                continue
            kind, e = o[0], o[1]
            reads, writes = o[3], o[4]
            deps = set(pend[e])
            pend[e] = set()
            if kind == "c":
                last_c[e] = i
            for b in reads:
                if b.w is not None:
                    deps.add(b.w)
                if b.psum:
                    deps.update(x for x in b.r if ops[x][1] != e)
            for b in writes:
                if b.w is not None:
                    deps.add(b.w)
                deps.update(b.r)
            if kind == "d":
                s = ndma % self.NDMA
                ndma += 1
                if dma_last[s] is not None:
                    deps.add(dma_last[s])
                dma_use[s] += 1
                dma_ev[i] = (s, dma_use[s])
                dma_last[s] = i
            kn = known[e]
            w = {}
            for d in sorted(deps, reverse=True):
                od = ops[d]
                if od[0] == "c":
                    src = od[1]
                    if src == "pe" and e == "pe" and kind == "c":
                        continue
                    val = d
                else:
                    src = ("dma", dma_ev[d][0])
                    val = dma_ev[d][1]
                if kn.get(src, -1) >= val:
                    continue
                w[src] = max(w.get(src, -1), val)
                kn[src] = val
                for k2, v2 in vc[d].items():
                    if kn.get(k2, -1) < v2:
                        kn[k2] = v2
                if od[0] == "c":
                    signaled[d] = True
            waits[i] = w
            c = dict(kn)
            if kind == "c":
                c[e] = max(c.get(e, -1), i)
            else:
                c[("dma", dma_ev[i][0])] = dma_ev[i][1]
            vc[i] = c
            for b in reads:
                b.r.append(i)
            for b in writes:
                b.w = i
                b.r = []
        cnt = {e: 0 for e in self.eng}
        sigval = {}
        for i, o in enumerate(ops):
            if o[0] == "c" and signaled[i]:
                cnt[o[1]] += 1
                sigval[i] = cnt[o[1]]
        nw = 0
        self.trace = {e: [] for e in self.eng}
        for i, o in enumerate(ops):
            if o[0] == "b":
                continue
            e = o[1]
            self.trace[e].append((i, [((src, 16 * val) if isinstance(src, tuple) else (src, sigval[val])) for src, val in waits[i].items()],
                                  (("c", e, 1) if (o[0] == "c" and signaled[i]) else (("d", dma_ev[i][0], 16) if o[0] == "d" else None))))
            eng = self.eng[e]
            for src, val in waits[i].items():
                nw += 1
                if isinstance(src, tuple):
                    eng.wait_ge(self.dsem[src[1]], 16 * val)
                else:
                    eng.wait_ge(self.sem[src], sigval[val])
            ins = o[2]()
            if o[0] == "c":
                if signaled[i]:
                    ins.then_inc(self.sem[e], 1)
            else:
                ins.then_inc(self.dsem[dma_ev[i][0]], 16)
        sp = self.eng["sp"]
        for s in range(self.NDMA):
            if dma_use[s] > 0:
                sp.wait_ge(self.dsem[s], 16 * dma_use[s])
        self.stats = dict(nops=n, nwaits=nw, nsig=dict(cnt))


def build_program(dbg=False, stop_after=99):
    nc = bass.Bass("TRN2", target_bir_lowering=False)
    S = Sched(nc)
    E = S.eng

    def din(name, shape, dt=F32):
        return nc.dram_tensor(name, list(shape), dt, kind="ExternalInput").ap()

    def dscr(name, shape, dt):
        return nc.dram_tensor(name, list(shape), dt, kind=("ExternalOutput" if dbg else "Internal")).ap()

    xT = din("xT", [D, SEQ])
    pos = din("pos", [128, 64], I32)
    cst = din("cst", [128, NCST])
    memT = din("memT", [D, 256])
    w_in = din("w_in", [D, 1440])
    w_uq = din("w_uq", [256, 768])
    w_ukv = din("w_ukv", [128, 1024])
    w_out = din("w_out", [D, D])
    w_mq = din("w_mem_q", [D, D])
    w_mkv = din("w_mem_kv", [D, 2 * D])
    w_mo = din("w_mem_o", [D, D])
    w_up = din("w_up", [D, 2 * DFF])
    w_dn = din("w_down", [DFF, D])
    yT = nc.dram_tensor("yT", [D, NTOK], F32, kind="ExternalOutput").ap()
    QT_scr = dscr("QT_scr", [8, 96, NQ], BF16)
    KT_scr = dscr("KT_scr", [8, 96, SEQ], BF16)
    V_scr = dscr("V_scr", [8, 128, 64 * 128], BF16)
    x2_scr = dscr("x2_scr", [D, NQ], F32)
    dbg_out = {}
    if dbg:
        dbg_out["uT"] = nc.dram_tensor("dbg_uT", [128, 4, NQ + 30], BF16, kind="ExternalOutput").ap()
        dbg_out["ufT"] = nc.dram_tensor("dbg_ufT", [128, 4, NQ], BF16, kind="ExternalOutput").ap()
        dbg_out["attnT"] = nc.dram_tensor("dbg_attnT", [128, 4, NQ], BF16, kind="ExternalOutput").ap()

    xT_v = xT.rearrange("(kc p) t -> p kc t", p=128)
    x2_v = x2_scr.rearrange("(kc p) t -> p kc t", p=128)
    yT_v = yT.rearrange("(kc p) t -> p kc t", p=128)

    with contextlib.ExitStack() as es0:
        S.init_sems(es0)

        uid = [0]

        def mk(es):
            def sb(name, shape, dt):
                uid[0] += 1
                return es.enter_context(nc.sbuf_tensor("%s_%d" % (name, uid[0]), list(shape), dt))

            def ps(name, shape, dt=F32):
                uid[0] += 1
                return es.enter_context(nc.psum_tensor("%s_%d" % (name, uid[0]), list(shape), dt))
            return sb, ps

        sb0, _ = mk(es0)

        def ACT(out, in_, func, r, w, **kw):
            S.op("act", lambda: nc.scalar.activation(out, in_, func, **kw), r, w)

        def TT(eng, out, a, b, op, r, w):
            S.op(eng, lambda: E[eng].tensor_tensor(out, a, b, op), r, w)

        def TS(eng, out, a, s1, s2, op0, op1, r, w):
            if op1 is None:
                S.op(eng, lambda: E[eng].tensor_scalar(out, a, s1, None, op0), r, w)
            else:
                S.op(eng, lambda: E[eng].tensor_scalar(out, a, s1, s2, op0, op1), r, w)

        def STT(out, a, s, b, op0, op1, r, w):
            S.op("dve", lambda: nc.vector.scalar_tensor_tensor(out, a, s, b, op0, op1), r, w)

        def CP(eng, out, in_, r, w):
            if eng == "act":
                S.op("act", lambda: nc.scalar.copy(out, in_), r, w)
            else:
                S.op(eng, lambda: E[eng].tensor_copy(out, in_), r, w)

        def MM(out, lhsT, rhs, start, stop, r, w):
            S.op("pe", lambda: nc.tensor.matmul(out, lhsT, rhs, start=start, stop=stop), r, w)

        def TR(out, in_, ident, r, w):
            S.op("pe", lambda: nc.tensor.transpose(out, in_, ident), r, w)

        def RED(out, in_, r, w):
            S.op("dve", lambda: nc.vector.tensor_reduce(out, in_, AX.X, ALU.add), r, w)

        def RCP(out, in_, r, w):
            S.op("dve", lambda: nc.vector.reciprocal(out, in_), r, w)

        def MSET(eng, out, val, w):
            S.op(eng, lambda: E[eng].memset(out, val), (), w)

        cst_t = sb0("cst_t", [128, NCST], F32)
        Bc = Buf("cst")
        S.dma("sp", cst_t[:], cst, writes=[Bc])

        def cc(name, j=0, w=1):
            return cst_t[:, C[name] + j:C[name] + j + w]

        ident = sb0("ident", [128, 128], BF16)
        ones_bf = sb0("ones_bf", [128, 128], BF16)
        ones_f = sb0("ones_f", [128, 128], F32)
        Bid, Bob, Bof = Buf("ident"), Buf("ones_bf"), Buf("ones_f")
        CP("dve", ident[:], cc("ident", 0, 128), [Bc], [Bid])
        MSET("pool", ones_bf[:], 1.0, [Bob])
        MSET("pool", ones_f[:], 1.0, [Bof])

        esX = es0.enter_context(contextlib.ExitStack())
        uT = esX.enter_context(nc.sbuf_tensor("uT", [128, 4, NQ + 30], BF16))
        Buf_u = [Buf("u%d" % i) for i in range(9)]
        Bupad = Buf("upad")
        MSET("pool", uT[:, :, 0:30], 0.0, [Bupad])
        Buf_at = [Buf("at%d" % i) for i in range(9)]
        B_KTs = [Buf("KTs%d" % i) for i in range(16)]
        B_Vs = [Buf("Vs%d" % i) for i in range(16)]
        B_QTs = [Buf("QTs%d" % i) for i in range(9)]
        B_x2 = [Buf("x2s%d" % i) for i in range(9)]
        stgs = []

        @contextlib.contextmanager
        def staging():
            with contextlib.ExitStack() as esS:
                sbS, _ = mk(esS)
                stgs[:] = [(sbS("stg%d" % i, [128, 1024], F32), Buf("stg%d" % i)) for i in range(3)]
                yield
                S.barrier()
            stgs[:] = []

        stg_n = [0]

        def load_w(dst, name, src, K, N, gain=None):
            bufs = []
            for kc in range(K // 128):
                for c0 in range(0, N, 1024):
                    c1 = min(N, c0 + 1024)
                    wd = c1 - c0
                    i = stg_n[0]
                    stg_n[0] += 1
                    st, Bst = stgs[i % 3]
                    S.dma("sp", st[:, 0:wd], src[kc * 128:(kc + 1) * 128, c0:c1], writes=[Bst])
                    b = Buf(name)
                    bufs.append(b)
                    eng = ("act", "dve", "pool")[i % 3]
                    o = dst[:, kc, c0:c1]
                    if gain is not None:
                        g = cc(gain, kc)
                        if eng == "act":
                            ACT(o, st[:, 0:wd], AF.Copy, [Bst, Bc], [b], scale=g)
                        else:
                            TS(eng, o, st[:, 0:wd], g, None, ALU.mult, None, [Bst, Bc], [b])
                    else:
                        CP(eng, o, st[:, 0:wd], [Bst], [b])
            return bufs

        def rms_bcast(xin, Bx, n, sq, Bsq, pbank, Bp, rs, Brs, rstd, Brstd, nfeat=1024):
            kcs = nfeat // 128
            ACT(sq[:, 0:kcs, 0:n], xin, AF.Square, Bx, [Bsq])
            for kc in range(kcs):
                MM(pbank[:, 0:n], ones_bf[:], sq[:, kc, 0:n], kc == 0, kc == kcs - 1, [Bob, Bsq], [Bp])
            ACT(rs[:, 0:n], pbank[:, 0:n], AF.Sqrt, [Bp, Bc], [Brs], scale=1.0 / nfeat, bias=cc("eps"))
            RCP(rstd[:, 0:n], rs[:, 0:n], [Brs], [Brstd])

        with contextlib.ExitStack() as es1:
            if True:
                esA = es1
                sb, ps = mk(esA)
                cos2 = sb("cos2", [128, 64, 32], F32)
                sin2 = sb("sin2", [128, 64, 32], F32)
                Btab = Buf("tab")
                with contextlib.ExitStack() as esT:
                    sbt, _ = mk(esT)
                    pos_i = sbt("pos_i", [128, 64], I32)
                    posf = sbt("posf", [128, 64], F32)
                    ang = sbt("ang", [128, 64, 2, 16], F32)
                    tf = sbt("tf", [128, 2048], F32)
                    ti_ = sbt("ti_", [128, 2048], I32)
                    sc = sbt("sc", [128, 64, 2, 16], F32)
                    Bt = Buf("t")
                    angf = ang[:].rearrange("p a b c -> p (a b c)")
                    S.dma("sp", pos_i[:], pos, writes=[Bt])
                    CP("dve", posf[:], pos_i[:], [Bt], [Bt])
                    TT("dve", ang[:, :, 0, :], posf[:].unsqueeze(2).to_broadcast([128, 64, 16]),
                       cc("invf", 0, 16).unsqueeze(1).to_broadcast([128, 64, 16]), ALU.mult, [Bt, Bc], [Bt])
                    TS("dve", ang[:, :, 1, :], ang[:, :, 0, :], PI / 2, None, ALU.add, None, [Bt], [Bt])
                    C1 = 6.28125
                    C2 = 2 * np.pi - 6.28125
                    TS("dve", tf[:], angf, 1.0 / (2 * np.pi), None, ALU.mult, None, [Bt], [Bt])
                    CP("dve", ti_[:], tf[:], [Bt], [Bt])
                    CP("dve", tf[:], ti_[:], [Bt], [Bt])
                    STT(angf, tf[:], -C1, angf, ALU.mult, ALU.add, [Bt], [Bt])
                    STT(angf, tf[:], -C2, angf, ALU.mult, ALU.add, [Bt], [Bt])
                    TS("dve", tf[:], angf, PI, -2 * PI, ALU.is_gt, ALU.mult, [Bt], [Bt])
                    TT("dve", angf, angf, tf[:], ALU.add, [Bt], [Bt])
                    TS("dve", tf[:], angf, -PI, 2 * PI, ALU.is_lt, ALU.mult, [Bt], [Bt])
                    TT("dve", angf, angf, tf[:], ALU.add, [Bt], [Bt])
                    ACT(sc[:].rearrange("p a b c -> p (a b c)"), angf, AF.Sin, [Bt], [Bt])
                    CP("dve", cos2[:, :, 0:16], sc[:, :, 1, :], [Bt], [Btab])
                    CP("dve", cos2[:, :, 16:32], sc[:, :, 1, :], [Bt, Btab], [Btab])
                    CP("dve", sin2[:, :, 16:32], sc[:, :, 0, :], [Bt, Btab], [Btab])
                    TS("dve", sin2[:, :, 0:16], sc[:, :, 0, :], -1.0, None, ALU.mult, None, [Bt, Btab], [Btab])
                    S.barrier()

                w_in_bf = sb("w_in_bf", [128, 8, 1440], BF16)
                w_uq_bf = sb("w_uq_bf", [128, 2, 768], BF16)
                w_ukv_bf = sb("w_ukv_bf", [128, 1, 1024], BF16)
                with staging():
                    Bwin = load_w(w_in_bf, "w_in_bf", w_in, D, 1440, gain="mixg")
                    Bwuq = load_w(w_uq_bf, "w_uq_bf", w_uq, 256, 768)
                    Bwukv = load_w(w_ukv_bf, "w_ukv_bf", w_ukv, 128, 1024)

                xbuf = [(sb("xg%d" % i, [128, 8, 512], F32), Buf("xg%d" % i)) for i in range(2)]
                sq = sb("sq", [128, 8, 512], BF16)
                Bsq = Buf("sq")
                hT = sb("hT", [128, 8, 512], BF16)
                BhT = Buf("hT")
                rs = sb("rs", [128, 512], F32)
                Brs = Buf("rs")
                rstd = sb("rstd", [128, 512], F32)
                Brstd = Buf("rstd")
                sig = sb("sig", [128, 512], F32)
                Bsig = Buf("sig")
                sqLs = [(sb("sqL%d" % i, [128, 384], F32), Buf("sqL%d" % i)) for i in range(2)]
                st2s = [(sb("st2%d" % i, [128, 8], F32), Buf("st2%d" % i)) for i in range(2)]
                cns = [(sb("cn%d" % i, [128, 384], BF16), Buf("cn%d" % i)) for i in range(2)]
                cTs = [(sb("cT%d" % i, [128, 384], BF16), Buf("cT%d" % i)) for i in range(2)]
                sqq = sb("sqq", [128, 768], F32)
                Bsqq = Buf("sqq")
                stqs = [(sb("stq%d" % i, [128, 24], F32), Buf("stq%d" % i)) for i in range(2)]
                rts = [(sb("rt1_%d" % i, [128, 8, 32], F32), sb("rt2_%d" % i, [128, 8, 32], F32), Buf("rt%d" % i)) for i in range(2)]
                qbfs = [(sb("qbf%d" % i, [128, 8, 96], BF16), Buf("qbf%d" % i)) for i in range(2)]
                kvs = [sb("kvs%d" % i, [128, 1056], F32) for i in range(2)]
                Bkv_k = [Buf("kvk%d" % i) for i in range(2)]
                Bkv_v = [Buf("kvv%d" % i) for i in range(2)]
                Bkv_r = [Buf("kvr%d" % i) for i in range(2)]
                qs = [sb("qs%d" % i, [128, 768], F32) for i in range(2)]
                Bqs = [Buf("qs%d" % i) for i in range(2)]
                sqk = sb("sqk", [128, 8, 64], F32)
                Bsqk = Buf("sqk")
                sqr = sb("sqr", [128, 32], F32)
                Bsqr = Buf("sqr")
                stks = [(sb("stk%d" % i, [128, 32], F32), Buf("stk%d" % i)) for i in range(2)]
                krs = [(sb("kr%d" % i, [128, 4, 32], F32), Buf("kr%d" % i)) for i in range(2)]
                kbfs = [(sb("kbf%d" % i, [128, 8, 96], BF16), Buf("kbf%d" % i)) for i in range(2)]
                KTst = [(sb("KTst%d" % i, [96, 8, 512], BF16), Buf("KTst%d" % i)) for i in range(1)]
                QTst = [(sb("QTst%d" % i, [96, 8, 512], BF16), Buf("QTst%d" % i)) for i in range(1)]
                Vst = [(sb("Vst%d" % i, [128, 8, 4, 128], BF16), Buf("Vst%d" % i)) for i in range(1)]
                for (v, bv) in Vst:
                    MSET("pool", v[:, 0:8:2, :, 64:128], 1.0, [bv])
                    MSET("pool", v[:, 1:8:2, :, 0:64], 1.0, [bv])
                pA = ps("pA", [128, 512])
                pG = ps("pG", [128, 512])
                pTk = ps("pTk", [128, 1024], BF16)
                pT = ps("pT", [128, 1024], BF16)
                pQ = ps("pQ", [128, 1024])
                pKV = ps("pKV", [128, 1024])
                BpA, BpG, BpTk, BpT, BpQ, BpKV = [Buf(n) for n in "pA pG pTk pT pQ pKV".split()]
                pLs = [(pA, BpA), (pG, BpG)]
                KT_v = KT_scr.rearrange("h d t -> d h t")
                QT_v = QT_scr.rearrange("h d t -> d h t")
                V_v = V_scr.rearrange("h p (t e) -> p h t e", e=128)
                R96 = 96 ** -0.5

                S.dma("sp", xbuf[0][0][:], xT_v[:, :, 0:512], writes=[xbuf[0][1]])
                for G in range(16):
                    own = G >= 8
                    xg, Bxg = xbuf[G % 2]
                    if G + 1 < 16:
                        S.dma("sp", xbuf[(G + 1) % 2][0][:], xT_v[:, :, (G + 1) * 512:(G + 2) * 512],
                              writes=[xbuf[(G + 1) % 2][1]])
                    rms_bcast(xg[:], [Bxg], 512, sq, Bsq, pA, BpA, rs, Brs, rstd, Brstd)
                    TT("dve", hT[:], xg[:], rstd[:].unsqueeze(1).to_broadcast([128, 8, 512]), ALU.mult,
                       [Bxg, Brstd], [BhT])
                    if own or G == 7:
                        c0, n = (0, 512) if own else (384, 128)
                        qg = (G - 7) if own else 0
                        ucol = 30 + (128 + (G - 8) * 512 if own else 0)
                        for c in range(4):
                            for kc in range(8):
                                MM(pA[:, 0:n], w_in_bf[:, kc, c * 128:(c + 1) * 128], hT[:, kc, c0:c0 + n],
                                   kc == 0, kc == 7, Bwin + [BhT], [BpA])
                            for kc in range(8):
                                MM(pG[:, 0:n], w_in_bf[:, kc, 512 + c * 128:512 + (c + 1) * 128], hT[:, kc, c0:c0 + n],
                                   kc == 0, kc == 7, Bwin + [BhT], [BpG])
                            ACT(sig[:, 0:n], pG[:, 0:n], AF.Sigmoid, [BpG, Bc], [Bsig], bias=cc("bcin", 4 + c))
                            STT(uT[:, c, ucol:ucol + n], pA[:, 0:n], cc("bcin", c), sig[:, 0:n], ALU.add, ALU.mult,
                                [BpA, Bsig, Bc], [Buf_u[qg]])
                            if not own:
                                TS("dve", uT[:, c, ucol:ucol + n], uT[:, c, ucol:ucol + n], cc("flag"), None,
                                   ALU.mult, None, [Buf_u[qg], Bc], [Buf_u[qg]])
                    KTs, BKTs = KTst[0]
                    QTs, BQTs = QTst[0]
                    Vs, BVs = Vst[0]
                    def lat_chain(ti):
                        t = 4 * G + ti
                        hasq = own or t == 31
                        tsl = slice(ti * 128, (ti + 1) * 128)
                        l0 = 0 if hasq else 256
                        pLx, BpLx = pLs[ti % 2]
                        sqLx, BsqLx = sqLs[ti % 2]
                        st2x, Bst2x = st2s[ti % 2]
                        cnx, Bcnx = cns[ti % 2]
                        cTx, BcTx = cTs[ti % 2]
                        for kc in range(8):
                            MM(pLx[:, l0:416], hT[:, kc, tsl], w_in_bf[:, kc, 1024 + l0:1440], kc == 0, kc == 7,
                               Bwin + [BhT], [BpLx])
                        yield
                        if hasq:
                            ACT(sqLx[:, 0:256], pLx[:, 0:256], AF.Square, [BpLx], [BsqLx], scale=1.0 / 16)
                            yield
                        ACT(sqLx[:, 256:384], pLx[:, 256:384], AF.Square, [BpLx], [BsqLx], scale=128 ** -0.5)
                        yield
                        if hasq:
                            RED(st2x[:, 0:1], sqLx[:, 0:256], [BsqLx], [Bst2x])
                            yield
                        RED(st2x[:, 1:2], sqLx[:, 256:384], [BsqLx], [Bst2x])
                        yield
                        ACT(st2x[:, 2 + l0 // 256:4], st2x[:, l0 // 256:2], AF.Sqrt, [Bst2x, Bc], [Bst2x], bias=cc("eps"))
                        yield
                        RCP(st2x[:, 4 + l0 // 256:6], st2x[:, 2 + l0 // 256:4], [Bst2x], [Bst2x])
                        yield
                        if hasq:
                            STT(cnx[:, 0:256], pLx[:, 0:256], st2x[:, 4:5], cc("gql", 0, 256), ALU.mult, ALU.mult,
                                [BpLx, Bst2x, Bc], [Bcnx])
                            yield
                        STT(cnx[:, 256:384], pLx[:, 256:384], st2x[:, 5:6], cc("gkvl", 0, 128), ALU.mult, ALU.mult,
                            [BpLx, Bst2x, Bc], [Bcnx])
                        yield
                        for j in ((0, 1, 2) if hasq else (2,)):
                            TR(pT[:, j * 128:(j + 1) * 128], cnx[:, j * 128:(j + 1) * 128], ident[:], [Bcnx, Bid], [BpT])
                        yield
                        CP("act", cTx[:, l0:384], pT[:, l0:384], [BpT], [BcTx])
                        yield

                    def kq_mm(ti):
                        t = 4 * G + ti
                        p = ti % 2
                        hasq = own or t == 31
                        pLx, BpLx = pLs[p]
                        cTx, BcTx = cTs[p]
                        for (a, b) in ((0, 512), (512, 1024)):
                            MM(pKV[:, a:b], cTx[:, 256:384], w_ukv_bf[:, 0, a:b], True, True, Bwukv + [BcTx], [BpKV])
                        yield
                        CP("act", kvs[p][:, 0:1024], pKV[:], [BpKV], [Bkv_k[p], Bkv_v[p]])
                        yield
                        CP("dve", kvs[p][:, 1024:1056], pLx[:, 384:416], [BpLx], [Bkv_r[p]])
                        yield
                        if hasq:
                            for (a, b) in ((0, 512), (512, 768)):
                                for kc in range(2):
                                    MM(pQ[:, a:b], cTx[:, kc * 128:(kc + 1) * 128], w_uq_bf[:, kc, a:b], kc == 0, kc == 1,
                                       Bwuq + [BcTx], [BpQ])
                            yield
                            CP("dve", qs[p][:], pQ[:, 0:768], [BpQ], [Bqs[p]])
                            yield

                    def k_chain(ti):
                        t = 4 * G + ti
                        p = ti % 2
                        tsl = slice(ti * 128, (ti + 1) * 128)
                        kvv = kvs[p][:, 0:1024].rearrange("p (h e) -> p h e", h=8)
                        krr = kvs[p][:, 1024:1056]
                        stkx, Bstkx = stks[p]
                        krx, Bkrx = krs[p]
                        kbfx, Bkbfx = kbfs[p]
                        TT("pool", krx[:, 0, :], krr, cc("gkn", 64, 32), ALU.mult, [Bkv_r[p], Bc], [Bkrx])
                        yield
                        ACT(sqk[:], kvv[:, :, 0:64], AF.Square, [Bkv_k[p]], [Bsqk], scale=R96)
                        yield
                        ACT(sqr[:], krr, AF.Square, [Bkv_r[p]], [Bsqr], scale=R96)
                        yield
                        TT("pool", krx[:, 1, :], krx[:, 0, :], cos2[:, t, :], ALU.mult, [Bkrx, Btab], [Bkrx])
                        yield
                        RED(stkx[:, 0:8], sqk[:], [Bsqk], [Bstkx])
                        yield
                        RED(stkx[:, 8:9], sqr[:], [Bsqr], [Bstkx])
                        yield
                        TT("pool", krx[:, 2, 0:16], krx[:, 0, 16:32], sin2[:, t, 0:16], ALU.mult, [Bkrx, Btab], [Bkrx])
                        yield
                        TS("dve", stkx[:, 9:10], stkx[:, 8:9], EPS, None, ALU.add, None, [Bstkx], [Bstkx])
                        yield
                        CP("act", Vs[:, 0:8:2, ti, 0:64], kvv[:, 0:8:2, 64:128], [Bkv_v[p]], [BVs])
                        yield
                        TT("pool", krx[:, 2, 16:32], krx[:, 0, 0:16], sin2[:, t, 16:32], ALU.mult, [Bkrx, Btab], [Bkrx])
                        yield
                        ACT(stkx[:, 16:24], stkx[:, 0:8], AF.Sqrt, [Bstkx], [Bstkx], bias=stkx[:, 9:10])
                        yield
                        TT("pool", krx[:, 3, :], krx[:, 1, :], krx[:, 2, :], ALU.add, [Bkrx], [Bkrx])
                        yield
                        RCP(stkx[:, 24:32], stkx[:, 16:24], [Bstkx], [Bstkx])
                        yield
                        CP("act", Vs[:, 1:8:2, ti, 64:128], kvv[:, 1:8:2, 64:128], [Bkv_v[p]], [BVs])
                        yield
                        rk = stkx[:, 24:32]
                        TT("dve", kvv[:, :, 0:64], kvv[:, :, 0:64], rk.unsqueeze(2).to_broadcast([128, 8, 64]), ALU.mult,
                           [Bkv_k[p], Bstkx], [Bkv_k[p]])
                        yield
                        TT("dve", kbfx[:, :, 64:96], krx[:, 3:4, :].to_broadcast([128, 8, 32]),
                           rk.unsqueeze(2).to_broadcast([128, 8, 32]), ALU.mult, [Bkrx, Bstkx], [Bkbfx])
                        yield
                        TT("pool", kbfx[:, :, 0:64], kvv[:, :, 0:64], cc("gkn", 0, 64).unsqueeze(1).to_broadcast([128, 8, 64]),
                           ALU.mult, [Bkv_k[p], Bc], [Bkbfx])
                        yield
                        for h in range(8):
                            TR(pTk[0:96, h * 128:(h + 1) * 128], kbfx[:, h, :], ident[:], [Bkbfx, Bid], [BpTk])
                        yield
                        CP("act", KTs[:, :, tsl], pTk[0:96, :].rearrange("p (h t) -> p h t", h=8), [BpTk], [BKTs])
                        yield

                    def q_chain(ti):
                        t = 4 * G + ti
                        p = ti % 2
                        tsl = slice(ti * 128, (ti + 1) * 128)
                        stqx, Bstqx = stqs[p]
                        rt1x, rt2x, Brtx = rts[p]
                        qbfx, Bqbfx = qbfs[p]
                        qv = qs[p][:].rearrange("p (h e) -> p h e", h=8)
                        ACT(sqq[:], qs[p][:], AF.Square, [Bqs[p]], [Bsqq], scale=R96)
                        yield
                        RED(stqx[:, 0:8], sqq[:].rearrange("p (h e) -> p h e", h=8), [Bsqq], [Bstqx])
                        yield
                        ACT(stqx[:, 8:16], stqx[:, 0:8], AF.Sqrt, [Bstqx, Bc], [Bstqx], bias=cc("eps"))
                        yield
                        RCP(stqx[:, 16:24], stqx[:, 8:16], [Bstqx], [Bstqx])
                        yield
                        TT("dve", qv, qv, stqx[:, 16:24].unsqueeze(2).to_broadcast([128, 8, 96]), ALU.mult,
                           [Bqs[p], Bstqx], [Bqs[p]])
                        yield
                        TT("pool", qv, qv, cc("gqn", 0, 96).unsqueeze(1).to_broadcast([128, 8, 96]), ALU.mult,
                           [Bqs[p], Bc], [Bqs[p]])
                        yield
                        TT("pool", rt1x[:], qv[:, :, 64:96], cos2[:, t:t + 1, :].to_broadcast([128, 8, 32]), ALU.mult,
                           [Bqs[p], Btab], [Brtx])
                        yield
                        TT("dve", rt2x[:, :, 0:16], qv[:, :, 80:96], sin2[:, t:t + 1, 0:16].to_broadcast([128, 8, 16]),
                           ALU.mult, [Bqs[p], Btab, Brtx], [Brtx])
                        yield
                        TT("pool", rt2x[:, :, 16:32], qv[:, :, 64:80], sin2[:, t:t + 1, 16:32].to_broadcast([128, 8, 16]),
                           ALU.mult, [Bqs[p], Btab, Brtx], [Brtx])
                        yield
                        CP("act", qbfx[:, :, 0:64], qv[:, :, 0:64], [Bqs[p]], [Bqbfx])
                        yield
                        TT("dve", qbfx[:, :, 64:96], rt1x[:], rt2x[:], ALU.add, [Brtx], [Bqbfx])
                        yield
                        for h in range(8):
                            TR(pT[0:96, h * 128:(h + 1) * 128], qbfx[:, h, :], ident[:], [Bqbfx, Bid], [BpT])
                        yield
                        qsl = tsl if own else slice(0, 128)
                        CP("dve", QTs[:, :, qsl], pT[0:96, :].rearrange("p (h t) -> p h t", h=8), [BpT], [BQTs])
                        yield
                        if not own:
                            S.dma("sp", QT_v[:, :, 0:128], QTs[:, :, 0:128], reads=[BQTs], writes=[B_QTs[0]])

                    def run_dyn():
                        done = set()
                        active = []
                        pending = []

                        def wrap(name, gen):
                            yield from gen
                            done.add(name)

                        for ti_ in range(4):
                            hq_ = own or (4 * G + ti_) == 31
                            prev2 = set()
                            if ti_ >= 2:
                                prev2.add(("k", ti_ - 2))
                                if own or (4 * G + ti_ - 2) == 31:
                                    prev2.add(("q", ti_ - 2))
                            lat_need = set()
                            if ti_ >= 1:
                                lat_need.add(("lat", ti_ - 1))
                            if ti_ >= 2:
                                lat_need.add(("kq", ti_ - 2))
                            pending.append((lat_need, ("lat", ti_), lat_chain(ti_)))
                            pending.append(({("lat", ti_)} | prev2, ("kq", ti_), kq_mm(ti_)))
                            pending.append(({("kq", ti_)}, ("k", ti_), k_chain(ti_)))
                            if hq_:
                                pending.append(({("kq", ti_)}, ("q", ti_), q_chain(ti_)))
                        while active or pending:
                            for it in list(pending):
                                if it[0] <= done:
                                    pending.remove(it)
                                    active.append(wrap(it[1], it[2]))
                            for g_ in list(active):
                                try:
                                    next(g_)
                                except StopIteration:
                                    active.remove(g_)

                    run_dyn()
                    S.dma("sp", KT_v[:, :, G * 512:(G + 1) * 512], KTs[:], reads=[BKTs], writes=[B_KTs[G]])
                    S.dma("sp", V_v[:, :, 4 * G:4 * G + 4, :], Vs[:], reads=[BVs], writes=[B_Vs[G]])
                    if own:
                        q0 = 128 + (G - 8) * 512
                        S.dma("sp", QT_v[:, :, q0:q0 + 512], QTs[:], reads=[BQTs], writes=[B_QTs[G - 7]])
                S.barrier()
        if dbg:
            S.dma("sp", dbg_out["uT"], uT[:], reads=Buf_u + [Bupad])
        attnT = esX.enter_context(nc.sbuf_tensor("attnT", [128, 4, NQ], BF16))
        memK = esX.enter_context(nc.sbuf_tensor("memK", [128, 8, 256], BF16))
        memV = esX.enter_context(nc.sbuf_tensor("memV", [128, 2, 1024], BF16))
        BmK, BmV = Buf("memK"), Buf("memV")
        if True:

            if stop_after >= 2:
                with contextlib.ExitStack() as esB:
                    sb, ps = mk(esB)
                    Dg = sb("Dg", [128, 4, 31, 128], BF16)
                    BDg = [Buf("Dg%d" % c) for c in range(4)]
                    for c in range(4):
                        for k in range(31):
                            TS(("dve", "pool")[k % 2], Dg[:, c, k, :], ident[:], cc("wdw", c * 31 + k), None, ALU.mult, None,
                               [Bid, Bc], [BDg[c]])
                    pY = [(ps("pY%d" % i, [128, 512]), Buf("pY%d" % i)) for i in range(2)]
                    pM = ps("pM", [128, 512])
                    pM2 = ps("pM2", [128, 512])
                    BpM, BpM2 = Buf("pM"), Buf("pM2")
                    ybf = sb("ybf", [128, 4, 512], BF16)
                    ysq = sb("ysq", [128, 4, 512], BF16)
                    Bybf = [Buf("ybf%d" % c) for c in range(4)]
                    Bysq = [Buf("ysq%d" % c) for c in range(4)]
                    mean = sb("mean", [128, 512], F32)
                    m2 = sb("m2", [128, 512], F32)
                    var = sb("var", [128, 512], F32)
                    rsd = sb("rsd", [128, 512], F32)
                    Bmean, Bm2, Bvar, Brsd = Buf("mean"), Buf("m2"), Buf("var"), Buf("rsd")
                    tmp = [(sb("lt%d" % i, [128, 512], F32), Buf("lt%d" % i)) for i in range(2)]
                    w_mkv_bf = sb("w_mkv_bf", [128, 8, 2 * D], BF16)
                    stgM = [(sb("stgM%d" % i, [128, 1024], F32), Buf("stgM%d" % i)) for i in range(3)]
                    pzm = [(ps("pzm%d" % i, [128, 512]), Buf("pz_m%d" % i)) for i in range(4)]
                    msq = sb("sqm", [128, 8, 256], BF16)
                    Bmsq = Buf("sqm")
                    mrs = sb("rsm", [128, 512], F32)
                    Bmrs = Buf("rsm")
                    mrstd = sb("rstdm", [128, 512], F32)
                    Bmrstd = Buf("rstdm")
                    mx = sb("mx", [128, 8, 256], F32)
                    Bmx = Buf("mx")
                    hm = sb("hm", [128, 8, 256], BF16)
                    Bhm = Buf("hm")
                    ksq = sb("ksq", [128, 2, 256], BF16)
                    Bksq = Buf("ksq")
                    Bwmkv = []
                    mkv_pieces = [(kc, c0) for kc in range(8) for c0 in (0, 1024)]

                    def load_mkv_piece():
                        if not mkv_pieces:
                            return
                        kc, c0 = mkv_pieces.pop(0)
                        i = len(mkv_pieces)
                        st, Bst = stgM[i % 3]
                        S.dma("sp", st[:], w_mkv[kc * 128:(kc + 1) * 128, c0:c0 + 1024], writes=[Bst])
                        b_ = Buf("wmkv")
                        Bwmkv.append(b_)
                        TS(("dve", "pool")[i % 2], w_mkv_bf[:, kc, c0:c0 + 1024], st[:], cc("memmg", kc), None, ALU.mult, None,
                           [Bst, Bc], [b_])

                    S.dma("sp", mx[:], memT.rearrange("(kc p) t -> p kc t", p=128), writes=[Bmx])
                    yi = 0
                    for g in reversed(range(9)):
                        n = 128 if g == 0 else 512
                        oc = 0 if g == 0 else 128 + (g - 1) * 512
                        load_mkv_piece()
                        load_mkv_piece()
                        rd = [Bupad] + Buf_u[max(0, g - 1):g + 1]
                        for c in range(4):
                            py, Bpy = pY[yi % 2]
                            yi += 1
                            for k in range(31):
                                MM(py[:, 0:n], Dg[:, c, k, :], uT[:, c, oc + k:oc + k + n], k == 0, k == 30, rd + [BDg[c]], [Bpy])
                            ACT(ybf[:, c, 0:n], py[:, 0:n], AF.Identity, [Bpy, Bc], [Bybf[c]], bias=cc("bdw", c))
                            ACT(ysq[:, c, 0:n], py[:, 0:n], AF.Square, [Bpy, Bc], [Bysq[c]], bias=cc("bdw", c))
                        for c in range(4):
                            MM(pM[:, 0:n], ones_bf[:], ybf[:, c, 0:n], c == 0, c == 3, [Bob, Bybf[c]], [BpM])
                        for c in range(4):
                            MM(pM2[:, 0:n], ones_bf[:], ysq[:, c, 0:n], c == 0, c == 3, [Bob, Bysq[c]], [BpM2])
                        TS("dve", mean[:, 0:n], pM[:, 0:n], 1.0 / 512, None, ALU.mult, None, [BpM], [Bmean])
                        TT("dve", m2[:, 0:n], mean[:, 0:n], mean[:, 0:n], ALU.mult, [Bmean], [Bm2])
                        STT(var[:, 0:n], pM2[:, 0:n], 1.0 / 512, m2[:, 0:n], ALU.mult, ALU.subtract, [BpM2, Bm2], [Bvar])
                        ACT(var[:, 0:n], var[:, 0:n], AF.Sqrt, [Bvar, Bc], [Bvar], bias=cc("eps"))
                        RCP(rsd[:, 0:n], var[:, 0:n], [Bvar], [Brsd])
                        for c in range(4):
                            tm, Btm = tmp[c % 2]
                            TT("dve", tm[:, 0:n], ybf[:, c, 0:n], mean[:, 0:n], ALU.subtract, [Bybf[c], Bmean], [Btm])
                            TT("pool", tm[:, 0:n], tm[:, 0:n], rsd[:, 0:n], ALU.mult, [Btm, Brsd], [Btm])
                            ACT(uT[:, c, 30 + oc:30 + oc + n], tm[:, 0:n], AF.Silu, [Btm, Bc], [Buf_u[g]],
                                scale=cc("lng", c), bias=cc("lnb", c))
                    if stop_after >= 3:
                        while mkv_pieces:
                            load_mkv_piece()
                        rms_bcast(mx[:], [Bmx], 256, msq, Bmsq, pzm[0][0], pzm[0][1], mrs, Bmrs, mrstd, Bmrstd)
                        TT("dve", hm[:], mx[:], mrstd[:, 0:256].unsqueeze(1).to_broadcast([128, 8, 256]), ALU.mult,
                           [Bmx, Bmrstd], [Bhm])
                        for hd in range(4):
                            for j in range(2):
                                ch = hd * 2 + j
                                p_, Bp_ = pzm[1 + j]
                                for kc in range(8):
                                    MM(p_[:, 0:256], w_mkv_bf[:, kc, ch * 128:(ch + 1) * 128], hm[:, kc, :], kc == 0, kc == 7,
                                       Bwmkv + [Bhm], [Bp_])
                                ACT(ksq[:, j, :], p_[:, 0:256], AF.Square, [Bp_], [Bksq], scale=1.0 / 16)
                            for j in range(2):
                                MM(pzm[3][0][:, 0:256], ones_bf[:], ksq[:, j, :], j == 0, j == 1, [Bob, Bksq], [pzm[3][1]])
                            ACT(mrs[:, 0:256], pzm[3][0][:, 0:256], AF.Sqrt, [pzm[3][1], Bc], [Bmrs], bias=cc("eps"))
                            RCP(mrstd[:, 0:256], mrs[:, 0:256], [Bmrs], [Bmrstd])
                            for j in range(2):
                                STT(memK[:, hd * 2 + j, :], pzm[1 + j][0][:, 0:256], cc("mkg", j), mrstd[:, 0:256], ALU.mult, ALU.mult,
                                    [pzm[1 + j][1], Bmrstd, Bc], [BmK])
                        for kt in range(2):
                            for nb in range(2):
                                p_, Bp_ = pzm[(kt * 2 + nb) % 2 + 1]
                                for kc in range(8):
                                    MM(p_[:], hm[:, kc, kt * 128:(kt + 1) * 128], w_mkv_bf[:, kc, 1024 + nb * 512:1024 + (nb + 1) * 512],
                                       kc == 0, kc == 7, Bwmkv + [Bhm], [Bp_])
                                CP("act", memV[:, kt, nb * 512:(nb + 1) * 512], p_[:], [Bp_], [BmV])
                    S.barrier()
                if dbg:
                    S.dma("sp", dbg_out["ufT"], uT[:, :, 30:30 + NQ], reads=Buf_u)

        w_out_bf = esX.enter_context(nc.sbuf_tensor("w_out_bf", [128, 8, D], BF16))
        w_mq_bf = esX.enter_context(nc.sbuf_tensor("w_mq_bf", [128, 8, D], BF16))
        w_mo_bf = esX.enter_context(nc.sbuf_tensor("w_mo_bf", [128, 8, D], BF16))
        stgP = (esX.enter_context(nc.sbuf_tensor("stgP", [128, 512], F32)), Buf("stgP"))
        if stop_after >= 3:
            with contextlib.ExitStack() as es2:
                sb, ps = mk(es2)
                kT = [(sb("kT%d" % i, [96, SEQ], BF16), Buf("kT%d" % i)) for i in range(2)]
                vA = [(sb("vA%d" % i, [128, 64, 128], BF16), Buf("vA%d" % i)) for i in range(2)]
                qT = [(sb("qT%d" % i, [96, NQ], BF16), Buf("qT%d" % i)) for i in range(1)] * 2
                NP = 3
                Pb = [(sb("P%d" % i, [128, 512], BF16), Buf("P%d" % i)) for i in range(NP)]
                pS = [(ps("pS%d" % i, [128, 512]), Buf("pS%d" % i)) for i in range(4)]
                pO = [(ps("pO%d" % i, [128, 512]), Buf("pO%d" % i)) for i in range(2)]
                pB = ps("pB", [128, 512])
                BpB = Buf("pB")
                rrow = sb("rrow", [128, 512], F32)
                Brrow = Buf("rrow")
                bcs, Bbcs = rrow, Brrow
                Bscr = Buf("scr")
                SC = 96 ** -0.5

                def load_head(h):
                    hb = h % 2
                    S.dma("sp", kT[hb][0][:], KT_scr[h], reads=B_KTs, writes=[kT[hb][1]])
                    S.dma("sp", vA[hb][0][:], V_scr[h].rearrange("p (t e) -> p t e", e=128), reads=B_Vs, writes=[vA[hb][1]])

                def load_q(h):
                    S.dma("sp", qT[0][0][:], QT_scr[h], reads=B_QTs, writes=[qT[0][1]])

                load_head(0)
                load_q(0)
                sidx = 0
                oidx = 0
                Bwo, Bwmq, Bwmo = [], [], []
                pf = []
                for (dst_, src_, gain_, lst_) in ((w_out_bf, w_out, None, Bwo), (w_mq_bf, w_mq, "memxg", Bwmq),
                                                   (w_mo_bf, w_mo, None, Bwmo)):
                    for kc_ in range(8):
                        for hc_ in range(2):
                            pf.append((dst_, src_, gain_, lst_, kc_, hc_))

                def prefetch_piece():
                    if not pf:
                        return
                    dst_, src_, gain_, lst_, kc_, hc_ = pf.pop(0)
                    st_, Bst_ = stgP
                    cs_ = slice(hc_ * 512, (hc_ + 1) * 512)
                    S.dma("sp", st_[:], src_[kc_ * 128:(kc_ + 1) * 128, cs_], writes=[Bst_])
                    b_ = Buf("wpf")
                    lst_.append(b_)
                    eng_ = "pool" if len(pf) % 2 else "dve"
                    if gain_ is not None:
                        TS(eng_, dst_[:, kc_, cs_], st_[:], cc(gain_, kc_), None, ALU.mult, None, [Bst_, Bc], [b_])
                    else:
                        CP(eng_, dst_[:, kc_, cs_], st_[:], [Bst_], [b_])
                for h in range(8):
                    hb = h % 2
                    if h + 1 < 8:
                        load_head(h + 1)
                    k_t, Bk = kT[hb]
                    v_t, Bv = vA[hb]
                    q_t, Bq = qT[hb]
                    sr = 64 if h % 2 == 0 else 0
                    orow = 0 if h % 2 == 0 else 64
                    for g in range(9):
                        if g == 0:
                            N, q0, nk = 128, 0, 32
                            steps = [(t, 0, False) for t in range(31)] + [(31, 0, True)]
                        else:
                            N, q0, nk = 512, 128 + 512 * (g - 1), 32 + 4 * g
                            steps = [(t, 0, False) for t in range(nk - 4)] + [(nk - 4 + d, 128 * d, True) for d in range(4)]
                        po, Bpo = pO[oidx % 2]
                        oidx += 1
                        ns = len(steps)
                        prefetch_piece()

                        def emit_S(i):
                            t, cs, dg = steps[i]
                            p_s, Bps = pS[(sidx + i) % 4]
                            MM(p_s[:, cs:N], k_t[:, t * 128:(t + 1) * 128], q_t[:, q0 + cs:q0 + N], True, True, [Bk, Bq], [Bps])

                        for i in range(min(3, ns)):
                            emit_S(i)
                        for i in range(ns):
                            t, cs, dg = steps[i]
                            p_s, Bps = pS[(sidx + i) % 4]
                            pb, Bpb = Pb[(sidx + i) % NP]
                            ACT(pb[:, cs:N], p_s[:, cs:N], AF.Exp, [Bps, Bc], [Bpb], scale=SC, bias=cc("kbias", t))
                            if dg:
                                MSET("pool", pb[64:128, cs:cs + 64], 0.0, [Bpb])
                            MM(po[:, cs:N], v_t[:, t, :], pb[:, cs:N], i == 0, i == ns - 1, [Bv, Bpb], [Bpo])
                            if i + 3 < ns:
                                emit_S(i + 3)
                        sidx += ns
                        if g == 0:
                            TS("dve", rrow[sr:sr + 1, 0:N], po[sr:sr + 1, 0:N], 1e-30, None, ALU.max, None, [Bpo], [Brrow])
                            RCP(rrow[sr:sr + 1, 0:N], rrow[sr:sr + 1, 0:N], [Brrow], [Brrow])
                        else:
                            RCP(rrow[sr:sr + 1, 0:N], po[sr:sr + 1, 0:N], [Bpo], [Brrow])
                        MM(pB[:, 0:N], ones_f[sr:sr + 1, :], rrow[sr:sr + 1, 0:N], True, True, [Bof, Brrow], [BpB])
                        CP("dve", bcs[orow:orow + 64, 0:N], pB[orow:orow + 64, 0:N], [BpB], [Bbcs])
                        TT("dve", attnT[orow:orow + 64, h // 2, q0:q0 + N], po[orow:orow + 64, 0:N], bcs[orow:orow + 64, 0:N],
                           ALU.mult, [Bpo, Bbcs], [Buf_at[g]])
                    if h + 1 < 8:
                        load_q(h + 1)
                S.barrier()
            if dbg:
                S.dma("sp", dbg_out["attnT"], attnT[:], reads=Buf_at)

        if stop_after >= 4:
            with contextlib.ExitStack() as es3:
                sb, ps = mk(es3)
                pz = [(ps("pz%d" % i, [128, 512]), Buf("pz%d" % i)) for i in range(8)]
                hq = sb("hq", [128, 8, 512], BF16)
                Bhq = Buf("hq")
                sq, Bsq = hq, Bhq
                rs = sb("rs3", [128, 512], F32)
                Brs = Buf("rs3")
                rstd = sb("rstd3", [128, 512], F32)
                Brstd = Buf("rstd3")
                xb3 = [(sb("x3_%d" % i, [128, 8, 512], F32), Buf("x3_%d" % i)) for i in range(2)]
                qraws = [(sb("qraw%d" % i, [128, 2, 512], BF16), Buf("qraw%d" % i)) for i in range(2)]
                qsqs = [(sb("qsq%d" % i, [128, 2, 512], BF16), Buf("qsq%d" % i)) for i in range(2)]
                qmns = [(sb("qmn%d" % i, [128, 2, 512], BF16), Buf("qmn%d" % i)) for i in range(2)]
                pms = [[(sb("pm%d_%d" % (l_, i), [128, 512], BF16), Buf("pm%d_%d" % (l_, i))) for i in range(2)] for l_ in range(2)]
                rss = [(sb("rsx%d" % i, [128, 512], F32), Buf("rsx%d" % i)) for i in range(2)]
                rcss = [(sb("rcs%d" % i, [128, 512], F32), Buf("rcs%d" % i)) for i in range(2)]
                om = sb("om", [128, 8, 512], BF16)
                Bom = [Buf("om%d" % i) for i in range(4)]

                def xcols(g):
                    return (3968, 128) if g == 0 else (4096 + (g - 1) * 512, 512)

                c_, n_ = xcols(0)
                S.dma("sp", xb3[0][0][:, :, 0:n_], xT_v[:, :, c_:c_ + n_], writes=[xb3[0][1]])
                for g in range(9):
                    n = 128 if g == 0 else 512
                    oc = 0 if g == 0 else 128 + (g - 1) * 512
                    xg, Bxg = xb3[g % 2]
                    if g + 1 < 9:
                        c_, n_ = xcols(g + 1)
                        S.dma("sp", xb3[(g + 1) % 2][0][:, :, 0:n_], xT_v[:, :, c_:c_ + n_], writes=[xb3[(g + 1) % 2][1]])
                    def outproj_chain(g2):
                        n2 = 128 if g2 == 0 else 512
                        oc2 = 0 if g2 == 0 else 128 + (g2 - 1) * 512
                        xg2, Bxg2 = xb3[g2 % 2]
                        for dc in range(8):
                            p_, Bp_ = pz[dc % 2]
                            for kc in range(8):
                                if kc < 4:
                                    rhs_, bsrc = uT[:, kc, 30 + oc2:30 + oc2 + n2], Buf_u[g2]
                                else:
                                    rhs_, bsrc = attnT[:, kc - 4, oc2:oc2 + n2], Buf_at[g2]
                                MM(p_[:, 0:n2], w_out_bf[:, kc, dc * 128:(dc + 1) * 128], rhs_, kc == 0, kc == 7,
                                   Bwo + [bsrc], [Bp_])
                            yield
                            TT("dve", xg2[:, dc, 0:n2], xg2[:, dc, 0:n2], p_[:, 0:n2], ALU.add, [Bxg2, Bp_], [Bxg2])
                            yield

                    if g == 0:
                        for _ in outproj_chain(0):
                            pass
                    rms_bcast(xg[:, :, 0:n], [Bxg], n, sq, Bsq, pz[0][0], pz[0][1], rs, Brs, rstd, Brstd)
                    TT("dve", hq[:, :, 0:n], xg[:, :, 0:n], rstd[:, 0:n].unsqueeze(1).to_broadcast([128, 8, n]), ALU.mult,
                       [Bxg, Brstd], [Bhq])
                    def head_chain(hd):
                        L = hd % 2
                        pq_, Bpq_ = pz[2 + 3 * L]
                        pn_, Bpn_ = pz[3 + 3 * L]
                        psc, Bpsc = pz[4 + 3 * L]
                        qraw, Bqraw = qraws[L]
                        qsq, Bqsq = qsqs[L]
                        qmn, Bqmn = qmns[L]
                        rsx, Brsx = rss[L]
                        rcs, Brcs = rcss[L]
                        for j in range(2):
                            ch = hd * 2 + j
                            for kc in range(8):
                                MM(pq_[:, 0:n], w_mq_bf[:, kc, ch * 128:(ch + 1) * 128], hq[:, kc, 0:n], kc == 0, kc == 7,
                                   Bwmq + [Bhq], [Bpq_])
                            yield
                            ACT(qsq[:, j, 0:n], pq_[:, 0:n], AF.Square, [Bpq_], [Bqsq], scale=1.0 / 16)
                            yield
                            CP("dve", qraw[:, j, 0:n], pq_[:, 0:n], [Bpq_], [Bqraw])
                            yield
                        for j in range(2):
                            MM(pn_[:, 0:n], ones_bf[:], qsq[:, j, 0:n], j == 0, j == 1, [Bob, Bqsq], [Bpn_])
                        yield
                        ACT(rsx[:, 0:n], pn_[:, 0:n], AF.Sqrt, [Bpn_, Bc], [Brsx], bias=cc("eps"))
                        yield
                        RCP(rsx[:, 0:n], rsx[:, 0:n], [Brsx], [Brsx])
                        yield
                        for j in range(2):
                            STT(qmn[:, j, 0:n], qraw[:, j, 0:n], cc("mqg", j), rsx[:, 0:n], ALU.mult, ALU.mult,
                                [Bqraw, Brsx, Bc], [Bqmn])
                            yield
                        for kt in range(2):
                            for j in range(2):
                                MM(psc[:, 0:n], memK[:, hd * 2 + j, kt * 128:(kt + 1) * 128], qmn[:, j, 0:n], j == 0, j == 1,
                                   [BmK, Bqmn], [Bpsc])
                            yield
                            ACT(pms[L][kt][0][:, 0:n], psc[:, 0:n], AF.Exp, [Bpsc], [pms[L][kt][1]], scale=1.0 / 16)
                            yield
                        for kt in range(2):
                            MM(pn_[:, 0:n], ones_bf[:], pms[L][kt][0][:, 0:n], kt == 0, kt == 1, [Bob, pms[L][kt][1]], [Bpn_])
                        yield
                        RCP(rcs[:, 0:n], pn_[:, 0:n], [Bpn_], [Brcs])
                        yield
                        for j in range(2):
                            for kt in range(2):
                                MM(pq_[:, 0:n], memV[:, kt, (hd * 2 + j) * 128:(hd * 2 + j + 1) * 128], pms[L][kt][0][:, 0:n],
                                   kt == 0, kt == 1, [BmV, pms[L][kt][1]], [Bpq_])
                            yield
                            TT("dve", om[:, hd * 2 + j, 0:n], pq_[:, 0:n], rcs[:, 0:n], ALU.mult, [Bpq_, Brcs], [Bom[hd]])
                            yield

                    def rr3(*gens):
                        gens = list(gens)
                        while gens:
                            for g_ in list(gens):
                                try:
                                    next(g_)
                                except StopIteration:
                                    gens.remove(g_)

                    nxt = [outproj_chain(g + 1)] if g + 1 < 9 else []
                    rr3(head_chain(0), head_chain(1), *nxt)
                    rr3(head_chain(2), head_chain(3))
                    for dc in range(8):
                        p_, Bp_ = pz[dc % 2]
                        for kc in range(8):
                            MM(p_[:, 0:n], w_mo_bf[:, kc, dc * 128:(dc + 1) * 128], om[:, kc, 0:n], kc == 0, kc == 7,
                               Bwmo + [Bom[kc // 2]], [Bp_])
                        TT("dve", xg[:, dc, 0:n], xg[:, dc, 0:n], p_[:, 0:n], ALU.add, [Bxg, Bp_], [Bxg])
                    S.dma("sp", x2_v[:, :, oc:oc + n], xg[:, :, 0:n], reads=[Bxg], writes=[B_x2[g]])
                S.barrier()
        esX.close()

        if stop_after >= 5:
            with contextlib.ExitStack() as es4:
                sb, ps = mk(es4)
                w_up_bf = sb("w_up_bf", [128, 8, 2 * DFF], BF16)
                w_dn_bf = sb("w_dn_bf", [128, 22, D], BF16)
                stg4 = [(sb("stg4_%d" % i, [128, 512], F32), Buf("stg4_%d" % i)) for i in range(3)]
                x4 = sb("x4", [128, 8, 512], F32)
                Bx4 = Buf("x4")
                x4h = sb("x4h", [128, 8, 2], F32)
                Bx4h = Buf("x4h")
                h3 = sb("h3", [128, 8, 512], BF16)
                Bh3 = Buf("h3")
                h3h = sb("h3h", [128, 8, 2], BF16)
                Bh3h = Buf("h3h")
                sq, Bsq = h3, Bh3
                rs = sb("rs4", [128, 512], F32)
                Brs = Buf("rs4")
                rstd, Brstd = rs, Brs
                prev = sb("prev", [128, 44, 2], F32)
                Bprev = [Buf("prev%d" % r) for r in range(44)]
                upx = [(sb("upx%d" % i, [128, 514], F32), Buf("upx%d" % i)) for i in range(2)]
                yv = [(sb("yv%d" % i, [128, 512], F32), Buf("yv%d" % i)) for i in range(3)]
                actT = sb("actT", [128, 22, 512], BF16)
                Bact = [Buf("act%d" % j) for j in range(22)]
                ost = [(sb("ost%d" % i, [128, 512], F32), Buf("ost%d" % i)) for i in range(2)]
                pu = [(ps("pu%d" % i, [128, 512]), Buf("pu%d" % i)) for i in range(4)]
                pd = [(ps("pd%d" % i, [128, 512]), Buf("pd%d" % i)) for i in range(2)]
                pss = ps("pss", [128, 512])
                Bpss = Buf("pss")
                Bup = {}
                Bdn = {}
                s4 = [0]

                def load_up_block(c0_, c1_):
                    for c0 in range(c0_, c1_, 512):
                        _load_up_piece(c0, min(c1_, c0 + 512))

                def _load_up_piece(c0, c1):
                    for kc in range(8):
                        i = s4[0]
                        s4[0] += 1
                        st, Bst = stg4[i % 3]
                        S.dma("sp", st[:, 0:c1 - c0], w_up[kc * 128:(kc + 1) * 128, c0:c1], writes=[Bst])
                        b_ = Buf("wup")
                        eng = ("pool", "dve", "act")[i % 3]
                        o = w_up_bf[:, kc, c0:c1]
                        if eng == "act":
                            ACT(o, st[:, 0:c1 - c0], AF.Copy, [Bst, Bc], [b_], scale=cc("ffng", kc))
                        else:
                            TS(eng, o, st[:, 0:c1 - c0], cc("ffng", kc), None, ALU.mult, None, [Bst, Bc], [b_])
                        for r in range(c0 // 128, c1 // 128):
                            Bup.setdefault(r, []).append(b_)

                def load_dn(kc):
                    Bdn[kc] = []
                    for hc in range(2):
                        i = s4[0]
                        s4[0] += 1
                        st, Bst = stg4[i % 3]
                        S.dma("sp", st[:], w_dn[kc * 128:(kc + 1) * 128, hc * 512:(hc + 1) * 512], writes=[Bst])
                        b_ = Buf("wdn")
                        CP(("pool", "dve", "act")[i % 3], w_dn_bf[:, kc, hc * 512:(hc + 1) * 512], st[:], [Bst], [b_])
                        Bdn[kc].append(b_)

                S.dma("sp", x4h[:], x2_v[:, :, 126:128], reads=[B_x2[0]], writes=[Bx4h])
                S.dma("sp", x4[:], x2_v[:, :, 128:640], reads=[B_x2[1]], writes=[Bx4])
                load_up_block(0, 1024)
                load_up_block(2816, 3840)
                rms_bcast(x4h[:], [Bx4h], 2, h3h, Bh3h, pss, Bpss, rs, Brs, rstd, Brstd)
                TT("dve", h3h[:], x4h[:], rstd[:, 0:2].unsqueeze(1).to_broadcast([128, 8, 2]), ALU.mult,
                   [Bx4h, Brstd], [Bh3h])
                ui = 0
                dn_next = [0]
                for g in range(8):
                    oc = 128 + g * 512
                    if g > 0:
                        S.dma("sp", x4[:], x2_v[:, :, oc:oc + 512], reads=[B_x2[g + 1]], writes=[Bx4])
                    rms_bcast(x4[:], [Bx4], 512, sq, Bsq, pss, Bpss, rs, Brs, rstd, Brstd)
                    TT("dve", h3[:], x4[:], rstd[:].unsqueeze(1).to_broadcast([128, 8, 512]), ALU.mult, [Bx4, Brstd], [Bh3])
                    for j in range(22):
                        if g == 0:
                            if j == 0:
                                load_up_block(1024, 2048)
                                load_up_block(3840, 4864)
                            if j == 6:
                                load_up_block(2048, 2816)
                                load_up_block(4864, 5632)
                            if j >= 10:
                                for _ in range(2):
                                    if dn_next[0] < 22:
                                        load_dn(dn_next[0])
                                        dn_next[0] += 1
                        ys = []
                        for r in (j, 22 + j):
                            p_, Bp_ = pu[ui % 4]
                            ux, Bux = upx[ui % 2]
                            y_, By_ = yv[ui % 3]
                            ui += 1
                            if g == 0:
                                ph, Bph = pu[ui % 4]
                                for kc in range(8):
                                    MM(ph[:, 0:2], w_up_bf[:, kc, r * 128:(r + 1) * 128], h3h[:, kc, :], kc == 0, kc == 7,
                                       Bup[r] + [Bh3h], [Bph])
                                TS("dve", prev[:, r, :], ph[:, 0:2], cc("flag"), None, ALU.mult, None, [Bph, Bc], [Bprev[r]])
                            for kc in range(8):
                                MM(p_[:], w_up_bf[:, kc, r * 128:(r + 1) * 128], h3[:, kc, :], kc == 0, kc == 7, Bup[r] + [Bh3], [Bp_])
                            CP("act", ux[:, 2:514], p_[:], [Bp_], [Bux])
                            ACT(y_[:], p_[:], AF.Identity, [Bp_, Bc], [By_], scale=cc("wffn", r * 3 + 2), bias=cc("bffn", r))
                            CP("pool", ux[:, 0:2], prev[:, r, :], [Bprev[r], Bux], [Bux])
                            STT(y_[:], ux[:, 1:513], cc("wffn", r * 3 + 1), y_[:], ALU.mult, ALU.add, [Bux, By_, Bc], [By_])
                            STT(y_[:], ux[:, 0:512], cc("wffn", r * 3 + 0), y_[:], ALU.mult, ALU.add, [Bux, By_, Bc], [By_])
                            CP("pool", prev[:, r, :], ux[:, 512:514], [Bux], [Bprev[r]])
                            ys.append((y_, By_))
                        ACT(ys[0][0][:], ys[0][0][:], AF.Silu, [ys[0][1]], [ys[0][1]])
                        TT("dve", actT[:, j, :], ys[0][0][:], ys[1][0][:], ALU.mult, [ys[0][1], ys[1][1]], [Bact[j]])
                    for dc in range(8):
                        p_, Bp_ = pd[dc % 2]
                        o_, Bo_ = ost[dc % 2]
                        for kc in range(22):
                            MM(p_[:], w_dn_bf[:, kc, dc * 128:(dc + 1) * 128], actT[:, kc, :], kc == 0, kc == 21, Bdn[kc] + [Bact[kc]], [Bp_])
                        TT("dve", o_[:], x4[:, dc, :], p_[:], ALU.add, [Bx4, Bp_], [Bo_])
                        S.dma("sp", yT_v[:, dc, g * 512:(g + 1) * 512], o_[:], reads=[Bo_])
        S.finish()
    return nc, S


def make_in_maps(inputs):
    x = np.asarray(inputs["x"], np.float32)
    mem = np.asarray(inputs["mem"], np.float32)
    positions = np.asarray(inputs["positions"], np.int32)
    shared = {
        "w_in": np.ascontiguousarray(inputs["w_in"][0], np.float32),
        "w_uq": np.ascontiguousarray(inputs["w_uq"][0], np.float32),
        "w_ukv": np.ascontiguousarray(inputs["w_ukv"][0], np.float32),
        "w_out": np.ascontiguousarray(inputs["w_out"][0], np.float32),
        "w_mem_q": np.ascontiguousarray(inputs["w_mem_q"][0], np.float32),
        "w_mem_kv": np.ascontiguousarray(inputs["w_mem_kv"][0], np.float32),
        "w_mem_o": np.ascontiguousarray(inputs["w_mem_o"][0], np.float32),
        "w_up": np.ascontiguousarray(inputs["w_up"][0], np.float32),
        "w_down": np.ascontiguousarray(inputs["w_down"][0], np.float32),
    }
    csts = [pack_consts(inputs, 0), pack_consts(inputs, 1)]
    in_maps = []
    for core in range(8):
        b, half = core // 2, core % 2
        xT = np.zeros((D, SEQ), np.float32)
        p = np.zeros((SEQ,), np.int32)
        if half == 1:
            xT[:] = x[b].T
            p[:] = positions[b]
        else:
            xT[:, 4096:] = x[b, 0:4096].T
            p[4096:] = positions[b, 0:4096]
        m = dict(shared)
        m["xT"] = xT
        m["pos"] = np.ascontiguousarray(p.reshape(64, 128).T)
        m["cst"] = csts[half]
        m["memT"] = np.ascontiguousarray(mem[b].T)
        in_maps.append(m)
    return in_maps


_PROG = {}


def kernel(**inputs):
    if "nc" not in _PROG:
        _PROG["nc"] = build_program()[0]
    nc = _PROG["nc"]
    in_maps = make_in_maps(inputs)
    res = run_bass_kernel_spmd(nc, in_maps, core_ids=list(range(8)))
    out = np.empty((4, SEQ, D), np.float32)
    for core in range(8):
        b, half = core // 2, core % 2
        out[b, half * 4096:(half + 1) * 4096, :] = res.results[core]["yT"].T
    return out
```

```python
import contextlib
import numpy as np
import concourse.bass as bass
import concourse.mybir as mybir
from concourse.bass_utils import run_bass_kernel_spmd

F32 = mybir.dt.float32
BF16 = mybir.dt.bfloat16
I32 = mybir.dt.int32
AF = mybir.ActivationFunctionType
ALU = mybir.AluOpType
AX = mybir.AxisListType

D = 1024
SEQ = 8192
NTOK = 4096
NQ = 4224
EPS = 1e-6
DFF = 2816
PI = float(np.pi)

_COLS = [("mixg", 8), ("bcin", 8), ("wdw", 124), ("bdw", 4), ("lng", 4), ("lnb", 4), ("memxg", 8),
         ("memmg", 8), ("mqg", 2), ("mkg", 2), ("ffng", 8), ("wffn", 132), ("bffn", 44), ("eps", 1),
         ("flag", 1), ("kbias", 64), ("gql", 256), ("gkvl", 128), ("gqn", 96), ("gkn", 96),
         ("invf", 16), ("ident", 128)]
C = {}
_o = 0
for _n, _w in _COLS:
    C[_n] = _o
    _o += _w
NCST = _o


def _colmajor(v, k):
    return np.ascontiguousarray(np.asarray(v, np.float32).reshape(k, 128).T)


def pack_consts(inp, half):
    c = np.zeros((128, NCST), np.float32)

    def put(name, arr):
        arr = np.asarray(arr, np.float32)
        c[:, C[name]:C[name] + arr.shape[1]] = arr

    put("mixg", _colmajor(inp["mix_norm_g"][0], 8))
    put("bcin", _colmajor(inp["b_conv_in"][0], 8))
    wdw = np.asarray(inp["w_conv_dw"][0], np.float32)
    put("wdw", wdw.reshape(31, 4, 128).transpose(2, 1, 0).reshape(128, 124))
    put("bdw", _colmajor(inp["b_conv_dw"][0], 4))
    put("lng", _colmajor(inp["conv_ln_g"][0], 4))
    put("lnb", _colmajor(inp["conv_ln_b"][0], 4))
    put("memxg", _colmajor(inp["mem_norm_x_g"][0], 8))
    put("memmg", _colmajor(inp["mem_norm_m_g"][0], 8))
    put("mqg", _colmajor(inp["mem_q_norm_g"][0], 2))
    put("mkg", _colmajor(inp["mem_k_norm_g"][0], 2))
    put("ffng", _colmajor(inp["ffn_norm_g"][0], 8))
    wf = np.asarray(inp["w_ffn_dw"][0], np.float32)
    put("wffn", wf.reshape(3, 44, 128).transpose(2, 1, 0).reshape(128, 132))
    put("bffn", _colmajor(inp["b_ffn_dw"][0], 44))
    c[:, C["eps"]] = EPS
    c[:, C["flag"]] = 1.0 if half == 1 else 0.0
    if half == 0:
        c[:, C["kbias"]:C["kbias"] + 32] = -30000.0
    put("gql", np.tile(np.asarray(inp["q_lat_norm_g"][0], np.float32)[None, :], (128, 1)))
    put("gkvl", np.tile(np.asarray(inp["kv_lat_norm_g"][0], np.float32)[None, :], (128, 1)))
    put("gqn", np.tile(np.asarray(inp["q_norm_g"][0], np.float32)[None, :], (128, 1)))
    put("gkn", np.tile(np.asarray(inp["k_norm_g"][0], np.float32)[None, :], (128, 1)))
    invf = (np.float32(10000.0) ** (-np.arange(0, 32, 2, dtype=np.float32) / np.float32(32))).astype(np.float32)
    put("invf", np.tile(invf[None, :], (128, 1)))
    put("ident", np.eye(128, dtype=np.float32))
    return c


_PSUM_PREFIX = ("pA", "pG", "pL", "pT", "pQ", "pKV", "pY", "pM", "pS", "pO", "pB", "pz", "pu", "pd", "pss")


class Buf:
    __slots__ = ("name", "w", "r", "psum")

    def __init__(self, name=""):
        self.name = name
        self.w = None
        self.r = []
        self.psum = name.startswith(_PSUM_PREFIX)


class Sched:
    NDMA = 24

    def __init__(self, nc):
        self.nc = nc
        self.ops = []
        self.eng = {"pe": nc.tensor, "act": nc.scalar, "dve": nc.vector, "pool": nc.gpsimd, "sp": nc.sync}

    def init_sems(self, es):
        self.sem = {e: es.enter_context(self.nc.semaphore("s_" + e)) for e in self.eng}
        self.dsem = [es.enter_context(self.nc.semaphore("d%d" % i)) for i in range(self.NDMA)]

    def op(self, eng, fn, reads=(), writes=()):
        self.ops.append(("c", eng, fn, tuple(reads), tuple(writes)))

    def dma(self, q, out, in_, reads=(), writes=()):
        eng = self.eng[q]
        self.ops.append(("d", q, (lambda: eng.dma_start(out=out, in_=in_)), tuple(reads), tuple(writes)))

    def barrier(self):
        self.ops.append(("b",))

    def finish(self):
        ops = self.ops
        n = len(ops)
        known = {e: {} for e in self.eng}
        vc = [None] * n
        waits = [None] * n
        signaled = [False] * n
        dma_use = [0] * self.NDMA
        dma_last = [None] * self.NDMA
        dma_ev = {}
        ndma = 0
        pend = {e: set() for e in self.eng}
        last_c = {}
        for i, o in enumerate(ops):
            if o[0] == "b":
                allp = set(last_c.values()) | set(x for x in dma_last if x is not None)
                for e in self.eng:
                    pend[e] |= allp
                waits[i] = {}
                vc[i] = {}
                continue
            kind, e = o[0], o[1]
            reads, writes = o[3], o[4]
            deps = set(pend[e])
            pend[e] = set()
            if kind == "c":
                last_c[e] = i
            for b in reads:
                if b.w is not None:
                    deps.add(b.w)
                if b.psum:
                    deps.update(x for x in b.r if ops[x][1] != e)
            for b in writes:
                if b.w is not None:
                    deps.add(b.w)
                deps.update(b.r)
            if kind == "d":
                s = ndma % self.NDMA
                ndma += 1
                if dma_last[s] is not None:
                    deps.add(dma_last[s])
                dma_use[s] += 1
                dma_ev[i] = (s, dma_use[s])
                dma_last[s] = i
            kn = known[e]
            w = {}
            for d in sorted(deps, reverse=True):
                od = ops[d]
                if od[0] == "c":
                    src = od[1]
                    if src == "pe" and e == "pe" and kind == "c":
                        continue
                    val = d
                else:
                    src = ("dma", dma_ev[d][0])
                    val = dma_ev[d][1]
                if kn.get(src, -1) >= val:
                    continue
                w[src] = max(w.get(src, -1), val)
                kn[src] = val
                for k2, v2 in vc[d].items():
                    if kn.get(k2, -1) < v2:
                        kn[k2] = v2
                if od[0] == "c":
                    signaled[d] = True
            waits[i] = w
            c = dict(kn)
            if kind == "c":
                c[e] = max(c.get(e, -1), i)
            else:
                c[("dma", dma_ev[i][0])] = dma_ev[i][1]
            vc[i] = c
            for b in reads:
                b.r.append(i)
            for b in writes:
                b.w = i
                b.r = []
        cnt = {e: 0 for e in self.eng}
        sigval = {}
        for i, o in enumerate(ops):
            if o[0] == "c" and signaled[i]:
                cnt[o[1]] += 1
                sigval[i] = cnt[o[1]]
        nw = 0
        self.trace = {e: [] for e in self.eng}
        for i, o in enumerate(ops):
            if o[0] == "b":
                continue
            e = o[1]
            self.trace[e].append((i, [((src, 16 * val) if isinstance(src, tuple) else (src, sigval[val])) for src, val in waits[i].items()],
                                  (("c", e, 1) if (o[0] == "c" and signaled[i]) else (("d", dma_ev[i][0], 16) if o[0] == "d" else None))))
            eng = self.eng[e]
            for src, val in waits[i].items():
                nw += 1
                if isinstance(src, tuple):
                    eng.wait_ge(self.dsem[src[1]], 16 * val)
                else:
                    eng.wait_ge(self.sem[src], sigval[val])
            ins = o[2]()
            if o[0] == "c":
                if signaled[i]:
                    ins.then_inc(self.sem[e], 1)
            else:
                ins.then_inc(self.dsem[dma_ev[i][0]], 16)
        sp = self.eng["sp"]
        for s in range(self.NDMA):
            if dma_use[s] > 0:
                sp.wait_ge(self.dsem[s], 16 * dma_use[s])
        self.stats = dict(nops=n, nwaits=nw, nsig=dict(cnt))


def build_program(dbg=False, stop_after=99):
    nc = bass.Bass("TRN2", target_bir_lowering=False)
    S = Sched(nc)
    E = S.eng

    def din(name, shape, dt=F32):
        return nc.dram_tensor(name, list(shape), dt, kind="ExternalInput").ap()

    def dscr(name, shape, dt):
        return nc.dram_tensor(name, list(shape), dt, kind=("ExternalOutput" if dbg else "Internal")).ap()

    xT = din("xT", [D, SEQ])
    pos = din("pos", [128, 64], I32)
    cst = din("cst", [128, NCST])
    memT = din("memT", [D, 256])
    w_in = din("w_in", [D, 1440])
    w_uq = din("w_uq", [256, 768])
    w_ukv = din("w_ukv", [128, 1024])
    w_out = din("w_out", [D, D])
    w_mq = din("w_mem_q", [D, D])
    w_mkv = din("w_mem_kv", [D, 2 * D])
    w_mo = din("w_mem_o", [D, D])
    w_up = din("w_up", [D, 2 * DFF])
    w_dn = din("w_down", [DFF, D])
    yT = nc.dram_tensor("yT", [D, NTOK], F32, kind="ExternalOutput").ap()
    QT_scr = dscr("QT_scr", [8, 96, NQ], BF16)
    KT_scr = dscr("KT_scr", [8, 96, SEQ], BF16)
    V_scr = dscr("V_scr", [8, 128, 64 * 128], BF16)
    x2_scr = dscr("x2_scr", [D, NQ], F32)
    dbg_out = {}
    if dbg:
        dbg_out["uT"] = nc.dram_tensor("dbg_uT", [128, 4, NQ + 30], BF16, kind="ExternalOutput").ap()
        dbg_out["ufT"] = nc.dram_tensor("dbg_ufT", [128, 4, NQ], BF16, kind="ExternalOutput").ap()
        dbg_out["attnT"] = nc.dram_tensor("dbg_attnT", [128, 4, NQ], BF16, kind="ExternalOutput").ap()

    xT_v = xT.rearrange("(kc p) t -> p kc t", p=128)
    x2_v = x2_scr.rearrange("(kc p) t -> p kc t", p=128)
    yT_v = yT.rearrange("(kc p) t -> p kc t", p=128)

    with contextlib.ExitStack() as es0:
        S.init_sems(es0)

        uid = [0]

        def mk(es):
            def sb(name, shape, dt):
                uid[0] += 1
                return es.enter_context(nc.sbuf_tensor("%s_%d" % (name, uid[0]), list(shape), dt))

            def ps(name, shape, dt=F32):
                uid[0] += 1
                return es.enter_context(nc.psum_tensor("%s_%d" % (name, uid[0]), list(shape), dt))
            return sb, ps

        sb0, _ = mk(es0)

        def ACT(out, in_, func, r, w, **kw):
            S.op("act", lambda: nc.scalar.activation(out, in_, func, **kw), r, w)

        def TT(eng, out, a, b, op, r, w):
            S.op(eng, lambda: E[eng].tensor_tensor(out, a, b, op), r, w)

        def TS(eng, out, a, s1, s2, op0, op1, r, w):
            if op1 is None:
                S.op(eng, lambda: E[eng].tensor_scalar(out, a, s1, None, op0), r, w)
            else:
                S.op(eng, lambda: E[eng].tensor_scalar(out, a, s1, s2, op0, op1), r, w)

        def STT(out, a, s, b, op0, op1, r, w):
            S.op("dve", lambda: nc.vector.scalar_tensor_tensor(out, a, s, b, op0, op1), r, w)

        def CP(eng, out, in_, r, w):
            if eng == "act":
                S.op("act", lambda: nc.scalar.copy(out, in_), r, w)
            else:
                S.op(eng, lambda: E[eng].tensor_copy(out, in_), r, w)

        def MM(out, lhsT, rhs, start, stop, r, w):
            S.op("pe", lambda: nc.tensor.matmul(out, lhsT, rhs, start=start, stop=stop), r, w)

        def TR(out, in_, ident, r, w):
            S.op("pe", lambda: nc.tensor.transpose(out, in_, ident), r, w)

        def RED(out, in_, r, w):
            S.op("dve", lambda: nc.vector.tensor_reduce(out, in_, AX.X, ALU.add), r, w)

        def RCP(out, in_, r, w):
            S.op("dve", lambda: nc.vector.reciprocal(out, in_), r, w)

        def MSET(eng, out, val, w):
            S.op(eng, lambda: E[eng].memset(out, val), (), w)

        cst_t = sb0("cst_t", [128, NCST], F32)
        Bc = Buf("cst")
        S.dma("sp", cst_t[:], cst, writes=[Bc])

        def cc(name, j=0, w=1):
            return cst_t[:, C[name] + j:C[name] + j + w]

        ident = sb0("ident", [128, 128], BF16)
        ones_bf = sb0("ones_bf", [128, 128], BF16)
        ones_f = sb0("ones_f", [128, 128], F32)
        Bid, Bob, Bof = Buf("ident"), Buf("ones_bf"), Buf("ones_f")
        CP("dve", ident[:], cc("ident", 0, 128), [Bc], [Bid])
        MSET("pool", ones_bf[:], 1.0, [Bob])
        MSET("pool", ones_f[:], 1.0, [Bof])

        esX = es0.enter_context(contextlib.ExitStack())
        uT = esX.enter_context(nc.sbuf_tensor("uT", [128, 4, NQ + 30], BF16))
        Buf_u = [Buf("u%d" % i) for i in range(9)]
        Bupad = Buf("upad")
        MSET("pool", uT[:, :, 0:30], 0.0, [Bupad])
        Buf_at = [Buf("at%d" % i) for i in range(9)]
        B_KTs = [Buf("KTs%d" % i) for i in range(16)]
        B_Vs = [Buf("Vs%d" % i) for i in range(16)]
        B_QTs = [Buf("QTs%d" % i) for i in range(9)]
        B_x2 = [Buf("x2s%d" % i) for i in range(9)]
        stgs = []

        @contextlib.contextmanager
        def staging():
            with contextlib.ExitStack() as esS:
                sbS, _ = mk(esS)
                stgs[:] = [(sbS("stg%d" % i, [128, 1024], F32), Buf("stg%d" % i)) for i in range(3)]
                yield
                S.barrier()
            stgs[:] = []

        stg_n = [0]

        def load_w(dst, name, src, K, N, gain=None):
            bufs = []
            for kc in range(K // 128):
                for c0 in range(0, N, 1024):
                    c1 = min(N, c0 + 1024)
                    wd = c1 - c0
                    i = stg_n[0]
                    stg_n[0] += 1
                    st, Bst = stgs[i % 3]
                    S.dma("sp", st[:, 0:wd], src[kc * 128:(kc + 1) * 128, c0:c1], writes=[Bst])
                    b = Buf(name)
                    bufs.append(b)
                    eng = ("act", "dve", "pool")[i % 3]
                    o = dst[:, kc, c0:c1]
                    if gain is not None:
                        g = cc(gain, kc)
                        if eng == "act":
                            ACT(o, st[:, 0:wd], AF.Copy, [Bst, Bc], [b], scale=g)
                        else:
                            TS(eng, o, st[:, 0:wd], g, None, ALU.mult, None, [Bst, Bc], [b])
                    else:
                        CP(eng, o, st[:, 0:wd], [Bst], [b])
            return bufs

        def rms_bcast(xin, Bx, n, sq, Bsq, pbank, Bp, rs, Brs, rstd, Brstd, nfeat=1024):
            kcs = nfeat // 128
            ACT(sq[:, 0:kcs, 0:n], xin, AF.Square, Bx, [Bsq])
            for kc in range(kcs):
                MM(pbank[:, 0:n], ones_bf[:], sq[:, kc, 0:n], kc == 0, kc == kcs - 1, [Bob, Bsq], [Bp])
            ACT(rs[:, 0:n], pbank[:, 0:n], AF.Sqrt, [Bp, Bc], [Brs], scale=1.0 / nfeat, bias=cc("eps"))
            RCP(rstd[:, 0:n], rs[:, 0:n], [Brs], [Brstd])

        with contextlib.ExitStack() as es1:
            if True:
                esA = es1
                sb, ps = mk(esA)
                cos2 = sb("cos2", [128, 64, 32], F32)
                sin2 = sb("sin2", [128, 64, 32], F32)
                Btab = Buf("tab")
                with contextlib.ExitStack() as esT:
                    sbt, _ = mk(esT)
                    pos_i = sbt("pos_i", [128, 64], I32)
                    posf = sbt("posf", [128, 64], F32)
                    ang = sbt("ang", [128, 64, 2, 16], F32)
                    tf = sbt("tf", [128, 2048], F32)
                    ti_ = sbt("ti_", [128, 2048], I32)
                    sc = sbt("sc", [128, 64, 2, 16], F32)
                    Bt = Buf("t")
                    angf = ang[:].rearrange("p a b c -> p (a b c)")
                    S.dma("sp", pos_i[:], pos, writes=[Bt])
                    CP("dve", posf[:], pos_i[:], [Bt], [Bt])
                    TT("dve", ang[:, :, 0, :], posf[:].unsqueeze(2).to_broadcast([128, 64, 16]),
                       cc("invf", 0, 16).unsqueeze(1).to_broadcast([128, 64, 16]), ALU.mult, [Bt, Bc], [Bt])
                    TS("dve", ang[:, :, 1, :], ang[:, :, 0, :], PI / 2, None, ALU.add, None, [Bt], [Bt])
                    C1 = 6.28125
                    C2 = 2 * np.pi - 6.28125
                    TS("dve", tf[:], angf, 1.0 / (2 * np.pi), None, ALU.mult, None, [Bt], [Bt])
                    CP("dve", ti_[:], tf[:], [Bt], [Bt])
                    CP("dve", tf[:], ti_[:], [Bt], [Bt])
                    STT(angf, tf[:], -C1, angf, ALU.mult, ALU.add, [Bt], [Bt])
                    STT(angf, tf[:], -C2, angf, ALU.mult, ALU.add, [Bt], [Bt])
                    TS("dve", tf[:], angf, PI, -2 * PI, ALU.is_gt, ALU.mult, [Bt], [Bt])
                    TT("dve", angf, angf, tf[:], ALU.add, [Bt], [Bt])
                    TS("dve", tf[:], angf, -PI, 2 * PI, ALU.is_lt, ALU.mult, [Bt], [Bt])
                    TT("dve", angf, angf, tf[:], ALU.add, [Bt], [Bt])
                    ACT(sc[:].rearrange("p a b c -> p (a b c)"), angf, AF.Sin, [Bt], [Bt])
                    CP("dve", cos2[:, :, 0:16], sc[:, :, 1, :], [Bt], [Btab])
                    CP("dve", cos2[:, :, 16:32], sc[:, :, 1, :], [Bt, Btab], [Btab])
                    CP("dve", sin2[:, :, 16:32], sc[:, :, 0, :], [Bt, Btab], [Btab])
                    TS("dve", sin2[:, :, 0:16], sc[:, :, 0, :], -1.0, None, ALU.mult, None, [Bt, Btab], [Btab])
                    S.barrier()

                w_in_bf = sb("w_in_bf", [128, 8, 1440], BF16)
                w_uq_bf = sb("w_uq_bf", [128, 2, 768], BF16)
                w_ukv_bf = sb("w_ukv_bf", [128, 1, 1024], BF16)
                with staging():
                    Bwin = load_w(w_in_bf, "w_in_bf", w_in, D, 1440, gain="mixg")
                    Bwuq = load_w(w_uq_bf, "w_uq_bf", w_uq, 256, 768)
                    Bwukv = load_w(w_ukv_bf, "w_ukv_bf", w_ukv, 128, 1024)

                xbuf = [(sb("xg%d" % i, [128, 8, 512], F32), Buf("xg%d" % i)) for i in range(2)]
                sq = sb("sq", [128, 8, 512], BF16)
                Bsq = Buf("sq")
                hT = sb("hT", [128, 8, 512], BF16)
                BhT = Buf("hT")
                rs = sb("rs", [128, 512], F32)
                Brs = Buf("rs")
                rstd = sb("rstd", [128, 512], F32)
                Brstd = Buf("rstd")
                sig = sb("sig", [128, 512], F32)
                Bsig = Buf("sig")
                sqLs = [(sb("sqL%d" % i, [128, 384], F32), Buf("sqL%d" % i)) for i in range(2)]
                st2s = [(sb("st2%d" % i, [128, 8], F32), Buf("st2%d" % i)) for i in range(2)]
                cns = [(sb("cn%d" % i, [128, 384], BF16), Buf("cn%d" % i)) for i in range(2)]
                cTs = [(sb("cT%d" % i, [128, 384], BF16), Buf("cT%d" % i)) for i in range(2)]
                sqq = sb("sqq", [128, 768], F32)
                Bsqq = Buf("sqq")
                stqs = [(sb("stq%d" % i, [128, 24], F32), Buf("stq%d" % i)) for i in range(2)]
                rts = [(sb("rt1_%d" % i, [128, 8, 32], F32), sb("rt2_%d" % i, [128, 8, 32], F32), Buf("rt%d" % i)) for i in range(2)]
                qbfs = [(sb("qbf%d" % i, [128, 8, 96], BF16), Buf("qbf%d" % i)) for i in range(2)]
                kvs = [sb("kvs%d" % i, [128, 1056], F32) for i in range(2)]
                Bkv_k = [Buf("kvk%d" % i) for i in range(2)]
                Bkv_v = [Buf("kvv%d" % i) for i in range(2)]
                Bkv_r = [Buf("kvr%d" % i) for i in range(2)]
                qs = [sb("qs%d" % i, [128, 768], F32) for i in range(2)]
                Bqs = [Buf("qs%d" % i) for i in range(2)]
                sqk = sb("sqk", [128, 8, 64], F32)
                Bsqk = Buf("sqk")
                sqr = sb("sqr", [128, 32], F32)
                Bsqr = Buf("sqr")
                stks = [(sb("stk%d" % i, [128, 32], F32), Buf("stk%d" % i)) for i in range(2)]
                krs = [(sb("kr%d" % i, [128, 4, 32], F32), Buf("kr%d" % i)) for i in range(2)]
                kbfs = [(sb("kbf%d" % i, [128, 8, 96], BF16), Buf("kbf%d" % i)) for i in range(2)]
                KTst = [(sb("KTst%d" % i, [96, 8, 512], BF16), Buf("KTst%d" % i)) for i in range(1)]
                QTst = [(sb("QTst%d" % i, [96, 8, 512], BF16), Buf("QTst%d" % i)) for i in range(1)]
                Vst = [(sb("Vst%d" % i, [128, 8, 4, 128], BF16), Buf("Vst%d" % i)) for i in range(1)]
                for (v, bv) in Vst:
                    MSET("pool", v[:, 0:8:2, :, 64:128], 1.0, [bv])
                    MSET("pool", v[:, 1:8:2, :, 0:64], 1.0, [bv])
                pA = ps("pA", [128, 512])
                pG = ps("pG", [128, 512])
                pTk = ps("pTk", [128, 1024], BF16)
                pT = ps("pT", [128, 1024], BF16)
                pQ = ps("pQ", [128, 1024])
                pKV = ps("pKV", [128, 1024])
                BpA, BpG, BpTk, BpT, BpQ, BpKV = [Buf(n) for n in "pA pG pTk pT pQ pKV".split()]
                pLs = [(pA, BpA), (pG, BpG)]
                KT_v = KT_scr.rearrange("h d t -> d h t")
                QT_v = QT_scr.rearrange("h d t -> d h t")
                V_v = V_scr.rearrange("h p (t e) -> p h t e", e=128)
                R96 = 96 ** -0.5

                S.dma("sp", xbuf[0][0][:], xT_v[:, :, 0:512], writes=[xbuf[0][1]])
                for G in range(16):
                    own = G >= 8
                    xg, Bxg = xbuf[G % 2]
                    if G + 1 < 16:
                        S.dma("sp", xbuf[(G + 1) % 2][0][:], xT_v[:, :, (G + 1) * 512:(G + 2) * 512],
                              writes=[xbuf[(G + 1) % 2][1]])
                    rms_bcast(xg[:], [Bxg], 512, sq, Bsq, pA, BpA, rs, Brs, rstd, Brstd)
                    TT("dve", hT[:], xg[:], rstd[:].unsqueeze(1).to_broadcast([128, 8, 512]), ALU.mult,
                       [Bxg, Brstd], [BhT])
                    if own or G == 7:
                        c0, n = (0, 512) if own else (384, 128)
                        qg = (G - 7) if own else 0
                        ucol = 30 + (128 + (G - 8) * 512 if own else 0)
                        for c in range(4):
                            for kc in range(8):
                                MM(pA[:, 0:n], w_in_bf[:, kc, c * 128:(c + 1) * 128], hT[:, kc, c0:c0 + n],
                                   kc == 0, kc == 7, Bwin + [BhT], [BpA])
                            for kc in range(8):
                                MM(pG[:, 0:n], w_in_bf[:, kc, 512 + c * 128:512 + (c + 1) * 128], hT[:, kc, c0:c0 + n],
                                   kc == 0, kc == 7, Bwin + [BhT], [BpG])
                            ACT(sig[:, 0:n], pG[:, 0:n], AF.Sigmoid, [BpG, Bc], [Bsig], bias=cc("bcin", 4 + c))
                            STT(uT[:, c, ucol:ucol + n], pA[:, 0:n], cc("bcin", c), sig[:, 0:n], ALU.add, ALU.mult,
                                [BpA, Bsig, Bc], [Buf_u[qg]])
                            if not own:
                                TS("dve", uT[:, c, ucol:ucol + n], uT[:, c, ucol:ucol + n], cc("flag"), None,
                                   ALU.mult, None, [Buf_u[qg], Bc], [Buf_u[qg]])
                    KTs, BKTs = KTst[0]
                    QTs, BQTs = QTst[0]
                    Vs, BVs = Vst[0]
                    def lat_chain(ti):
                        t = 4 * G + ti
                        hasq = own or t == 31
                        tsl = slice(ti * 128, (ti + 1) * 128)
                        l0 = 0 if hasq else 256
                        pLx, BpLx = pLs[ti % 2]
                        sqLx, BsqLx = sqLs[ti % 2]
                        st2x, Bst2x = st2s[ti % 2]
                        cnx, Bcnx = cns[ti % 2]
                        cTx, BcTx = cTs[ti % 2]
                        for kc in range(8):
                            MM(pLx[:, l0:416], hT[:, kc, tsl], w_in_bf[:, kc, 1024 + l0:1440], kc == 0, kc == 7,
                               Bwin + [BhT], [BpLx])
                        yield
                        if hasq:
                            ACT(sqLx[:, 0:256], pLx[:, 0:256], AF.Square, [BpLx], [BsqLx], scale=1.0 / 16)
                            yield
                        ACT(sqLx[:, 256:384], pLx[:, 256:384], AF.Square, [BpLx], [BsqLx], scale=128 ** -0.5)
                        yield
                        if hasq:
                            RED(st2x[:, 0:1], sqLx[:, 0:256], [BsqLx], [Bst2x])
                            yield
                        RED(st2x[:, 1:2], sqLx[:, 256:384], [BsqLx], [Bst2x])
                        yield
                        ACT(st2x[:, 2 + l0 // 256:4], st2x[:, l0 // 256:2], AF.Sqrt, [Bst2x, Bc], [Bst2x], bias=cc("eps"))
                        yield
                        RCP(st2x[:, 4 + l0 // 256:6], st2x[:, 2 + l0 // 256:4], [Bst2x], [Bst2x])
                        yield
                        if hasq:
                            STT(cnx[:, 0:256], pLx[:, 0:256], st2x[:, 4:5], cc("gql", 0, 256), ALU.mult, ALU.mult,
                                [BpLx, Bst2x, Bc], [Bcnx])
                            yield
                        STT(cnx[:, 256:384], pLx[:, 256:384], st2x[:, 5:6], cc("gkvl", 0, 128), ALU.mult, ALU.mult,
                            [BpLx, Bst2x, Bc], [Bcnx])
                        yield
                        for j in ((0, 1, 2) if hasq else (2,)):
                            TR(pT[:, j * 128:(j + 1) * 128], cnx[:, j * 128:(j + 1) * 128], ident[:], [Bcnx, Bid], [BpT])
                        yield
                        CP("act", cTx[:, l0:384], pT[:, l0:384], [BpT], [BcTx])
                        yield

                    def kq_mm(ti):
                        t = 4 * G + ti
                        p = ti % 2
                        hasq = own or t == 31
                        pLx, BpLx = pLs[p]
                        cTx, BcTx = cTs[p]
                        for (a, b) in ((0, 512), (512, 1024)):
                            MM(pKV[:, a:b], cTx[:, 256:384], w_ukv_bf[:, 0, a:b], True, True, Bwukv + [BcTx], [BpKV])
                        yield
                        CP("act", kvs[p][:, 0:1024], pKV[:], [BpKV], [Bkv_k[p], Bkv_v[p]])
                        yield
                        CP("dve", kvs[p][:, 1024:1056], pLx[:, 384:416], [BpLx], [Bkv_r[p]])
                        yield
                        if hasq:
                            for (a, b) in ((0, 512), (512, 768)):
                                for kc in range(2):
                                    MM(pQ[:, a:b], cTx[:, kc * 128:(kc + 1) * 128], w_uq_bf[:, kc, a:b], kc == 0, kc == 1,
                                       Bwuq + [BcTx], [BpQ])
                            yield
                            CP("dve", qs[p][:], pQ[:, 0:768], [BpQ], [Bqs[p]])
                            yield

                    def k_chain(ti):
                        t = 4 * G + ti
                        p = ti % 2
                        tsl = slice(ti * 128, (ti + 1) * 128)
                        kvv = kvs[p][:, 0:1024].rearrange("p (h e) -> p h e", h=8)
                        krr = kvs[p][:, 1024:1056]
                        stkx, Bstkx = stks[p]
                        krx, Bkrx = krs[p]
                        kbfx, Bkbfx = kbfs[p]
                        TT("pool", krx[:, 0, :], krr, cc("gkn", 64, 32), ALU.mult, [Bkv_r[p], Bc], [Bkrx])
                        yield
                        ACT(sqk[:], kvv[:, :, 0:64], AF.Square, [Bkv_k[p]], [Bsqk], scale=R96)
                        yield
                        ACT(sqr[:], krr, AF.Square, [Bkv_r[p]], [Bsqr], scale=R96)
                        yield
                        TT("pool", krx[:, 1, :], krx[:, 0, :], cos2[:, t, :], ALU.mult, [Bkrx, Btab], [Bkrx])
                        yield
                        RED(stkx[:, 0:8], sqk[:], [Bsqk], [Bstkx])
                        yield
                        RED(stkx[:, 8:9], sqr[:], [Bsqr], [Bstkx])
                        yield
                        TT("pool", krx[:, 2, 0:16], krx[:, 0, 16:32], sin2[:, t, 0:16], ALU.mult, [Bkrx, Btab], [Bkrx])
                        yield
                        TS("dve", stkx[:, 9:10], stkx[:, 8:9], EPS, None, ALU.add, None, [Bstkx], [Bstkx])
                        yield
                        CP("act", Vs[:, 0:8:2, ti, 0:64], kvv[:, 0:8:2, 64:128], [Bkv_v[p]], [BVs])
                        yield
                        TT("pool", krx[:, 2, 16:32], krx[:, 0, 0:16], sin2[:, t, 16:32], ALU.mult, [Bkrx, Btab], [Bkrx])
                        yield
                        ACT(stkx[:, 16:24], stkx[:, 0:8], AF.Sqrt, [Bstkx], [Bstkx], bias=stkx[:, 9:10])
                        yield
                        TT("pool", krx[:, 3, :], krx[:, 1, :], krx[:, 2, :], ALU.add, [Bkrx], [Bkrx])
                        yield
                        RCP(stkx[:, 24:32], stkx[:, 16:24], [Bstkx], [Bstkx])
                        yield
                        CP("act", Vs[:, 1:8:2, ti, 64:128], kvv[:, 1:8:2, 64:128], [Bkv_v[p]], [BVs])
                        yield
                        rk = stkx[:, 24:32]
                        TT("dve", kvv[:, :, 0:64], kvv[:, :, 0:64], rk.unsqueeze(2).to_broadcast([128, 8, 64]), ALU.mult,
                           [Bkv_k[p], Bstkx], [Bkv_k[p]])
                        yield
                        TT("dve", kbfx[:, :, 64:96], krx[:, 3:4, :].to_broadcast([128, 8, 32]),
                           rk.unsqueeze(2).to_broadcast([128, 8, 32]), ALU.mult, [Bkrx, Bstkx], [Bkbfx])
                        yield
                        TT("pool", kbfx[:, :, 0:64], kvv[:, :, 0:64], cc("gkn", 0, 64).unsqueeze(1).to_broadcast([128, 8, 64]),
                           ALU.mult, [Bkv_k[p], Bc], [Bkbfx])
                        yield
                        for h in range(8):
                            TR(pTk[0:96, h * 128:(h + 1) * 128], kbfx[:, h, :], ident[:], [Bkbfx, Bid], [BpTk])
                        yield
                        CP("act", KTs[:, :, tsl], pTk[0:96, :].rearrange("p (h t) -> p h t", h=8), [BpTk], [BKTs])
                        yield

                    def q_chain(ti):
                        t = 4 * G + ti
                        p = ti % 2
                        tsl = slice(ti * 128, (ti + 1) * 128)
                        stqx, Bstqx = stqs[p]
                        rt1x, rt2x, Brtx = rts[p]
                        qbfx, Bqbfx = qbfs[p]
                        qv = qs[p][:].rearrange("p (h e) -> p h e", h=8)
                        ACT(sqq[:], qs[p][:], AF.Square, [Bqs[p]], [Bsqq], scale=R96)
                        yield
                        RED(stqx[:, 0:8], sqq[:].rearrange("p (h e) -> p h e", h=8), [Bsqq], [Bstqx])
                        yield
                        ACT(stqx[:, 8:16], stqx[:, 0:8], AF.Sqrt, [Bstqx, Bc], [Bstqx], bias=cc("eps"))
                        yield
                        RCP(stqx[:, 16:24], stqx[:, 8:16], [Bstqx], [Bstqx])
                        yield
                        TT("dve", qv, qv, stqx[:, 16:24].unsqueeze(2).to_broadcast([128, 8, 96]), ALU.mult,
                           [Bqs[p], Bstqx], [Bqs[p]])
                        yield
                        TT("pool", qv, qv, cc("gqn", 0, 96).unsqueeze(1).to_broadcast([128, 8, 96]), ALU.mult,
                           [Bqs[p], Bc], [Bqs[p]])
                        yield
                        TT("pool", rt1x[:], qv[:, :, 64:96], cos2[:, t:t + 1, :].to_broadcast([128, 8, 32]), ALU.mult,
                           [Bqs[p], Btab], [Brtx])
                        yield
                        TT("dve", rt2x[:, :, 0:16], qv[:, :, 80:96], sin2[:, t:t + 1, 0:16].to_broadcast([128, 8, 16]),
                           ALU.mult, [Bqs[p], Btab, Brtx], [Brtx])
                        yield
                        TT("pool", rt2x[:, :, 16:32], qv[:, :, 64:80], sin2[:, t:t + 1, 16:32].to_broadcast([128, 8, 16]),
                           ALU.mult, [Bqs[p], Btab, Brtx], [Brtx])
                        yield
                        CP("act", qbfx[:, :, 0:64], qv[:, :, 0:64], [Bqs[p]], [Bqbfx])
                        yield
                        TT("dve", qbfx[:, :, 64:96], rt1x[:], rt2x[:], ALU.add, [Brtx], [Bqbfx])
                        yield
                        for h in range(8):
                            TR(pT[0:96, h * 128:(h + 1) * 128], qbfx[:, h, :], ident[:], [Bqbfx, Bid], [BpT])
                        yield
                        qsl = tsl if own else slice(0, 128)
                        CP("dve", QTs[:, :, qsl], pT[0:96, :].rearrange("p (h t) -> p h t", h=8), [BpT], [BQTs])
                        yield
                        if not own:
                            S.dma("sp", QT_v[:, :, 0:128], QTs[:, :, 0:128], reads=[BQTs], writes=[B_QTs[0]])

                    def run_dyn():
                        done = set()
                        active = []
                        pending = []

                        def wrap(name, gen):
                            yield from gen
                            done.add(name)

                        for ti_ in range(4):
                            hq_ = own or (4 * G + ti_) == 31
                            prev2 = set()
                            if ti_ >= 2:
                                prev2.add(("k", ti_ - 2))
                                if own or (4 * G + ti_ - 2) == 31:
                                    prev2.add(("q", ti_ - 2))
                            lat_need = set()
                            if ti_ >= 1:
                                lat_need.add(("lat", ti_ - 1))
                            if ti_ >= 2:
                                lat_need.add(("kq", ti_ - 2))
                            pending.append((lat_need, ("lat", ti_), lat_chain(ti_)))
                            pending.append(({("lat", ti_)} | prev2, ("kq", ti_), kq_mm(ti_)))
                            pending.append(({("kq", ti_)}, ("k", ti_), k_chain(ti_)))
                            if hq_:
                                pending.append(({("kq", ti_)}, ("q", ti_), q_chain(ti_)))
                        while active or pending:
                            for it in list(pending):
                                if it[0] <= done:
                                    pending.remove(it)
                                    active.append(wrap(it[1], it[2]))
                            for g_ in list(active):
                                try:
                                    next(g_)
                                except StopIteration:
                                    active.remove(g_)

                    run_dyn()
                    S.dma("sp", KT_v[:, :, G * 512:(G + 1) * 512], KTs[:], reads=[BKTs], writes=[B_KTs[G]])
                    S.dma("sp", V_v[:, :, 4 * G:4 * G + 4, :], Vs[:], reads=[BVs], writes=[B_Vs[G]])
                    if own:
                        q0 = 128 + (G - 8) * 512
                        S.dma("sp", QT_v[:, :, q0:q0 + 512], QTs[:], reads=[BQTs], writes=[B_QTs[G - 7]])
                S.barrier()
        if dbg:
            S.dma("sp", dbg_out["uT"], uT[:], reads=Buf_u + [Bupad])
        attnT = esX.enter_context(nc.sbuf_tensor("attnT", [128, 4, NQ], BF16))
        memK = esX.enter_context(nc.sbuf_tensor("memK", [128, 8, 256], BF16))
        memV = esX.enter_context(nc.sbuf_tensor("memV", [128, 2, 1024], BF16))
        BmK, BmV = Buf("memK"), Buf("memV")
        if True:

            if stop_after >= 2:
                with contextlib.ExitStack() as esB:
                    sb, ps = mk(esB)
                    Dg = sb("Dg", [128, 4, 31, 128], BF16)
                    BDg = [Buf("Dg%d" % c) for c in range(4)]
                    for c in range(4):
                        for k in range(31):
                            TS(("dve", "pool")[k % 2], Dg[:, c, k, :], ident[:], cc("wdw", c * 31 + k), None, ALU.mult, None,
                               [Bid, Bc], [BDg[c]])
                    pY = [(ps("pY%d" % i, [128, 512]), Buf("pY%d" % i)) for i in range(2)]
                    pM = ps("pM", [128, 512])
                    pM2 = ps("pM2", [128, 512])
                    BpM, BpM2 = Buf("pM"), Buf("pM2")
                    ybf = sb("ybf", [128, 4, 512], BF16)
                    ysq = sb("ysq", [128, 4, 512], BF16)
                    Bybf = [Buf("ybf%d" % c) for c in range(4)]
                    Bysq = [Buf("ysq%d" % c) for c in range(4)]
                    mean = sb("mean", [128, 512], F32)
                    m2 = sb("m2", [128, 512], F32)
                    var = sb("var", [128, 512], F32)
                    rsd = sb("rsd", [128, 512], F32)
                    Bmean, Bm2, Bvar, Brsd = Buf("mean"), Buf("m2"), Buf("var"), Buf("rsd")
                    tmp = [(sb("lt%d" % i, [128, 512], F32), Buf("lt%d" % i)) for i in range(2)]
                    w_mkv_bf = sb("w_mkv_bf", [128, 8, 2 * D], BF16)
                    stgM = [(sb("stgM%d" % i, [128, 1024], F32), Buf("stgM%d" % i)) for i in range(3)]
                    pzm = [(ps("pzm%d" % i, [128, 512]), Buf("pz_m%d" % i)) for i in range(4)]
                    msq = sb("sqm", [128, 8, 256], BF16)
                    Bmsq = Buf("sqm")
                    mrs = sb("rsm", [128, 512], F32)
                    Bmrs = Buf("rsm")
                    mrstd = sb("rstdm", [128, 512], F32)
                    Bmrstd = Buf("rstdm")
                    mx = sb("mx", [128, 8, 256], F32)
                    Bmx = Buf("mx")
                    hm = sb("hm", [128, 8, 256], BF16)
                    Bhm = Buf("hm")
                    ksq = sb("ksq", [128, 2, 256], BF16)
                    Bksq = Buf("ksq")
                    Bwmkv = []
                    mkv_pieces = [(kc, c0) for kc in range(8) for c0 in (0, 1024)]

                    def load_mkv_piece():
                        if not mkv_pieces:
                            return
                        kc, c0 = mkv_pieces.pop(0)
                        i = len(mkv_pieces)
                        st, Bst = stgM[i % 3]
                        S.dma("sp", st[:], w_mkv[kc * 128:(kc + 1) * 128, c0:c0 + 1024], writes=[Bst])
                        b_ = Buf("wmkv")
                        Bwmkv.append(b_)
                        TS(("dve", "pool")[i % 2], w_mkv_bf[:, kc, c0:c0 + 1024], st[:], cc("memmg", kc), None, ALU.mult, None,
                           [Bst, Bc], [b_])

                    S.dma("sp", mx[:], memT.rearrange("(kc p) t -> p kc t", p=128), writes=[Bmx])
                    yi = 0
                    for g in reversed(range(9)):
                        n = 128 if g == 0 else 512
                        oc = 0 if g == 0 else 128 + (g - 1) * 512
                        load_mkv_piece()
                        load_mkv_piece()
                        rd = [Bupad] + Buf_u[max(0, g - 1):g + 1]
                        for c in range(4):
                            py, Bpy = pY[yi % 2]
                            yi += 1
                            for k in range(31):
                                MM(py[:, 0:n], Dg[:, c, k, :], uT[:, c, oc + k:oc + k + n], k == 0, k == 30, rd + [BDg[c]], [Bpy])
                            ACT(ybf[:, c, 0:n], py[:, 0:n], AF.Identity, [Bpy, Bc], [Bybf[c]], bias=cc("bdw", c))
                            ACT(ysq[:, c, 0:n], py[:, 0:n], AF.Square, [Bpy, Bc], [Bysq[c]], bias=cc("bdw", c))
                        for c in range(4):
                            MM(pM[:, 0:n], ones_bf[:], ybf[:, c, 0:n], c == 0, c == 3, [Bob, Bybf[c]], [BpM])
                        for c in range(4):
                            MM(pM2[:, 0:n], ones_bf[:], ysq[:, c, 0:n], c == 0, c == 3, [Bob, Bysq[c]], [BpM2])
                        TS("dve", mean[:, 0:n], pM[:, 0:n], 1.0 / 512, None, ALU.mult, None, [BpM], [Bmean])
                        TT("dve", m2[:, 0:n], mean[:, 0:n], mean[:, 0:n], ALU.mult, [Bmean], [Bm2])
                        STT(var[:, 0:n], pM2[:, 0:n], 1.0 / 512, m2[:, 0:n], ALU.mult, ALU.subtract, [BpM2, Bm2], [Bvar])
                        ACT(var[:, 0:n], var[:, 0:n], AF.Sqrt, [Bvar, Bc], [Bvar], bias=cc("eps"))
                        RCP(rsd[:, 0:n], var[:, 0:n], [Bvar], [Brsd])
                        for c in range(4):
                            tm, Btm = tmp[c % 2]
                            TT("dve", tm[:, 0:n], ybf[:, c, 0:n], mean[:, 0:n], ALU.subtract, [Bybf[c], Bmean], [Btm])
                            TT("pool", tm[:, 0:n], tm[:, 0:n], rsd[:, 0:n], ALU.mult, [Btm, Brsd], [Btm])
                            ACT(uT[:, c, 30 + oc:30 + oc + n], tm[:, 0:n], AF.Silu, [Btm, Bc], [Buf_u[g]],
                                scale=cc("lng", c), bias=cc("lnb", c))
                    if stop_after >= 3:
                        while mkv_pieces:
                            load_mkv_piece()
                        rms_bcast(mx[:], [Bmx], 256, msq, Bmsq, pzm[0][0], pzm[0][1], mrs, Bmrs, mrstd, Bmrstd)
                        TT("dve", hm[:], mx[:], mrstd[:, 0:256].unsqueeze(1).to_broadcast([128, 8, 256]), ALU.mult,
                           [Bmx, Bmrstd], [Bhm])
                        for hd in range(4):
                            for j in range(2):
                                ch = hd * 2 + j
                                p_, Bp_ = pzm[1 + j]
                                for kc in range(8):
                                    MM(p_[:, 0:256], w_mkv_bf[:, kc, ch * 128:(ch + 1) * 128], hm[:, kc, :], kc == 0, kc == 7,
                                       Bwmkv + [Bhm], [Bp_])
                                ACT(ksq[:, j, :], p_[:, 0:256], AF.Square, [Bp_], [Bksq], scale=1.0 / 16)
                            for j in range(2):
                                MM(pzm[3][0][:, 0:256], ones_bf[:], ksq[:, j, :], j == 0, j == 1, [Bob, Bksq], [pzm[3][1]])
                            ACT(mrs[:, 0:256], pzm[3][0][:, 0:256], AF.Sqrt, [pzm[3][1], Bc], [Bmrs], bias=cc("eps"))
                            RCP(mrstd[:, 0:256], mrs[:, 0:256], [Bmrs], [Bmrstd])
                            for j in range(2):
                                STT(memK[:, hd * 2 + j, :], pzm[1 + j][0][:, 0:256], cc("mkg", j), mrstd[:, 0:256], ALU.mult, ALU.mult,
                                    [pzm[1 + j][1], Bmrstd, Bc], [BmK])
                        for kt in range(2):
                            for nb in range(2):
                                p_, Bp_ = pzm[(kt * 2 + nb) % 2 + 1]
                                for kc in range(8):
                                    MM(p_[:], hm[:, kc, kt * 128:(kt + 1) * 128], w_mkv_bf[:, kc, 1024 + nb * 512:1024 + (nb + 1) * 512],
                                       kc == 0, kc == 7, Bwmkv + [Bhm], [Bp_])
                                CP("act", memV[:, kt, nb * 512:(nb + 1) * 512], p_[:], [Bp_], [BmV])
                    S.barrier()
                if dbg:
                    S.dma("sp", dbg_out["ufT"], uT[:, :, 30:30 + NQ], reads=Buf_u)

        w_out_bf = esX.enter_context(nc.sbuf_tensor("w_out_bf", [128, 8, D], BF16))
        w_mq_bf = esX.enter_context(nc.sbuf_tensor("w_mq_bf", [128, 8, D], BF16))
        w_mo_bf = esX.enter_context(nc.sbuf_tensor("w_mo_bf", [128, 8, D], BF16))
        stgP = (esX.enter_context(nc.sbuf_tensor("stgP", [128, 512], F32)), Buf("stgP"))
        if stop_after >= 3:
            with contextlib.ExitStack() as es2:
                sb, ps = mk(es2)
                kT = [(sb("kT%d" % i, [96, SEQ], BF16), Buf("kT%d" % i)) for i in range(2)]
                vA = [(sb("vA%d" % i, [128, 64, 128], BF16), Buf("vA%d" % i)) for i in range(2)]
                qT = [(sb("qT%d" % i, [96, NQ], BF16), Buf("qT%d" % i)) for i in range(1)] * 2
                NP = 3
                Pb = [(sb("P%d" % i, [128, 512], BF16), Buf("P%d" % i)) for i in range(NP)]
                pS = [(ps("pS%d" % i, [128, 512]), Buf("pS%d" % i)) for i in range(4)]
                pO = [(ps("pO%d" % i, [128, 512]), Buf("pO%d" % i)) for i in range(2)]
                pB = ps("pB", [128, 512])
                BpB = Buf("pB")
                rrow = sb("rrow", [128, 512], F32)
                Brrow = Buf("rrow")
                bcs, Bbcs = rrow, Brrow
                Bscr = Buf("scr")
                SC = 96 ** -0.5

                def load_head(h):
                    hb = h % 2
                    S.dma("sp", kT[hb][0][:], KT_scr[h], reads=B_KTs, writes=[kT[hb][1]])
                    S.dma("sp", vA[hb][0][:], V_scr[h].rearrange("p (t e) -> p t e", e=128), reads=B_Vs, writes=[vA[hb][1]])

                def load_q(h):
                    S.dma("sp", qT[0][0][:], QT_scr[h], reads=B_QTs, writes=[qT[0][1]])

                load_head(0)
                load_q(0)
                sidx = 0
                oidx = 0
                Bwo, Bwmq, Bwmo = [], [], []
                pf = []
                for (dst_, src_, gain_, lst_) in ((w_out_bf, w_out, None, Bwo), (w_mq_bf, w_mq, "memxg", Bwmq),
                                                   (w_mo_bf, w_mo, None, Bwmo)):
                    for kc_ in range(8):
                        for hc_ in range(2):
                            pf.append((dst_, src_, gain_, lst_, kc_, hc_))

                def prefetch_piece():
                    if not pf:
                        return
                    dst_, src_, gain_, lst_, kc_, hc_ = pf.pop(0)
                    st_, Bst_ = stgP
                    cs_ = slice(hc_ * 512, (hc_ + 1) * 512)
                    S.dma("sp", st_[:], src_[kc_ * 128:(kc_ + 1) * 128, cs_], writes=[Bst_])
                    b_ = Buf("wpf")
                    lst_.append(b_)
                    eng_ = "pool" if len(pf) % 2 else "dve"
                    if gain_ is not None:
                        TS(eng_, dst_[:, kc_, cs_], st_[:], cc(gain_, kc_), None, ALU.mult, None, [Bst_, Bc], [b_])
                    else:
                        CP(eng_, dst_[:, kc_, cs_], st_[:], [Bst_], [b_])
                for h in range(8):
                    hb = h % 2
                    if h + 1 < 8:
                        load_head(h + 1)
                    k_t, Bk = kT[hb]
                    v_t, Bv = vA[hb]
                    q_t, Bq = qT[hb]
                    sr = 64 if h % 2 == 0 else 0
                    orow = 0 if h % 2 == 0 else 64
                    for g in range(9):
                        if g == 0:
                            N, q0, nk = 128, 0, 32
                            steps = [(t, 0, False) for t in range(31)] + [(31, 0, True)]
                        else:
                            N, q0, nk = 512, 128 + 512 * (g - 1), 32 + 4 * g
                            steps = [(t, 0, False) for t in range(nk - 4)] + [(nk - 4 + d, 128 * d, True) for d in range(4)]
                        po, Bpo = pO[oidx % 2]
                        oidx += 1
                        ns = len(steps)
                        prefetch_piece()

                        def emit_S(i):
                            t, cs, dg = steps[i]
                            p_s, Bps = pS[(sidx + i) % 4]
                            MM(p_s[:, cs:N], k_t[:, t * 128:(t + 1) * 128], q_t[:, q0 + cs:q0 + N], True, True, [Bk, Bq], [Bps])

                        for i in range(min(3, ns)):
                            emit_S(i)
                        for i in range(ns):
                            t, cs, dg = steps[i]
                            p_s, Bps = pS[(sidx + i) % 4]
                            pb, Bpb = Pb[(sidx + i) % NP]
                            ACT(pb[:, cs:N], p_s[:, cs:N], AF.Exp, [Bps, Bc], [Bpb], scale=SC, bias=cc("kbias", t))
                            if dg:
                                MSET("pool", pb[64:128, cs:cs + 64], 0.0, [Bpb])
                            MM(po[:, cs:N], v_t[:, t, :], pb[:, cs:N], i == 0, i == ns - 1, [Bv, Bpb], [Bpo])
                            if i + 3 < ns:
                                emit_S(i + 3)
                        sidx += ns
                        if g == 0:
                            TS("dve", rrow[sr:sr + 1, 0:N], po[sr:sr + 1, 0:N], 1e-30, None, ALU.max, None, [Bpo], [Brrow])
                            RCP(rrow[sr:sr + 1, 0:N], rrow[sr:sr + 1, 0:N], [Brrow], [Brrow])
                        else:
                            RCP(rrow[sr:sr + 1, 0:N], po[sr:sr + 1, 0:N], [Bpo], [Brrow])
                        MM(pB[:, 0:N], ones_f[sr:sr + 1, :], rrow[sr:sr + 1, 0:N], True, True, [Bof, Brrow], [BpB])
                        CP("dve", bcs[orow:orow + 64, 0:N], pB[orow:orow + 64, 0:N], [BpB], [Bbcs])
                        TT("dve", attnT[orow:orow + 64, h // 2, q0:q0 + N], po[orow:orow + 64, 0:N], bcs[orow:orow + 64, 0:N],
                           ALU.mult, [Bpo, Bbcs], [Buf_at[g]])
                    if h + 1 < 8:
                        load_q(h + 1)
                S.barrier()
            if dbg:
                S.dma("sp", dbg_out["attnT"], attnT[:], reads=Buf_at)

        if stop_after >= 4:
            with contextlib.ExitStack() as es3:
                sb, ps = mk(es3)
                pz = [(ps("pz%d" % i, [128, 512]), Buf("pz%d" % i)) for i in range(8)]
                hq = sb("hq", [128, 8, 512], BF16)
                Bhq = Buf("hq")
                sq, Bsq = hq, Bhq
                rs = sb("rs3", [128, 512], F32)
                Brs = Buf("rs3")
                rstd = sb("rstd3", [128, 512], F32)
                Brstd = Buf("rstd3")
                xb3 = [(sb("x3_%d" % i, [128, 8, 512], F32), Buf("x3_%d" % i)) for i in range(2)]
                qraws = [(sb("qraw%d" % i, [128, 2, 512], BF16), Buf("qraw%d" % i)) for i in range(2)]
                qsqs = [(sb("qsq%d" % i, [128, 2, 512], BF16), Buf("qsq%d" % i)) for i in range(2)]
                qmns = [(sb("qmn%d" % i, [128, 2, 512], BF16), Buf("qmn%d" % i)) for i in range(2)]
                pms = [[(sb("pm%d_%d" % (l_, i), [128, 512], BF16), Buf("pm%d_%d" % (l_, i))) for i in range(2)] for l_ in range(2)]
                rss = [(sb("rsx%d" % i, [128, 512], F32), Buf("rsx%d" % i)) for i in range(2)]
                rcss = [(sb("rcs%d" % i, [128, 512], F32), Buf("rcs%d" % i)) for i in range(2)]
                om = sb("om", [128, 8, 512], BF16)
                Bom = [Buf("om%d" % i) for i in range(4)]

                def xcols(g):
                    return (3968, 128) if g == 0 else (4096 + (g - 1) * 512, 512)

                c_, n_ = xcols(0)
                S.dma("sp", xb3[0][0][:, :, 0:n_], xT_v[:, :, c_:c_ + n_], writes=[xb3[0][1]])
                for g in range(9):
                    n = 128 if g == 0 else 512
                    oc = 0 if g == 0 else 128 + (g - 1) * 512
                    xg, Bxg = xb3[g % 2]
                    if g + 1 < 9:
                        c_, n_ = xcols(g + 1)
                        S.dma("sp", xb3[(g + 1) % 2][0][:, :, 0:n_], xT_v[:, :, c_:c_ + n_], writes=[xb3[(g + 1) % 2][1]])
                    def outproj_chain(g2):
                        n2 = 128 if g2 == 0 else 512
                        oc2 = 0 if g2 == 0 else 128 + (g2 - 1) * 512
                        xg2, Bxg2 = xb3[g2 % 2]
                        for dc in range(8):
                            p_, Bp_ = pz[dc % 2]
                            for kc in range(8):
                                if kc < 4:
                                    rhs_, bsrc = uT[:, kc, 30 + oc2:30 + oc2 + n2], Buf_u[g2]
                                else:
                                    rhs_, bsrc = attnT[:, kc - 4, oc2:oc2 + n2], Buf_at[g2]
                                MM(p_[:, 0:n2], w_out_bf[:, kc, dc * 128:(dc + 1) * 128], rhs_, kc == 0, kc == 7,
                                   Bwo + [bsrc], [Bp_])
                            yield
                            TT("dve", xg2[:, dc, 0:n2], xg2[:, dc, 0:n2], p_[:, 0:n2], ALU.add, [Bxg2, Bp_], [Bxg2])
                            yield

                    if g == 0:
                        for _ in outproj_chain(0):
                            pass
                    rms_bcast(xg[:, :, 0:n], [Bxg], n, sq, Bsq, pz[0][0], pz[0][1], rs, Brs, rstd, Brstd)
                    TT("dve", hq[:, :, 0:n], xg[:, :, 0:n], rstd[:, 0:n].unsqueeze(1).to_broadcast([128, 8, n]), ALU.mult,
                       [Bxg, Brstd], [Bhq])
                    def head_chain(hd):
                        L = hd % 2
                        pq_, Bpq_ = pz[2 + 3 * L]
                        pn_, Bpn_ = pz[3 + 3 * L]
                        psc, Bpsc = pz[4 + 3 * L]
                        qraw, Bqraw = qraws[L]
                        qsq, Bqsq = qsqs[L]
                        qmn, Bqmn = qmns[L]
                        rsx, Brsx = rss[L]
                        rcs, Brcs = rcss[L]
                        for j in range(2):
                            ch = hd * 2 + j
                            for kc in range(8):
                                MM(pq_[:, 0:n], w_mq_bf[:, kc, ch * 128:(ch + 1) * 128], hq[:, kc, 0:n], kc == 0, kc == 7,
                                   Bwmq + [Bhq], [Bpq_])
                            yield
                            ACT(qsq[:, j, 0:n], pq_[:, 0:n], AF.Square, [Bpq_], [Bqsq], scale=1.0 / 16)
                            yield
                            CP("dve", qraw[:, j, 0:n], pq_[:, 0:n], [Bpq_], [Bqraw])
                            yield
                        for j in range(2):
                            MM(pn_[:, 0:n], ones_bf[:], qsq[:, j, 0:n], j == 0, j == 1, [Bob, Bqsq], [Bpn_])
                        yield
                        ACT(rsx[:, 0:n], pn_[:, 0:n], AF.Sqrt, [Bpn_, Bc], [Brsx], bias=cc("eps"))
                        yield
                        RCP(rsx[:, 0:n], rsx[:, 0:n], [Brsx], [Brsx])
                        yield
                        for j in range(2):
                            STT(qmn[:, j, 0:n], qraw[:, j, 0:n], cc("mqg", j), rsx[:, 0:n], ALU.mult, ALU.mult,
                                [Bqraw, Brsx, Bc], [Bqmn])
                            yield
                        for kt in range(2):
                            for j in range(2):
                                MM(psc[:, 0:n], memK[:, hd * 2 + j, kt * 128:(kt + 1) * 128], qmn[:, j, 0:n], j == 0, j == 1,
                                   [BmK, Bqmn], [Bpsc])
                            yield
                            ACT(pms[L][kt][0][:, 0:n], psc[:, 0:n], AF.Exp, [Bpsc], [pms[L][kt][1]], scale=1.0 / 16)
                            yield
                        for kt in range(2):
                            MM(pn_[:, 0:n], ones_bf[:], pms[L][kt][0][:, 0:n], kt == 0, kt == 1, [Bob, pms[L][kt][1]], [Bpn_])
                        yield
                        RCP(rcs[:, 0:n], pn_[:, 0:n], [Bpn_], [Brcs])
                        yield
                        for j in range(2):
                            for kt in range(2):
                                MM(pq_[:, 0:n], memV[:, kt, (hd * 2 + j) * 128:(hd * 2 + j + 1) * 128], pms[L][kt][0][:, 0:n],
                                   kt == 0, kt == 1, [BmV, pms[L][kt][1]], [Bpq_])
                            yield
                            TT("dve", om[:, hd * 2 + j, 0:n], pq_[:, 0:n], rcs[:, 0:n], ALU.mult, [Bpq_, Brcs], [Bom[hd]])
                            yield

                    def rr3(*gens):
                        gens = list(gens)
                        while gens:
                            for g_ in list(gens):
                                try:
                                    next(g_)
                                except StopIteration:
                                    gens.remove(g_)

                    nxt = [outproj_chain(g + 1)] if g + 1 < 9 else []
                    rr3(head_chain(0), head_chain(1), *nxt)
                    rr3(head_chain(2), head_chain(3))
                    for dc in range(8):
                        p_, Bp_ = pz[dc % 2]
                        for kc in range(8):
                            MM(p_[:, 0:n], w_mo_bf[:, kc, dc * 128:(dc + 1) * 128], om[:, kc, 0:n], kc == 0, kc == 7,
                               Bwmo + [Bom[kc // 2]], [Bp_])
                        TT("dve", xg[:, dc, 0:n], xg[:, dc, 0:n], p_[:, 0:n], ALU.add, [Bxg, Bp_], [Bxg])
                    S.dma("sp", x2_v[:, :, oc:oc + n], xg[:, :, 0:n], reads=[Bxg], writes=[B_x2[g]])
                S.barrier()
        esX.close()

        if stop_after >= 5:
            with contextlib.ExitStack() as es4:
                sb, ps = mk(es4)
                w_up_bf = sb("w_up_bf", [128, 8, 2 * DFF], BF16)
                w_dn_bf = sb("w_dn_bf", [128, 22, D], BF16)
                stg4 = [(sb("stg4_%d" % i, [128, 512], F32), Buf("stg4_%d" % i)) for i in range(3)]
                x4 = sb("x4", [128, 8, 512], F32)
                Bx4 = Buf("x4")
                x4h = sb("x4h", [128, 8, 2], F32)
                Bx4h = Buf("x4h")
                h3 = sb("h3", [128, 8, 512], BF16)
                Bh3 = Buf("h3")
                h3h = sb("h3h", [128, 8, 2], BF16)
                Bh3h = Buf("h3h")
                sq, Bsq = h3, Bh3
                rs = sb("rs4", [128, 512], F32)
                Brs = Buf("rs4")
                rstd, Brstd = rs, Brs
                prev = sb("prev", [128, 44, 2], F32)
                Bprev = [Buf("prev%d" % r) for r in range(44)]
                upx = [(sb("upx%d" % i, [128, 514], F32), Buf("upx%d" % i)) for i in range(2)]
                yv = [(sb("yv%d" % i, [128, 512], F32), Buf("yv%d" % i)) for i in range(3)]
                actT = sb("actT", [128, 22, 512], BF16)
                Bact = [Buf("act%d" % j) for j in range(22)]
                ost = [(sb("ost%d" % i, [128, 512], F32), Buf("ost%d" % i)) for i in range(2)]
                pu = [(ps("pu%d" % i, [128, 512]), Buf("pu%d" % i)) for i in range(4)]
                pd = [(ps("pd%d" % i, [128, 512]), Buf("pd%d" % i)) for i in range(2)]
                pss = ps("pss", [128, 512])
                Bpss = Buf("pss")
                Bup = {}
                Bdn = {}
                s4 = [0]

                def load_up_block(c0_, c1_):
                    for c0 in range(c0_, c1_, 512):
                        _load_up_piece(c0, min(c1_, c0 + 512))

                def _load_up_piece(c0, c1):
                    for kc in range(8):
                        i = s4[0]
                        s4[0] += 1
                        st, Bst = stg4[i % 3]
                        S.dma("sp", st[:, 0:c1 - c0], w_up[kc * 128:(kc + 1) * 128, c0:c1], writes=[Bst])
                        b_ = Buf("wup")
                        eng = ("pool", "dve", "act")[i % 3]
                        o = w_up_bf[:, kc, c0:c1]
                        if eng == "act":
                            ACT(o, st[:, 0:c1 - c0], AF.Copy, [Bst, Bc], [b_], scale=cc("ffng", kc))
                        else:
                            TS(eng, o, st[:, 0:c1 - c0], cc("ffng", kc), None, ALU.mult, None, [Bst, Bc], [b_])
                        for r in range(c0 // 128, c1 // 128):
                            Bup.setdefault(r, []).append(b_)

                def load_dn(kc):
                    Bdn[kc] = []
                    for hc in range(2):
                        i = s4[0]
                        s4[0] += 1
                        st, Bst = stg4[i % 3]
                        S.dma("sp", st[:], w_dn[kc * 128:(kc + 1) * 128, hc * 512:(hc + 1) * 512], writes=[Bst])
                        b_ = Buf("wdn")
                        CP(("pool", "dve", "act")[i % 3], w_dn_bf[:, kc, hc * 512:(hc + 1) * 512], st[:], [Bst], [b_])
                        Bdn[kc].append(b_)

                S.dma("sp", x4h[:], x2_v[:, :, 126:128], reads=[B_x2[0]], writes=[Bx4h])
                S.dma("sp", x4[:], x2_v[:, :, 128:640], reads=[B_x2[1]], writes=[Bx4])
                load_up_block(0, 1024)
                load_up_block(2816, 3840)
                rms_bcast(x4h[:], [Bx4h], 2, h3h, Bh3h, pss, Bpss, rs, Brs, rstd, Brstd)
                TT("dve", h3h[:], x4h[:], rstd[:, 0:2].unsqueeze(1).to_broadcast([128, 8, 2]), ALU.mult,
                   [Bx4h, Brstd], [Bh3h])
                ui = 0
                dn_next = [0]

                def rms_h3_chain():
                    ACT(sq[:, 0:8, :], x4[:], AF.Square, [Bx4], [Bsq])
                    yield
                    for kc in range(8):
                        MM(pss[:], ones_bf[:], sq[:, kc, :], kc == 0, kc == 7, [Bob, Bsq], [Bpss])
                    yield
                    ACT(rs[:], pss[:], AF.Sqrt, [Bpss, Bc], [Brs], scale=1.0 / 1024, bias=cc("eps"))
                    yield
                    RCP(rstd[:], rs[:], [Brs], [Brstd])
                    yield
                    TT("dve", h3[:], x4[:], rstd[:].unsqueeze(1).to_broadcast([128, 8, 512]), ALU.mult, [Bx4, Brstd], [Bh3])
                    yield

                for _ in rms_h3_chain():
                    pass
                for g in range(8):
                    oc = 128 + g * 512
                    if g + 1 < 8:
                        S.dma("sp", x4[:], x2_v[:, :, oc + 512:oc + 1024], reads=[B_x2[g + 2]], writes=[Bx4])
                    for j in range(22):
                        if g == 0:
                            if j == 0:
                                load_up_block(1024, 2048)
                                load_up_block(3840, 4864)
                            if j == 6:
                                load_up_block(2048, 2816)
                                load_up_block(4864, 5632)
                            if j >= 10:
                                for _ in range(2):
                                    if dn_next[0] < 22:
                                        load_dn(dn_next[0])
                                        dn_next[0] += 1
                        ys = []
                        for r in (j, 22 + j):
                            p_, Bp_ = pu[ui % 4]
                            ux, Bux = upx[ui % 2]
                            y_, By_ = yv[ui % 3]
                            ui += 1
                            if g == 0:
                                ph, Bph = pu[ui % 4]
                                for kc in range(8):
                                    MM(ph[:, 0:2], w_up_bf[:, kc, r * 128:(r + 1) * 128], h3h[:, kc, :], kc == 0, kc == 7,
                                       Bup[r] + [Bh3h], [Bph])
                                TS("dve", prev[:, r, :], ph[:, 0:2], cc("flag"), None, ALU.mult, None, [Bph, Bc], [Bprev[r]])
                            for kc in range(8):
                                MM(p_[:], w_up_bf[:, kc, r * 128:(r + 1) * 128], h3[:, kc, :], kc == 0, kc == 7, Bup[r] + [Bh3], [Bp_])
                            CP("act", ux[:, 2:514], p_[:], [Bp_], [Bux])
                            ACT(y_[:], p_[:], AF.Identity, [Bp_, Bc], [By_], scale=cc("wffn", r * 3 + 2), bias=cc("bffn", r))
                            CP("pool", ux[:, 0:2], prev[:, r, :], [Bprev[r], Bux], [Bux])
                            STT(y_[:], ux[:, 1:513], cc("wffn", r * 3 + 1), y_[:], ALU.mult, ALU.add, [Bux, By_, Bc], [By_])
                            STT(y_[:], ux[:, 0:512], cc("wffn", r * 3 + 0), y_[:], ALU.mult, ALU.add, [Bux, By_, Bc], [By_])
                            CP("pool", prev[:, r, :], ux[:, 512:514], [Bux], [Bprev[r]])
                            ys.append((y_, By_))
                        ACT(ys[0][0][:], ys[0][0][:], AF.Silu, [ys[0][1]], [ys[0][1]])
                        TT("dve", actT[:, j, :], ys[0][0][:], ys[1][0][:], ALU.mult, [ys[0][1], ys[1][1]], [Bact[j]])
                    nxt4 = rms_h3_chain() if g + 1 < 8 else None
                    for dc in range(8):
                        p_, Bp_ = pd[dc % 2]
                        o_, Bo_ = ost[dc % 2]
                        S.dma("sp", o_[:], x2_v[:, dc, oc:oc + 512], reads=[B_x2[g + 1]], writes=[Bo_])
                        for kc in range(22):
                            MM(p_[:], w_dn_bf[:, kc, dc * 128:(dc + 1) * 128], actT[:, kc, :], kc == 0, kc == 21, Bdn[kc] + [Bact[kc]], [Bp_])
                        TT("dve", o_[:], o_[:], p_[:], ALU.add, [Bo_, Bp_], [Bo_])
                        S.dma("sp", yT_v[:, dc, g * 512:(g + 1) * 512], o_[:], reads=[Bo_])
                        if nxt4 is not None and dc >= 1:
                            try:
                                next(nxt4)
                            except StopIteration:
                                nxt4 = None
                    if nxt4 is not None:
                        for _ in nxt4:
                            pass
        S.finish()
    return nc, S


def make_in_maps(inputs):
    x = np.asarray(inputs["x"], np.float32)
    mem = np.asarray(inputs["mem"], np.float32)
    positions = np.asarray(inputs["positions"], np.int32)
    shared = {
        "w_in": np.ascontiguousarray(inputs["w_in"][0], np.float32),
        "w_uq": np.ascontiguousarray(inputs["w_uq"][0], np.float32),
        "w_ukv": np.ascontiguousarray(inputs["w_ukv"][0], np.float32),
        "w_out": np.ascontiguousarray(inputs["w_out"][0], np.float32),
        "w_mem_q": np.ascontiguousarray(inputs["w_mem_q"][0], np.float32),
        "w_mem_kv": np.ascontiguousarray(inputs["w_mem_kv"][0], np.float32),
        "w_mem_o": np.ascontiguousarray(inputs["w_mem_o"][0], np.float32),
        "w_up": np.ascontiguousarray(inputs["w_up"][0], np.float32),
        "w_down": np.ascontiguousarray(inputs["w_down"][0], np.float32),
    }
    csts = [pack_consts(inputs, 0), pack_consts(inputs, 1)]
    in_maps = []
    for core in range(8):
        b, half = core // 2, core % 2
        xT = np.zeros((D, SEQ), np.float32)
        p = np.zeros((SEQ,), np.int32)
        if half == 1:
            xT[:] = x[b].T
            p[:] = positions[b]
        else:
            xT[:, 4096:] = x[b, 0:4096].T
            p[4096:] = positions[b, 0:4096]
        m = dict(shared)
        m["xT"] = xT
        m["pos"] = np.ascontiguousarray(p.reshape(64, 128).T)
        m["cst"] = csts[half]
        m["memT"] = np.ascontiguousarray(mem[b].T)
        in_maps.append(m)
    return in_maps


_PROG = {}


def kernel(**inputs):
    if "nc" not in _PROG:
        _PROG["nc"] = build_program()[0]
    nc = _PROG["nc"]
    in_maps = make_in_maps(inputs)
    res = run_bass_kernel_spmd(nc, in_maps, core_ids=list(range(8)))
    out = np.empty((4, SEQ, D), np.float32)
    for core in range(8):
        b, half = core // 2, core % 2
        out[b, half * 4096:(half + 1) * 4096, :] = res.results[core]["yT"].T
    return out
```
